# Optimizing a Trainium2 kernel written in Bass

```python
import jax
import jax.numpy as jnp
from jax import lax
import numpy as np

D_MODEL = 1024
BATCH = 8
SEQ = 4096
DEPTH = 2

GRID_W = 64
CTX_LEN = 256
EPS = 1e-6

NA_HEADS = 8
NA_HEAD_DIM = 64
NA_WIN_R = 8
NA_WIN_C = 16
NA_WIDTH = NA_HEADS * NA_HEAD_DIM

CONV_CH = 256
CONV_WIDTH = 31

GLA_HEADS = 4
GLA_DK = 32
GLA_DV = 64
GLA_GATE_RANK = 16
GLA_TAU = 16.0
GLA_CHUNK = 64
GLA_K = GLA_HEADS * GLA_DK
GLA_V = GLA_HEADS * GLA_DV
ROPE_THETA = 10000.0

MIX_WIDTH = NA_WIDTH + CONV_CH + GLA_V
D_FF = 4 * D_MODEL
IN_SPLITS = (NA_WIDTH, NA_WIDTH, NA_WIDTH, 2 * CONV_CH, GLA_K, GLA_K, GLA_V, GLA_V, GLA_GATE_RANK, GLA_GATE_RANK)
IN_WIDTH = 3 * NA_WIDTH + 2 * CONV_CH + 2 * GLA_K + 2 * GLA_V + 2 * GLA_GATE_RANK

kernel_name = 'hymba_style_natten_conformer_gla_dit'


def rms_norm(x, g):
    xf = x.astype(jnp.float32)
    y = xf * lax.rsqrt(jnp.mean(xf * xf, axis=-1, keepdims=True) + EPS)
    return (y * g.astype(jnp.float32)).astype(x.dtype)


def layer_norm(x, g, b):
    xf = x.astype(jnp.float32)
    mu = jnp.mean(xf, axis=-1, keepdims=True)
    var = jnp.mean(jnp.square(xf - mu), axis=-1, keepdims=True)
    y = (xf - mu) * lax.rsqrt(var + EPS)
    return (y * g.astype(jnp.float32) + b.astype(jnp.float32)).astype(x.dtype)


def modulate(h, shift, scale):
    return h * (1.0 + scale) + shift


def split_heads(t, n_heads, head_dim):
    return t.reshape(t.shape[0], t.shape[1], n_heads, head_dim)


def split_columns(u):
    offsets = np.cumsum(IN_SPLITS)[:-1].tolist()
    return jnp.split(u, offsets, axis=-1)


def axial_rope(t):
    n, d = t.shape[1], t.shape[-1]
    half = d // 2
    n_freq = half // 2
    pos = jnp.arange(n)
    inv_freq = ROPE_THETA ** (-jnp.arange(n_freq, dtype=jnp.float32) / n_freq)
    tf = t.astype(jnp.float32)

    def rotate(u, p):
        ang = p.astype(jnp.float32)[:, None] * inv_freq[None, :]
        cos = jnp.cos(ang)[None, :, None, :]
        sin = jnp.sin(ang)[None, :, None, :]
        u1, u2 = u[..., :n_freq], u[..., n_freq:]
        return jnp.concatenate([u1 * cos - u2 * sin, u1 * sin + u2 * cos], axis=-1)

    out = jnp.concatenate([rotate(tf[..., :half], pos // GRID_W), rotate(tf[..., half:], pos % GRID_W)], axis=-1)
    return out.astype(t.dtype)


def neighborhood_attention(q, k, v, k_ctx, v_ctx, rpb, rows):
    b = q.shape[0]
    wr = min(NA_WIN_R, rows)
    n_loc = wr * NA_WIN_C
    qg = q.reshape(b, rows, GRID_W, NA_HEADS, NA_HEAD_DIM)
    kg = k.reshape(b, rows, GRID_W, NA_HEADS, NA_HEAD_DIM)
    vg = v.reshape(b, rows, GRID_W, NA_HEADS, NA_HEAD_DIM)
    cols = jnp.arange(GRID_W)
    col_start = jnp.clip(cols - NA_WIN_C // 2, 0, GRID_W - NA_WIN_C)
    col_idx = col_start[:, None] + jnp.arange(NA_WIN_C)[None, :]
    col_off = col_idx - cols[:, None] + (NA_WIN_C - 1)

    def row_block(r):
        r0 = jnp.clip(r - wr // 2, 0, rows - wr)
        q_r = lax.dynamic_index_in_dim(qg, r, axis=1, keepdims=False)
        k_win = lax.dynamic_slice_in_dim(kg, r0, wr, axis=1)[:, :, col_idx]
        v_win = lax.dynamic_slice_in_dim(vg, r0, wr, axis=1)[:, :, col_idx]
        row_off = r0 + jnp.arange(wr) - r + (NA_WIN_R - 1)
        bias = rpb[:, row_off[:, None, None], col_off[None, :, :]]
        bias = jnp.transpose(bias, (0, 2, 1, 3))[None].astype(jnp.float32)
        s_loc = jnp.einsum('bqhd,brqchd->bhqrc', q_r, k_win).astype(jnp.float32) + bias
        s_ctx = jnp.einsum('bqhd,blhd->bhql', q_r, k_ctx).astype(jnp.float32)
        s = jnp.concatenate([s_loc.reshape(b, NA_HEADS, GRID_W, n_loc), s_ctx], axis=-1)
        p = jax.nn.softmax(s, axis=-1).astype(v.dtype)
        p_loc = p[..., :n_loc].reshape(b, NA_HEADS, GRID_W, wr, NA_WIN_C)
        return (jnp.einsum('bhqrc,brqchd->bqhd', p_loc, v_win)
                + jnp.einsum('bhql,blhd->bqhd', p[..., n_loc:], v_ctx))

    o = lax.map(row_block, jnp.arange(rows))
    return jnp.transpose(o, (1, 0, 2, 3, 4)).reshape(b, rows * GRID_W, NA_WIDTH)


def context_attention(q, k, v):
    s = jnp.einsum('blhd,bmhd->bhlm', q, k).astype(jnp.float32)
    p = jax.nn.softmax(s, axis=-1).astype(v.dtype)
    o = jnp.einsum('bhlm,bmhd->blhd', p, v)
    return o.reshape(o.shape[0], o.shape[1], NA_WIDTH)


def conv_module(u, conv_w, conv_b, ln_g, ln_b, pw_w, pw_b):
    a, g = jnp.split(u, 2, axis=-1)
    h = a * jax.nn.sigmoid(g)
    h = lax.conv_general_dilated(
        h, conv_w[:, None, :].astype(h.dtype), window_strides=(1,),
        padding=[(CONV_WIDTH // 2, CONV_WIDTH // 2)],
        dimension_numbers=('NWC', 'WIO', 'NWC'), feature_group_count=CONV_CH) + conv_b
    h = jax.nn.silu(layer_norm(h, ln_g, ln_b))
    return h @ pw_w + pw_b


def gla_log_decay(z, w, b):
    a = (z @ w + b).astype(jnp.float32)
    return (jax.nn.log_sigmoid(a) / GLA_TAU).reshape(z.shape[0], z.shape[1], GLA_HEADS, GLA_DK)


def gla_chunked_scan(q, k, v, log_a, s0):
    b, n = q.shape[0], q.shape[1]
    n_chunks = n // GLA_CHUNK

    def to_chunks(t):
        t = t.astype(jnp.float32).reshape(b, n_chunks, GLA_CHUNK, t.shape[2], t.shape[3])
        return jnp.transpose(t, (1, 0, 3, 2, 4))

    mask = jnp.tril(jnp.ones((GLA_CHUNK, GLA_CHUNK), dtype=bool))[:, :, None]

    def step(state, inp):
        q_c, k_c, v_c, la_c = inp
        cum = jnp.cumsum(la_c, axis=2)
        rel = jnp.exp(jnp.where(mask, cum[:, :, :, None, :] - cum[:, :, None, :, :], -jnp.inf))
        att = jnp.einsum('bhtd,bhsd,bhtsd->bhts', q_c, k_c, rel)
        out = (jnp.einsum('bhts,bhsv->bhtv', att, v_c)
               + jnp.einsum('bhtd,bhdv->bhtv', q_c * jnp.exp(cum), state))
        last = cum[:, :, -1:, :]
        state = (jnp.exp(last[:, :, 0, :])[..., None] * state
                 + jnp.einsum('bhsd,bhsv->bhdv', k_c * jnp.exp(last - cum), v_c))
        return state, out

    state, out = lax.scan(step, s0, (to_chunks(q), to_chunks(k), to_chunks(v), to_chunks(log_a)))
    out = jnp.transpose(out, (1, 0, 3, 2, 4)).reshape(b, n, GLA_HEADS, GLA_DV)
    return out.astype(v.dtype), state


def gla_bidirectional(q, k, v, la_fwd, la_bwd, s_fwd0, s_bwd0):
    o_f, s_f = gla_chunked_scan(q, k, v, la_fwd, s_fwd0)
    o_b, s_b = gla_chunked_scan(q[:, ::-1], k[:, ::-1], v[:, ::-1], la_bwd[:, ::-1], s_bwd0)
    return o_f + o_b[:, ::-1], s_f, s_b


def gla_output(o, r, g):
    o = rms_norm(o, g)
    return o.reshape(o.shape[0], o.shape[1], GLA_V).astype(r.dtype) * jax.nn.silu(r)


def squared_relu_mlp(h, w1, w2):
    return jnp.square(jax.nn.relu(h @ w1)) @ w2


def hybrid_layer(x, xc, c, c_ctx, p, rows, need_ctx_out):
    b = x.shape[0]
    mod = jax.nn.silu(c) @ p['w_ada'] + p['b_ada']
    mod_c = jax.nn.silu(c_ctx) @ p['w_ada'] + p['b_ada']
    sh_a, sc_a, g_a, sh_m, sc_m, g_m = jnp.split(mod[:, None, :], 6, axis=-1)
    sh_ac, sc_ac, g_ac, sh_mc, sc_mc, g_mc = jnp.split(mod_c, 6, axis=-1)

    h = modulate(rms_norm(x, p['norm1_g']), sh_a, sc_a)
    hc = modulate(rms_norm(xc, p['norm1_g']), sh_ac, sc_ac)
    qa, ka, va, ub, qg, kg, vg, rg, zf, zb = split_columns(h @ p['w_in'])
    qac, kac, vac, ubc, qgc, kgc, vgc, rgc, zfc, zbc = split_columns(hc @ p['w_in'])

    na_scale = NA_HEAD_DIM ** -0.5
    k_ctx = rms_norm(split_heads(kac, NA_HEADS, NA_HEAD_DIM), p['na_k_g'])
    v_ctx = split_heads(vac, NA_HEADS, NA_HEAD_DIM)
    o_a = neighborhood_attention(
        rms_norm(split_heads(qa, NA_HEADS, NA_HEAD_DIM), p['na_q_g']) * na_scale,
        rms_norm(split_heads(ka, NA_HEADS, NA_HEAD_DIM), p['na_k_g']),
        split_heads(va, NA_HEADS, NA_HEAD_DIM), k_ctx, v_ctx, p['na_rpb'], rows)

    o_b = conv_module(ub, p['conv_w'], p['conv_b'], p['conv_ln_g'], p['conv_ln_b'], p['conv_pw_w'], p['conv_pw_b'])

    gla_scale = GLA_DK ** -0.5
    zero_state = jnp.zeros((b, GLA_HEADS, GLA_DK, GLA_DV), jnp.float32)
    o_gc, s_f, s_b = gla_bidirectional(
        split_heads(qgc, GLA_HEADS, GLA_DK) * gla_scale, split_heads(kgc, GLA_HEADS, GLA_DK),
        split_heads(vgc, GLA_HEADS, GLA_DV),
        gla_log_decay(zfc, p['gla_gw_f'], p['gla_gb_f']), gla_log_decay(zbc, p['gla_gw_b'], p['gla_gb_b']),
        zero_state, zero_state)
    o_g, _, _ = gla_bidirectional(
        axial_rope(split_heads(qg, GLA_HEADS, GLA_DK)) * gla_scale, axial_rope(split_heads(kg, GLA_HEADS, GLA_DK)),
        split_heads(vg, GLA_HEADS, GLA_DV),
        gla_log_decay(zf, p['gla_gw_f'], p['gla_gb_f']), gla_log_decay(zb, p['gla_gw_b'], p['gla_gb_b']),
        s_f, s_b)
    o_c = gla_output(o_g, rg, p['gla_out_g'])

    x = x + g_a * (jnp.concatenate([o_a, o_b, o_c], axis=-1) @ p['w_out'])
    x = x + g_m * squared_relu_mlp(modulate(rms_norm(x, p['norm2_g']), sh_m, sc_m), p['w_mlp_in'], p['w_mlp_out'])
    if not need_ctx_out:
        return x, None

    o_ac = context_attention(rms_norm(split_heads(qac, NA_HEADS, NA_HEAD_DIM), p['na_q_g']) * na_scale, k_ctx, v_ctx)
    o_bc = conv_module(ubc, p['conv_w'], p['conv_b'], p['conv_ln_g'], p['conv_ln_b'], p['conv_pw_w'], p['conv_pw_b'])
    o_cc = gla_output(o_gc, rgc, p['gla_out_g'])
    xc = xc + g_ac * (jnp.concatenate([o_ac, o_bc, o_cc], axis=-1) @ p['w_out'])
    xc = xc + g_mc * squared_relu_mlp(modulate(rms_norm(xc, p['norm2_g']), sh_mc, sc_mc), p['w_mlp_in'], p['w_mlp_out'])
    return x, xc


def setup_inputs(seed: int = 0) -> dict:
    key = jax.random.key(seed)
    ks = jax.random.split(key, 32)
    D = D_MODEL

    def nrm(k, shape, s):
        return jax.random.normal(k, shape, jnp.float32) * s

    return {
        'x': nrm(ks[0], (BATCH, SEQ, D), 1.0),
        'c': nrm(ks[1], (BATCH, D), 1.0),
        'ctx': nrm(ks[2], (BATCH, CTX_LEN, D), 1.0),
        'c_ctx': nrm(ks[3], (D,), 1.0),
        'w_ada': nrm(ks[4], (DEPTH, D, 6 * D), D ** -0.5),
        'b_ada': nrm(ks[5], (DEPTH, 6 * D), 0.02),
        'norm1_g': 1.0 + nrm(ks[6], (DEPTH, D), 0.02),
        'w_in': nrm(ks[7], (DEPTH, D, IN_WIDTH), D ** -0.5),
        'na_q_g': 1.0 + nrm(ks[8], (DEPTH, NA_HEAD_DIM), 0.02),
        'na_k_g': 1.0 + nrm(ks[9], (DEPTH, NA_HEAD_DIM), 0.02),
        'na_rpb': nrm(ks[10], (DEPTH, NA_HEADS, 2 * NA_WIN_R - 1, 2 * NA_WIN_C - 1), 0.1),
        'conv_w': nrm(ks[11], (DEPTH, CONV_WIDTH, CONV_CH), CONV_WIDTH ** -0.5),
        'conv_b': nrm(ks[12], (DEPTH, CONV_CH), 0.02),
        'conv_ln_g': 1.0 + nrm(ks[13], (DEPTH, CONV_CH), 0.02),
        'conv_ln_b': nrm(ks[14], (DEPTH, CONV_CH), 0.02),
        'conv_pw_w': nrm(ks[15], (DEPTH, CONV_CH, CONV_CH), CONV_CH ** -0.5),
        'conv_pw_b': nrm(ks[16], (DEPTH, CONV_CH), 0.02),
        'gla_gw_f': nrm(ks[17], (DEPTH, GLA_GATE_RANK, GLA_K), GLA_GATE_RANK ** -0.5),
        'gla_gb_f': nrm(ks[18], (DEPTH, GLA_K), 0.1),
        'gla_gw_b': nrm(ks[19], (DEPTH, GLA_GATE_RANK, GLA_K), GLA_GATE_RANK ** -0.5),
        'gla_gb_b': nrm(ks[20], (DEPTH, GLA_K), 0.1),
        'gla_out_g': 1.0 + nrm(ks[21], (DEPTH, GLA_DV), 0.02),
        'w_out': nrm(ks[22], (DEPTH, MIX_WIDTH, D), MIX_WIDTH ** -0.5),
        'norm2_g': 1.0 + nrm(ks[23], (DEPTH, D), 0.02),
        'w_mlp_in': nrm(ks[24], (DEPTH, D, D_FF), D ** -0.5),
        'w_mlp_out': nrm(ks[25], (DEPTH, D_FF, D), D_FF ** -0.5),
    }


def reference(x, c, ctx, c_ctx, w_ada, b_ada, norm1_g, w_in, na_q_g, na_k_g, na_rpb,
              conv_w, conv_b, conv_ln_g, conv_ln_b, conv_pw_w, conv_pw_b,
              gla_gw_f, gla_gb_f, gla_gw_b, gla_gb_b, gla_out_g, w_out, norm2_g,
              w_mlp_in, w_mlp_out):
    rows = x.shape[1] // GRID_W
    xc = ctx
    for layer in range(DEPTH):
        p = {
            'w_ada': w_ada[layer], 'b_ada': b_ada[layer], 'norm1_g': norm1_g[layer], 'w_in': w_in[layer],
            'na_q_g': na_q_g[layer], 'na_k_g': na_k_g[layer], 'na_rpb': na_rpb[layer],
            'conv_w': conv_w[layer], 'conv_b': conv_b[layer], 'conv_ln_g': conv_ln_g[layer],
            'conv_ln_b': conv_ln_b[layer], 'conv_pw_w': conv_pw_w[layer], 'conv_pw_b': conv_pw_b[layer],
            'gla_gw_f': gla_gw_f[layer], 'gla_gb_f': gla_gb_f[layer], 'gla_gw_b': gla_gw_b[layer],
            'gla_gb_b': gla_gb_b[layer], 'gla_out_g': gla_out_g[layer], 'w_out': w_out[layer],
            'norm2_g': norm2_g[layer], 'w_mlp_in': w_mlp_in[layer], 'w_mlp_out': w_mlp_out[layer],
        }
        x, xc = hybrid_layer(x, xc, c, c_ctx, p, rows, layer < DEPTH - 1)
    return x
```

```python
import os
from contextlib import ExitStack
import numpy as np
import concourse.bass as bass
import concourse.mybir as mybir
from concourse.bass_utils import run_bass_kernel_spmd

F32 = mybir.dt.float32
BF16 = mybir.dt.bfloat16
AF = mybir.ActivationFunctionType
ALU = mybir.AluOpType
AX = mybir.AxisListType

D = 1024
S = 4096
LC = 256
NTOK = S + LC
DEPTH = 2
NCOL = 3104
NB = 256
EPS = 1e-6
HCW = 4416
NEG = -30000.0


class Tok:
    __slots__ = ("w", "r")

    def __init__(self):
        self.w = None
        self.r = {}


class Prog:
    NDMA = 12

    def __init__(self, nc):
        self.nc = nc
        self.engs = {"pe": nc.tensor, "dve": nc.vector, "act": nc.scalar, "pool": nc.gpsimd, "sp": nc.sync}
        self.sems = {}
        self.cnt = {}
        self.pending = {}
        for k in self.engs:
            self.sems[k] = nc.alloc_semaphore("s_" + k)
            self.cnt[k] = 0
            self.pending[k] = False
        self.seen = {k: {} for k in self.engs}
        self.dq = {}
        for q in ("sp", "pool"):
            ring = []
            for i in range(self.NDMA):
                key = "d_%s_%d" % (q, i)
                self.sems[key] = nc.alloc_semaphore(key)
                self.cnt[key] = 0
                ring.append(key)
            self.dq[q] = [ring, 0]
        self.n_inst = 0

    def _wait(self, e, deps):
        eng = self.engs[e]
        seen = self.seen[e]
        best = {}
        for (k, v) in deps:
            if k == "pe" and e == "pe":
                continue
            if v > best.get(k, 0):
                best[k] = v
        for k, v in best.items():
            if seen.get(k, 0) >= v:
                continue
            eng.wait_ge(self.sems[k], v)
            seen[k] = v
            self.n_inst += 1

    @staticmethod
    def _deps(R, W):
        deps = []
        for t in R:
            if t.w is not None:
                deps.append(t.w)
        for t in W:
            if t.w is not None:
                deps.append(t.w)
            deps.extend(t.r.items())
        return deps

    def op(self, e, fn, R=(), W=(), inc=True):
        self._wait(e, self._deps(R, W))
        ins = fn(self.engs[e])
        self.n_inst += 1
        v = self.cnt[e] + 1
        if inc:
            ins.then_inc(self.sems[e], 1)
            self.cnt[e] = v
            self.pending[e] = False
        else:
            self.pending[e] = True
        for t in R:
            if t.r.get(e, 0) < v:
                t.r[e] = v
        for t in W:
            t.w = (e, v)
            t.r = {}
        return ins

    def dma(self, q, out, in_, R=(), W=()):
        ring, idx = self.dq[q]
        key = ring[idx % len(ring)]
        self.dq[q][1] = idx + 1
        deps = self._deps(R, W)
        if self.cnt[key] > 0:
            deps.append((key, self.cnt[key]))
        self._wait(q, deps)
        ins = self.engs[q].dma_start(out=out, in_=in_)
        self.n_inst += 1
        v = self.cnt[key] + 16
        ins.then_inc(self.sems[key], 16)
        self.cnt[key] = v
        for t in R:
            if t.r.get(key, 0) < v:
                t.r[key] = v
        for t in W:
            t.w = (key, v)
            t.r = {}
        return ins

    def barrier(self):
        deps = [(k, v) for k, v in self.cnt.items() if v > 0]
        for e in self.engs:
            assert not self.pending[e], e
            self._wait(e, deps)


def _rope_tables():
    pos = np.arange(S)
    inv = (10000.0 ** (-np.arange(8, dtype=np.float32) / 8)).astype(np.float32)
    row = (pos // 64).astype(np.float32)
    col = (pos % 64).astype(np.float32)
    cos = np.ones((32, NTOK), np.float32)
    sin = np.zeros((32, NTOK), np.float32)
    for d in range(32):
        p = row if d < 16 else col
        ang = (p * inv[d % 8]).astype(np.float32)
        sgn = -1.0 if (d % 16) < 8 else 1.0
        cos[d, :S] = np.cos(ang)
        sin[d, :S] = sgn * np.sin(ang)
    cos = np.tile(cos, (4, 1))
    sin = np.tile(sin, (4, 1))
    sc = np.float32(32 ** -0.5)
    return np.stack([cos * sc, sin * sc, cos, sin]).astype(np.float32)


def _bias_index():
    Ev = [0, 0, 0, 54, 54]
    ro = np.zeros((5, 128, 5, 128), np.int64)
    co = np.zeros((5, 128, 5, 128), np.int64)
    mask = np.zeros((5, 128, 5, 128), np.float32)
    ki = np.arange(128)[:, None, None]
    c = np.arange(5)[None, :, None]
    qi = np.arange(128)[None, None, :]
    for vi in range(5):
        E = Ev[vi]
        r = E + 2 * vi
        qr = r + qi // 64
        qc = qi % 64
        kr = E + 2 * c + ki // 64
        kc = ki % 64
        r0 = np.clip(qr - 4, 0, 56)
        cs = np.clip(qc - 8, 0, 48)
        valid = (kr >= r0) & (kr < r0 + 8) & (kc >= cs) & (kc < cs + 16)
        ro[vi] = np.clip(kr - qr + 7, 0, 14)
        co[vi] = np.clip(kc - qc + 15, 0, 30)
        mask[vi] = np.where(valid, 0.0, NEG)
    return ro, co, mask


def _prep_shared(inp):
    f = lambda a: np.ascontiguousarray(np.asarray(a, dtype=np.float32))
    w_in = f(inp["w_in"])
    perm = np.arange(128)
    for h in range(4):
        for d in range(32):
            perm[h * 32 + d] = h * 32 + (d + 8 if (d % 16) < 8 else d - 8)
    qg = w_in[:, :, 2048:2176]
    kg = w_in[:, :, 2176:2304]
    win = np.concatenate([
        w_in[:, :, 0:1536],
        w_in[:, :, 2304:2816],
        w_in[:, :, 1536:2048],
        qg, qg[:, :, perm], kg, kg[:, :, perm],
        w_in[:, :, 2816:2848]], axis=2)
    assert win.shape[2] == NCOL
    sh = {"win": f(win), "w_ada": f(inp["w_ada"]), "w_out": f(inp["w_out"]),
          "w1": f(inp["w_mlp_in"]), "w2": f(inp["w_mlp_out"])}
    sh["badaT"] = f(np.asarray(inp["b_ada"]).reshape(DEPTH, 48, 128).transpose(0, 2, 1))
    sh["n1g"] = f(np.asarray(inp["norm1_g"]).reshape(DEPTH, 8, 128).transpose(0, 2, 1))
    sh["n2g"] = f(np.asarray(inp["norm2_g"]).reshape(DEPTH, 8, 128).transpose(0, 2, 1))
    sh["naqg"] = f(np.tile(np.asarray(inp["na_q_g"])[:, None, None, :], (1, 128, 8, 1)).reshape(DEPTH, 128, 512))
    sh["nakg"] = f(np.tile(np.asarray(inp["na_k_g"])[:, None, None, :], (1, 128, 8, 1)).reshape(DEPTH, 128, 512))
    ro, co, mask = _bias_index()
    rpb = np.asarray(inp["na_rpb"], dtype=np.float32)
    g = rpb[:, :, ro, co]
    sh["rpbT"] = f(g.transpose(0, 2, 3, 1, 4, 5))
    sh["bmask"] = f(mask)
    sh["convw"] = f(np.asarray(inp["conv_w"]).reshape(DEPTH, 31, 2, 128).transpose(0, 3, 2, 1))
    v2 = lambda a: f(np.asarray(a).reshape(DEPTH, 2, 128).transpose(0, 2, 1))
    sh["cvec"] = f(np.stack([v2(inp["conv_b"]), v2(inp["conv_ln_g"]), v2(inp["conv_ln_b"]), v2(inp["conv_pw_b"])], axis=2))
    sh["pww"] = f(inp["conv_pw_w"])
    gw = np.zeros((DEPTH, 2, 33, 128), np.float32)
    gw[:, 0, 0:16] = np.asarray(inp["gla_gw_f"]); gw[:, 0, 32] = np.asarray(inp["gla_gb_f"])
    gw[:, 1, 0:16] = np.asarray(inp["gla_gw_b"]); gw[:, 1, 32] = np.asarray(inp["gla_gb_b"])
    sh["gw"] = gw
    sh["goutg"] = f(np.tile(np.asarray(inp["gla_out_g"])[:, None, None, :], (1, 128, 4, 1)).reshape(DEPTH, 128, 256))
    sh["ident"] = np.eye(128, dtype=np.float32)
    sh["rope"] = _rope_tables()
    s_i = np.arange(128)[:, None]
    t_i = np.arange(128)[None, :]
    tri = np.stack([(s_i <= t_i), (s_i >= t_i)]).astype(np.float32)
    sh["tri"] = tri
    sh["lmat"] = (-tri / 16.0).astype(np.float32)
    hd = np.arange(128) // 32
    sh["bdq"] = f(np.tile((hd[:, None] == np.arange(4)[None, :]).astype(np.float32)[:, :, None], (1, 1, 128)))
    sh["bds"] = f((hd[:, None] == (np.arange(256) // 64)[None, :]).astype(np.float32))
    return sh


def build(dbg=False, stop_after=None):
    nc = bass.Bass("TRN2", target_bir_lowering=False)
    P = Prog(nc)

    def din(name, shape, dt=F32):
        return nc.dram_tensor(name, list(shape), dt, kind="ExternalInput").ap()

    def dscr(name, shape, dt):
        if dbg:
            return nc.dram_tensor(name, list(shape), dt, kind="ExternalOutput").ap()
        return nc.dram_tensor(name, list(shape), dt).ap()

    x_in = din("x", [S, D]); ctx_in = din("ctx", [LC, D]); cc_in = din("cc", [128, 8, 2])
    win_d = din("win", [DEPTH, D, NCOL]); wada_d = din("w_ada", [DEPTH, D, 6 * D])
    wout_d = din("w_out", [DEPTH, D, D]); w1_d = din("w1", [DEPTH, D, 4 * D]); w2_d = din("w2", [DEPTH, 4 * D, D])
    badaT_d = din("badaT", [DEPTH, 128, 48]); n1g_d = din("n1g", [DEPTH, 128, 8]); n2g_d = din("n2g", [DEPTH, 128, 8])
    naqg_d = din("naqg", [DEPTH, 128, 512]); nakg_d = din("nakg", [DEPTH, 128, 512])
    rpbT_d = din("rpbT", [DEPTH, 5, 128, 8, 5, 128]); bmask_d = din("bmask", [5, 128, 5, 128])
    convw_d = din("convw", [DEPTH, 128, 2, 31]); cvec_d = din("cvec", [DEPTH, 128, 4, 2]); pww_d = din("pww", [DEPTH, 256, 256])
    gw_d = din("gw", [DEPTH, 2, 33, 128]); goutg_d = din("goutg", [DEPTH, 128, 256])
    ident_d = din("ident", [128, 128]); rope_d = din("rope", [4, 128, NTOK]); tri_d = din("tri", [2, 128, 128])
    lmat_d = din("lmat", [2, 128, 128]); bdq_d = din("bdq", [128, 4, 128]); bds_d = din("bds", [128, 256])
    y_out = nc.dram_tensor("y", [S, D], F32, kind="ExternalOutput").ap()

    xs_d = dscr("xs", [D, NTOK], F32)
    qT_d = dscr("qT", [512, NTOK], BF16); kT_d = dscr("kT", [512, NTOK], BF16); vA_d = dscr("vA", [NTOK, 512], BF16)
    hcT_d = dscr("hcT", [256, HCW], BF16)
    gq_d = dscr("gq", [128, NTOK], BF16); gk_d = dscr("gk", [128, NTOK], BF16)
    gz_d = dscr("gz", [2, 16, NTOK], BF16)
    gv_d = dscr("gv", [NTOK, 256], BF16); gr_d = dscr("gr", [NTOK, 256], BF16)
    gof_d = dscr("gof", [NTOK, 256], F32); gob_d = dscr("gob", [NTOK, 256], F32)
    oT_d = dscr("oT", [D, NTOK], BF16)
    xs_v = xs_d.rearrange("(k p) t -> p k t", p=128)
    oT_v = oT_d.rearrange("(k p) t -> p k t", p=128)

    uid = [0]

    def sb(name, shape, dt, stack=None):
        uid[0] += 1
        name = "%s_%d" % (name, uid[0])
        if stack is None:
            return nc.alloc_sbuf_tensor(name, list(shape), dt)
        return stack.enter_context(nc.sbuf_tensor(name, list(shape), dt))

    def pm(name, shape, dt, stack):
        uid[0] += 1
        name = "%s_%d" % (name, uid[0])
        return stack.enter_context(nc.psum_tensor(name, list(shape), dt))

    def mm(out, lhsT, rhs, start, stop, R, W, inc=None):
        if inc is None:
            inc = stop
        P.op("pe", lambda e: e.matmul(out, lhsT=lhsT, rhs=rhs, start=start, stop=stop), R=R, W=W, inc=inc)

    def tr(out, in_, idn, R, W, inc=True):
        P.op("pe", lambda e: e.transpose(out, in_, idn), R=R, W=W, inc=inc)

    def act(out, in_, func, R, W, bias=None, scale=None, accum=None):
        kw = {}
        if bias is not None:
            kw["bias"] = bias
        if scale is not None:
            kw["scale"] = scale
        if accum is not None:
            kw["accum_out"] = accum
        P.op("act", lambda e: e.activation(out=out, in_=in_, func=func, **kw), R=R, W=W)

    def tt(eng, out, in0, in1, op, R, W):
        P.op(eng, lambda e: e.tensor_tensor(out=out, in0=in0, in1=in1, op=op), R=R, W=W)

    def ts(eng, out, in0, s1, s2, op0, op1, R, W):
        if s2 is None:
            P.op(eng, lambda e: e.tensor_scalar(out=out, in0=in0, scalar1=s1, scalar2=None, op0=op0), R=R, W=W)
        else:
            P.op(eng, lambda e: e.tensor_scalar(out=out, in0=in0, scalar1=s1, scalar2=s2, op0=op0, op1=op1), R=R, W=W)

    def stt(eng, out, in0, scalar, in1, op0, op1, R, W):
        P.op(eng, lambda e: e.scalar_tensor_tensor(out=out, in0=in0, scalar=scalar, in1=in1, op0=op0, op1=op1), R=R, W=W)

    def cp(eng, out, in_, R, W):
        if eng == "act":
            P.op("act", lambda e: e.copy(out=out, in_=in_), R=R, W=W)
        else:
            P.op(eng, lambda e: e.tensor_copy(out=out, in_=in_), R=R, W=W)

    def recip(out, in_, R, W):
        P.op("dve", lambda e: e.reciprocal(out=out, in_=in_), R=R, W=W)

    WA = sb("WA", [128, 32768], BF16)
    WB = sb("WB", [128, 32768], BF16)
    WO = sb("WO", [128, 8, 1024], BF16)
    identb = sb("identb", [128, 128], BF16); identf = sb("identf", [128, 128], F32)
    onesb = sb("onesb", [128, 128], BF16); onesf = sb("onesf", [128, 128], F32)
    ccs = sb("ccs", [128, 8, 2], F32)
    ccsb = sb("ccsb", [128, 8, 2], BF16)
    modTs = [sb("modT%d" % i, [128, 48, 2], F32) for i in range(DEPTH)]
    gsc1s = [sb("gsc1%d" % i, [128, 8, 2], F32) for i in range(DEPTH)]
    gsc2s = [sb("gsc2%d" % i, [128, 8, 2], F32) for i in range(DEPTH)]
    t_mods = [Tok() for _ in range(DEPTH)]
    modT = modTs[0]; gsc1 = gsc1s[0]; gsc2 = gsc2s[0]
    badaTs = [sb("badaTs%d" % i, [128, 48], F32) for i in range(DEPTH)]
    n1gs = [sb("n1gs%d" % i, [128, 8], F32) for i in range(DEPTH)]
    n2gs = [sb("n2gs%d" % i, [128, 8], F32) for i in range(DEPTH)]
    t_c = Tok(); t_cc = Tok(); t_mod = t_mods[0]; t_vec = Tok()
    P.dma("sp", identf[:], ident_d, W=[t_c])
    P.dma("pool", identb[:], ident_d, W=[t_c])
    P.op("pool", lambda e: e.memset(onesb[:], 1.0), W=[t_c])
    P.op("pool", lambda e: e.memset(onesf[:], 1.0 / 256), W=[t_c])
    P.dma("sp", ccs[:], cc_in, W=[t_cc])
    act(ccs[:], ccs[:], AF.Silu, R=[t_cc], W=[t_cc])
    cp("dve", ccsb[:], ccs[:], R=[t_cc], W=[t_cc])
    for i in range(DEPTH):
        P.dma("sp", badaTs[i][:], badaT_d[i], W=[Tok()])
        P.dma("sp", n1gs[i][:], n1g_d[i], W=[Tok()])
        P.dma("sp", n2gs[i][:], n2g_d[i], W=[Tok()])
    P.barrier()

    def blocks(with_ctx):
        bl = [(i * NB, NB) for i in range(S // NB)]
        if with_ctx:
            bl.append((S, LC))
        return bl

    def phase0():
        with ExitStack() as ph:
            xin = [sb("p0x%d" % i, [128, D], F32, ph) for i in range(2)]; t_xin = [Tok(), Tok()]
            xo = [sb("p0o%d" % i, [128, 8, 128], F32, ph) for i in range(2)]; t_xo = [Tok(), Tok()]
            pT = [pm("p0p%d" % i, [128, 8, 128], F32, ph) for i in range(2)]; t_pT = [Tok(), Tok()]
            yield
            for i in range(34):
                b = i % 2
                src = x_in[i * 128:(i + 1) * 128, :] if i < 32 else ctx_in[(i - 32) * 128:(i - 31) * 128, :]
                P.dma("sp", xin[b][:], src, W=[t_xin[b]])
                for k in range(8):
                    tr(pT[b][:, k, :], xin[b][:, k * 128:(k + 1) * 128], identf[:], R=[t_xin[b], t_c], W=[t_pT[b]], inc=(k == 7))
                cp("act", xo[b][:, 0:4, :], pT[b][:, 0:4, :], R=[t_pT[b]], W=[t_xo[b]])
                cp("dve", xo[b][:, 4:8, :], pT[b][:, 4:8, :], R=[t_pT[b]], W=[t_xo[b]])
                P.dma("pool", xs_v[:, :, i * 128:(i + 1) * 128], xo[b][:], R=[t_xo[b]], W=[t_xs])
                yield
            P.barrier()

    t_xs = Tok()

    def setup_gen(l, ph, CB):
        modT = modTs[l]; gsc1 = gsc1s[l]; gsc2 = gsc2s[l]; t_mod = t_mods[l]
        nblk = (6 * D) // CB; cpb = CB // 128
        wst = [sb("wst%d" % i, [128, 8, CB], BF16, ph) for i in range(2)]; t_wst = [[Tok() for _ in range(8)] for _ in range(2)]
        pmod = pm("pmod", [128, 48, 2], F32, ph); t_pmod = Tok()
        tmp = sb("stmp", [128, 8, 2], F32, ph); t_tmp = Tok()
        wv = wada_d[l].rearrange("(k p) n -> p k n", p=128)

        def ldb(jb):
            for k in range(8):
                P.dma("pool", wst[jb % 2][:, k, :], wv[:, k, jb * CB:(jb + 1) * CB], W=[t_wst[jb % 2][k]])
        ldb(0)
        yield
        for jb in range(nblk):
            b = jb % 2
            if jb + 1 < nblk:
                ldb(jb + 1)
                yield
            for jj in range(cpb):
                j = jb * cpb + jj
                for k in range(8):
                    mm(pmod[:, j, :], wst[b][:, k, jj * 128:(jj + 1) * 128], ccsb[:, k, :], k == 0, k == 7,
                       R=[t_wst[b][k], t_cc], W=[t_pmod])
                yield
        tt("dve", modT[:], pmod[:], badaTs[l][:].unsqueeze(2).to_broadcast([128, 48, 2]), ALU.add, R=[t_pmod, t_vec], W=[t_mod])
        ts("dve", tmp[:], modT[:, 8:16, :], 1.0, None, ALU.add, None, R=[t_mod], W=[t_tmp])
        tt("dve", gsc1[:], tmp[:], n1gs[l][:].unsqueeze(2).to_broadcast([128, 8, 2]), ALU.mult, R=[t_tmp, t_vec], W=[t_mod])
        ts("dve", tmp[:], modT[:, 32:40, :], 1.0, None, ALU.add, None, R=[t_mod], W=[t_tmp])
        tt("dve", gsc2[:], tmp[:], n2gs[l][:].unsqueeze(2).to_broadcast([128, 8, 2]), ALU.mult, R=[t_tmp, t_vec], W=[t_mod])
        yield

    def load_w(dst, src_v, nk, width, q="pool"):
        toks = []
        for k in range(nk):
            t = Tok()
            P.dma(q, dst[:, k, :], src_v[:, k, :], W=[t])
            toks.append(t)
        return toks

    def norm_gen(xb, t_xb, n, hT, t_hT, gsc, shbase, v, tl):
        for k in range(8):
            b = k % 2
            act(tl["sq"][b][:, :n], xb[:, k, :n], AF.Square, R=[t_xb], W=[tl["t_sq"][b]])
            mm(tl["pss"][:, :n], onesb[:], tl["sq"][b][:, :n], k == 0, k == 7, R=[tl["t_sq"][b], t_c], W=[tl["t_pss"]], inc=True)
            yield
        act(tl["rs"][:, :n], tl["pss"][:, :n], AF.Ln, R=[tl["t_pss"]], W=[tl["t_rs"]], scale=1.0 / D, bias=EPS)
        act(tl["rs"][:, :n], tl["rs"][:, :n], AF.Exp, R=[tl["t_rs"]], W=[tl["t_rs"]], scale=-0.5)
        yield
        for k in range(8):
            b = k % 2
            tt("dve", tl["tm"][b][:, :n], xb[:, k, :n], tl["rs"][:, :n], ALU.mult, R=[t_xb, tl["t_rs"]], W=[tl["t_tm"][b]])
            act(hT[:, k, :n], tl["tm"][b][:, :n], AF.Identity, R=[tl["t_tm"][b], t_mod], W=[t_hT],
                scale=gsc[:, k, v:v + 1], bias=modT[:, shbase + k, v:v + 1])
            yield

    def norm_mod(*a):
        for _ in norm_gen(*a):
            pass

    def norm_tiles(ph, pfx):
        tl = {}
        tl["sq"] = [sb(pfx + "sq%d" % i, [128, NB], BF16, ph) for i in range(2)]; tl["t_sq"] = [Tok(), Tok()]
        tl["tm"] = [sb(pfx + "tm%d" % i, [128, NB], F32, ph) for i in range(2)]; tl["t_tm"] = [Tok(), Tok()]
        tl["rs"] = sb(pfx + "rs", [128, NB], F32, ph); tl["t_rs"] = Tok()
        tl["pss"] = pm(pfx + "pss", [128, NB], F32, ph); tl["t_pss"] = Tok()
        return tl

    def phaseA(l, t_win, with_ctx):
        WIN = WA[:, 0:8 * NCOL].rearrange("p (k n) -> p k n", k=8)
        with ExitStack() as ph:
            tl = norm_tiles(ph, "a")
            xb = [sb("axb%d" % i, [128, 8, NB], F32, ph) for i in range(2)]; t_xb = [Tok(), Tok()]
            hT = [sb("ahT%d" % i, [128, 8, NB], BF16, ph) for i in range(2)]; t_hT = [Tok(), Tok()]
            qg_s = sb("aqg", [128, 512], F32, ph); kg_s = sb("akg", [128, 512], F32, ph); t_g = Tok()
            rope = sb("arope", [128, 4, NB], F32, ph); t_rope = Tok()
            zero = sb("azero", [128, 32], BF16, ph); t_zero = Tok()
            ptm = [pm("aptm%d" % i, [128, 512], F32, ph) for i in range(2)]; t_ptm = [Tok(), Tok()]
            pfm = [pm("apfm%d" % i, [128, NB], F32, ph) for i in range(2)]; t_pfm = [Tok() for _ in range(2)]
            ptr = [pm("aptr%d" % i, [128, 4, 128], BF16, ph) for i in range(2)]; t_ptr = [Tok(), Tok()]
            st = [sb("ast%d" % i, [128, 512], F32, ph) for i in range(2)]; t_st = [Tok(), Tok()]
            sq2_ = [sb("asq2%d" % i, [128, 512], F32, ph) for i in range(2)]; t_sq2_ = [Tok(), Tok()]
            ssh_ = [sb("assh%d" % i, [128, 8], F32, ph) for i in range(2)]; t_ssh_ = [Tok(), Tok()]
            qn = [sb("aqn%d" % i, [128, 512], BF16, ph) for i in range(2)]; t_qn = [Tok(), Tok()]
            qTs = [sb("aqTs%d" % i, [128, 4, 128], BF16, ph) for i in range(2)]; t_qTs = [Tok(), Tok()]
            vb = [sb("avb%d" % i, [128, 512], BF16, ph) for i in range(2)]; t_vb = [Tok(), Tok()]
            sg_ = [sb("asg%d" % i, [128, NB], F32, ph) for i in range(2)]; t_sg_ = [Tok(), Tok()]
            fo = [sb("afo%d" % i, [128, NB], BF16, ph) for i in range(2)]; t_fo = [Tok(), Tok()]
            r1_ = [sb("ar1%d" % i, [128, NB], F32, ph) for i in range(2)]; t_r1_ = [Tok(), Tok()]
            r2_ = [sb("ar2%d" % i, [128, NB], F32, ph) for i in range(2)]; t_r2_ = [Tok(), Tok()]
            P.dma("sp", qg_s[:], naqg_d[l], W=[t_g])
            P.dma("sp", kg_s[:], nakg_d[l], W=[t_g])
            P.op("pool", lambda e: e.memset(zero[:], 0.0), W=[t_zero])
            P.barrier()
            t_scr = Tok()
            for c0, wd in ((0, 15), (15 + S, 15), (15 + S + 15, 15), (4412 - 15, HCW - 4412 + 15)):
                for ch in range(2):
                    P.dma("pool", hcT_d[ch * 128:(ch + 1) * 128, c0:c0 + wd], zero[:, 0:wd], R=[t_zero], W=[Tok()])
            bl = blocks(with_ctx)
            KA = int(os.environ.get("KA", "255"))
            if os.environ.get("KBL"):
                bl = bl[:int(os.environ["KBL"])]
            def ldx(bj):
                tj, nj = bl[bj]
                P.dma("sp", xb[bj % 2][:, :, :nj], xs_v[:, :, tj:tj + nj], R=[t_xs], W=[t_xb[bj % 2]])

            def emit_norm(bj):
                tj, nj = bl[bj]
                return norm_gen(xb[bj % 2], t_xb[bj % 2], nj, hT[bj % 2], t_hT[bj % 2], gsc1, 0, 0 if tj < S else 1, tl)
            ldx(0)
            if len(bl) > 1:
                ldx(1)
            for _ in emit_norm(0):
                pass
            ng = [None]
            cnt = {"fm": 0, "fo": 0}

            def tile_gen(bi, ti):
                t0, n = bl[bi]; b = bi % 2; g0 = t0 + ti * 128; s_ = ti
                sq2 = sq2_[s_]; t_sq2 = t_sq2_[s_]; ssh = ssh_[s_]; t_ssh = t_ssh_[s_]
                for grp in range(4):
                    for k in range(8):
                        mm(ptm[s_][:], hT[b][:, k, ti * 128:(ti + 1) * 128], WIN[:, k, grp * 512:(grp + 1) * 512], k == 0, k == 7,
                           R=[t_hT[b], t_win[k]], W=[t_ptm[s_]])
                    yield
                    if grp < 2:
                        act(sq2[:], ptm[s_][:], AF.Square, R=[t_ptm[s_]], W=[t_sq2])
                        yield
                        P.op("dve", lambda e: e.tensor_reduce(out=ssh[:], in_=sq2[:].rearrange("p (h d) -> p h d", h=8), axis=AX.X, op=ALU.add),
                             R=[t_sq2], W=[t_ssh])
                        yield
                        if grp == 0:
                            act(ssh[:], ssh[:], AF.Ln, R=[t_ssh], W=[t_ssh], scale=1.0, bias=64 * EPS)
                        else:
                            act(ssh[:], ssh[:], AF.Ln, R=[t_ssh], W=[t_ssh], scale=1.0 / 64, bias=EPS)
                        yield
                        act(ssh[:], ssh[:], AF.Exp, R=[t_ssh], W=[t_ssh], scale=-0.5)
                        yield
                        tt("dve", st[s_][:].rearrange("p (h d) -> p h d", h=8), ptm[s_][:].rearrange("p (h d) -> p h d", h=8),
                           ssh[:].unsqueeze(2).to_broadcast([128, 8, 64]), ALU.mult, R=[t_ptm[s_], t_ssh], W=[t_st[s_]])
                        yield
                        tt("dve", qn[s_][:], st[s_][:], (qg_s if grp == 0 else kg_s)[:], ALU.mult, R=[t_st[s_], t_g], W=[t_qn[s_]])
                        yield
                        for j in range(4):
                            tr(ptr[s_][:, j, :], qn[s_][:, j * 128:(j + 1) * 128], identb[:], R=[t_qn[s_], t_c], W=[t_ptr[s_]], inc=(j == 3))
                        yield
                        cp("act", qTs[s_][:], ptr[s_][:], R=[t_ptr[s_]], W=[t_qTs[s_]])
                        yield
                        dst = (qT_d if grp == 0 else kT_d).rearrange("(j p) t -> p j t", p=128)[:, :, g0:g0 + 128]
                        P.dma("pool", dst, qTs[s_][:], R=[t_qTs[s_]], W=[Tok()])
                        yield
                    else:
                        cp("act" if grp == 2 else "dve", vb[s_][:], ptm[s_][:], R=[t_ptm[s_]], W=[t_vb[s_]])
                        yield
                        if grp == 2:
                            P.dma("pool", vA_d[g0:g0 + 128, :], vb[s_][:], R=[t_vb[s_]], W=[Tok()])
                        else:
                            P.dma("pool", gv_d[g0:g0 + 128, :], vb[s_][:, 0:256], R=[t_vb[s_]], W=[Tok()])
                            P.dma("pool", gr_d[g0:g0 + 128, :], vb[s_][:, 256:512], R=[t_vb[s_]], W=[Tok()])
                        yield

            def fm_gen(bi):
                t0, n = bl[bi]; b = bi % 2

                def fm(col0, width):
                    pb = cnt["fm"] % 2; cnt["fm"] += 1
                    for k in range(8):
                        mm(pfm[pb][0:width, :n], WIN[:, k, col0:col0 + width], hT[b][:, k, :n], k == 0, k == 7,
                           R=[t_hT[b], t_win[k]], W=[t_pfm[pb]])
                    return pb
                hc0 = (15 + t0) if t0 < S else (15 + S + 15 + 15 + (t0 - S))
                for ch in range(2):
                    pa = fm(2048 + ch * 128, 128)
                    yield
                    pg = fm(2304 + ch * 128, 128)
                    yield
                    sg = sg_[ch]; t_sg = t_sg_[ch]
                    act(sg[:, :n], pfm[pg][:, :n], AF.Exp, R=[t_pfm[pg]], W=[t_sg], scale=-1.0)
                    yield
                    act(sg[:, :n], sg[:, :n], AF.Ln, R=[t_sg], W=[t_sg], scale=1.0, bias=1.0)
                    yield
                    act(sg[:, :n], sg[:, :n], AF.Exp, R=[t_sg], W=[t_sg], scale=-1.0)
                    yield
                    fi = cnt["fo"] % 2; cnt["fo"] += 1
                    tt("dve", fo[fi][:, :n], pfm[pa][:, :n], sg[:, :n], ALU.mult, R=[t_pfm[pa], t_sg], W=[t_fo[fi]])
                    yield
                    P.dma("pool", hcT_d[ch * 128:(ch + 1) * 128, hc0:hc0 + n], fo[fi][:, :n], R=[t_fo[fi]], W=[Tok()])
                    yield
                for qi, (dst, cbase) in enumerate(((gq_d, 2560), (gk_d, 2816))):
                    p1 = fm(cbase, 128)
                    yield
                    p2 = fm(cbase + 128, 128)
                    yield
                    r1 = r1_[qi]; t_r1 = t_r1_[qi]; r2 = r2_[qi]; t_r2 = t_r2_[qi]
                    tt("dve", r1[:, :n], pfm[p1][:, :n], rope[:, 2 * qi, :n], ALU.mult, R=[t_pfm[p1], t_rope], W=[t_r1])
                    yield
                    tt("dve", r2[:, :n], pfm[p2][:, :n], rope[:, 2 * qi + 1, :n], ALU.mult, R=[t_pfm[p2], t_rope], W=[t_r2])
                    yield
                    fi = cnt["fo"] % 2; cnt["fo"] += 1
                    tt("dve", fo[fi][:, :n], r1[:, :n], r2[:, :n], ALU.add, R=[t_r1, t_r2], W=[t_fo[fi]])
                    yield
                    P.dma("pool", dst[:, t0:t0 + n], fo[fi][:, :n], R=[t_fo[fi]], W=[Tok()])
                    yield
                for zi in range(2):
                    pz = fm(3072 + zi * 16, 16)
                    yield
                    fi = cnt["fo"] % 2; cnt["fo"] += 1
                    cp("act", fo[fi][0:16, :n], pfm[pz][0:16, :n], R=[t_pfm[pz]], W=[t_fo[fi]])
                    yield
                    P.dma("pool", gz_d[zi, :, t0:t0 + n], fo[fi][0:16, :n], R=[t_fo[fi]], W=[Tok()])
                    yield

            def rr(gens):
                gens = list(gens)
                while gens:
                    for gq in list(gens):
                        try:
                            next(gq)
                        except StopIteration:
                            gens.remove(gq)
            for bi, (t0, n) in enumerate(bl):
                P.dma("sp", rope[:, :, :n], rope_d[:, :, t0:t0 + n].rearrange("a p t -> p a t"), W=[t_rope])
                gl = [tile_gen(bi, ti) for ti in range(n // 128)]
                gl.append(fm_gen(bi))
                if bi + 1 < len(bl):
                    gl.append(emit_norm(bi + 1))
                rr(gl)
                if bi + 2 < len(bl):
                    ldx(bi + 2)
            P.barrier()

    def phaseB1(l, with_ctx, defer_setup=None):
        qT_v = qT_d.rearrange("(j p) t -> p j t", p=128)
        kT_v = kT_d.rearrange("(j p) t -> p j t", p=128)
        with ExitStack() as ph:
            kw = [sb("bk%d" % i, [128, 4, 128], BF16, ph) for i in range(6)]; t_kw = [Tok() for _ in range(6)]
            vw = [sb("bv%d" % i, [128, 8, 80], BF16, ph) for i in range(6)]; t_vw = [Tok() for _ in range(6)]
            kc = sb("bkc", [128, 4, 256], BF16, ph); t_kc = Tok()
            vc = [sb("bvc%d" % i, [128, 8, 80], BF16, ph) for i in range(2)]; t_vc = [Tok(), Tok()]
            qt = [sb("bq%d" % i, [128, 4, 128], BF16, ph) for i in range(2)]; t_qt = [Tok(), Tok()]
            bias = sb("bbias", [128, 8, 5, 128], BF16, ph); t_bias = Tok()
            bst = [sb("bbst%d" % i, [128, 5, 128], F32, ph) for i in range(2)]; t_bst = [Tok(), Tok()]
            bmk = sb("bbmk", [128, 5, 128], F32, ph); t_bmk = Tok()
            PT = [sb("bPT%d" % i, [128, 8, 128], BF16, ph) for i in range(2)]; t_PT = [Tok(), Tok()]
            rden = sb("brden", [128, 2, 4], F32, ph); t_rden = Tok()
            onb = sb("bonb", [128, 512], BF16, ph); t_onb = Tok()
            oTs = [sb("boTs%d" % i, [128, 4, 128], BF16, ph) for i in range(2)]; t_oTs = [Tok(), Tok()]
            ps = [pm("bps%d" % i, [128, 8, 128], F32, ph) for i in range(2)]; t_ps = [Tok(), Tok()]
            po = pm("bpo", [128, 2, 512], F32, ph); t_po = Tok()
            pT = pm("bpT", [128, 4, 128], BF16, ph); t_pT = Tok()
            for i in range(6):
                P.op("pool", lambda e: e.memset(vw[i][:], 1.0), W=[t_vw[i]])
            for i in range(2):
                P.op("pool", lambda e: e.memset(vc[i][:], 1.0), W=[t_vc[i]])
            P.barrier()
            P.dma("sp", kc[:], kT_v[:, :, S:S + LC], W=[t_kc])
            for i in range(2):
                P.dma("sp", vc[i][:, :, 0:64], vA_d[S + i * 128:S + (i + 1) * 128, :].rearrange("t (h d) -> t h d", h=8), W=[t_vc[i]])
            loaded = {}
            cur_var = [-1]
            tiles = list(range(32)) + ([32, 33] if with_ctx else [])
            hcount = 0
            pend = []
            dgen = setup_gen(defer_setup, ph, 384) if defer_setup is not None else None
            for qi_, i in enumerate(tiles):
                qb = qi_ % 2
                P.dma("sp", qt[qb][:], qT_v[:, :, i * 128:(i + 1) * 128], W=[t_qt[qb]])
                chunks = []
                if i < 32:
                    E = min(max(2 * i - 4, 0), 54)
                    var = (2 * i - E) // 2
                    for c in range(5):
                        kt = E // 2 + c
                        slot = kt % 6
                        if loaded.get(slot) != kt:
                            P.dma("sp", kw[slot][:], kT_v[:, :, kt * 128:(kt + 1) * 128], W=[t_kw[slot]])
                            P.dma("sp", vw[slot][:, :, 0:64], vA_d[kt * 128:(kt + 1) * 128, :].rearrange("t (h d) -> t h d", h=8), W=[t_vw[slot]])
                            loaded[slot] = kt
                        chunks.append((kw[slot], None, vw[slot], [t_kw[slot]], [t_vw[slot]], c))
                    if var != cur_var[0]:
                        cur_var[0] = var
                        P.dma("sp", bmk[:], bmask_d[var], W=[t_bmk])
                        for h in range(8):
                            sbi = h % 2
                            P.dma("sp", bst[sbi][:], rpbT_d[l, var, :, h, :, :], W=[t_bst[sbi]])
                            tt("dve", bst[sbi][:], bst[sbi][:], bmk[:], ALU.add, R=[t_bst[sbi], t_bmk], W=[t_bst[sbi]])
                            act(bias[:, h, :, :], bst[sbi][:], AF.Exp, R=[t_bst[sbi]], W=[t_bias])
                for c in range(2):
                    chunks.append((kc, c, vc[c], [t_kc], [t_vc[c]], None))
                ncn = len(chunks)
                for h in range(8):
                    j = h // 2; hp = (h % 2) * 64
                    pb = hcount % 2; hcount += 1
                    for ci, (ktile, csub, vtile, tk, tv, loc) in enumerate(chunks):
                        kap = ktile[hp:hp + 64, j, :] if csub is None else ktile[hp:hp + 64, j, csub * 128:(csub + 1) * 128]
                        last = (ci == ncn - 1)
                        mm(ps[pb][:, ci, :], kap, qt[qb][hp:hp + 64, j, :], True, True, R=tk + [t_qt[qb]], W=[t_ps[pb]], inc=last)
                    n1 = min(ncn, 4)
                    act(PT[pb][:, 0:n1, :], ps[pb][:, 0:n1, :], AF.Exp, R=[t_ps[pb]], W=[t_PT[pb]])
                    if ncn > 4:
                        act(PT[pb][:, 4:ncn, :], ps[pb][:, 4:ncn, :], AF.Exp, R=[t_ps[pb]], W=[t_PT[pb]])
                    if i < 32:
                        tt("dve", PT[pb][:, 0:5, :], PT[pb][:, 0:5, :], bias[:, h, :, :], ALU.mult, R=[t_PT[pb], t_bias], W=[t_PT[pb]])
                    def pv(h=h, pb=pb, chunks=chunks, ncn=ncn, qb=qb, i=i):
                        for ci, (ktile, csub, vtile, tk, tv, loc) in enumerate(chunks):
                            mm(po[:, h // 4, (h % 4) * 66:(h % 4) * 66 + 66], PT[pb][:, ci, :], vtile[:, h, 0:66], ci == 0, ci == ncn - 1,
                               R=[t_PT[pb]] + tv, W=[t_po])
                        if h < 7:
                            return
                        po4 = po[:, :, 0:264].rearrange("p b (h e) -> p b h e", e=66)
                        recip(rden[:], po4[:, :, :, 64], R=[t_po], W=[t_rden])
                        tt("dve", onb[:].rearrange("p (b h e) -> p b h e", b=2, h=4), po4[:, :, :, 0:64],
                           rden[:].unsqueeze(3).to_broadcast([128, 2, 4, 64]), ALU.mult, R=[t_po, t_rden], W=[t_onb])
                        for j in range(4):
                            tr(pT[:, j, :], onb[:, j * 128:(j + 1) * 128], identb[:], R=[t_onb, t_c], W=[t_pT], inc=(j == 3))
                        cp("dve", oTs[qb][:], pT[:], R=[t_pT], W=[t_oTs[qb]])
                        P.dma("pool", oT_v[:, 0:4, i * 128:(i + 1) * 128], oTs[qb][:], R=[t_oTs[qb]], W=[Tok()])
                    if pend:
                        pend.pop(0)()
                    pend.append(pv)
                    if dgen is not None and qi_ >= 6:
                        next(dgen, None)
            while pend:
                pend.pop(0)()
            if dgen is not None:
                for _ in dgen:
                    pass
            P.barrier()

    def phaseB2(l, with_ctx):
        with ExitStack() as ph:
            cw = sb("ccw", [128, 2, 31], F32, ph); cv = sb("ccv", [128, 4, 2], F32, ph); t_cw = Tok()
            pw = sb("cpw", [128, 2, 256], BF16, ph); t_pw = Tok()
            hw = [[[sb("chw%d_%d_%d" % (i, ch, o), [128, NB + 32], BF16, ph) for o in range(2)] for ch in range(2)] for i in range(2)]
            t_hw = [[[Tok(), Tok()] for ch in range(2)] for i in range(2)]
            dg = sb("cdg", [128, 2, 31, 128], BF16, ph); t_dg = Tok()
            pcv = [pm("cpcv%d" % ch, [128, NB], F32, ph) for ch in range(2)]; t_pcv = [Tok(), Tok()]
            acc = [sb("cacc%d" % ch, [128, NB], F32, ph) for ch in range(2)]; t_acc = [Tok(), Tok()]
            sqc = [sb("csq%d" % ch, [128, NB], F32, ph) for ch in range(2)]; t_sqc = [Tok(), Tok()]
            mean = sb("cmean", [128, NB], F32, ph); t_mean = Tok()
            m2 = sb("cm2", [128, NB], F32, ph); t_m2 = Tok()
            rstd = sb("crstd", [128, NB], F32, ph); t_rstd = Tok()
            yn = [sb("cyn%d" % ch, [128, NB], F32, ph) for ch in range(2)]; t_yn = [Tok(), Tok()]
            yc = [sb("cyc%d" % ch, [128, NB], BF16, ph) for ch in range(2)]; t_yc = [Tok(), Tok()]
            ob = [sb("cob%d" % ch, [128, NB], BF16, ph) for ch in range(2)]; t_ob = [Tok(), Tok()]
            pmean = pm("cpm", [128, NB], F32, ph); t_pmean = Tok()
            pex2 = pm("cpe", [128, NB], F32, ph); t_pex2 = Tok()
            ppw = [pm("cpp%d" % i, [128, NB], F32, ph) for i in range(2)]; t_ppw = [Tok(), Tok()]
            P.dma("sp", cw[:], convw_d[l], W=[t_cw])
            P.barrier()
            P.dma("sp", cv[:], cvec_d[l], W=[t_cw])
            P.dma("pool", pw[:], pww_d[l].rearrange("(k p) n -> p k n", p=128), W=[t_pw])
            P.barrier()
            for ch in range(2):
                for j in range(31):
                    ts("dve" if j % 2 else "pool", dg[:, ch, j, :], identf[:], cw[:, ch, j:j + 1], None, ALU.mult, None, R=[t_c, t_cw], W=[t_dg])
            P.barrier()
            bl = blocks(with_ctx)

            def ld(bi):
                t0, n = bl[bi]
                c0 = t0 if t0 < S else (15 + S + 15 + (t0 - S))
                for ch in range(2):
                    for o in range(2):
                        P.dma("sp", hw[bi % 2][ch][o][:, :n + 30], hcT_d[ch * 128:(ch + 1) * 128, c0 + o:c0 + o + n + 30], W=[t_hw[bi % 2][ch][o]])
            ld(0)
            for bi, (t0, n) in enumerate(bl):
                b = bi % 2
                if bi + 1 < len(bl):
                    ld(bi + 1)
                for ch in range(2):
                    for j in range(31):
                        o = j % 2
                        mm(pcv[ch][:, :n], dg[:, ch, j, :], hw[b][ch][o][:, j - o:j - o + n], j == 0, j == 30,
                           R=[t_dg, t_hw[b][ch][o]], W=[t_pcv[ch]])
                    act(acc[ch][:, :n], pcv[ch][:, :n], AF.Identity, R=[t_pcv[ch], t_cw], W=[t_acc[ch]], scale=1.0, bias=cv[:, 0, ch:ch + 1])
                for ch in range(2):
                    act(sqc[ch][:, :n], acc[ch][:, :n], AF.Square, R=[t_acc[ch]], W=[t_sqc[ch]])
                for ch in range(2):
                    mm(pmean[:, :n], onesf[:], acc[ch][:, :n], ch == 0, ch == 1, R=[t_acc[ch], t_c], W=[t_pmean])
                for ch in range(2):
                    mm(pex2[:, :n], onesf[:], sqc[ch][:, :n], ch == 0, ch == 1, R=[t_sqc[ch], t_c], W=[t_pex2])
                cp("act", mean[:, :n], pmean[:, :n], R=[t_pmean], W=[t_mean])
                act(m2[:, :n], pmean[:, :n], AF.Square, R=[t_pmean], W=[t_m2])
                tt("dve", rstd[:, :n], pex2[:, :n], m2[:, :n], ALU.subtract, R=[t_pex2, t_m2], W=[t_rstd])
                act(rstd[:, :n], rstd[:, :n], AF.Ln, R=[t_rstd], W=[t_rstd], scale=1.0, bias=EPS)
                act(rstd[:, :n], rstd[:, :n], AF.Exp, R=[t_rstd], W=[t_rstd], scale=-0.5)
                for ch in range(2):
                    tt("dve", yn[ch][:, :n], acc[ch][:, :n], mean[:, :n], ALU.subtract, R=[t_acc[ch], t_mean], W=[t_yn[ch]])
                    tt("dve", yn[ch][:, :n], yn[ch][:, :n], rstd[:, :n], ALU.mult, R=[t_yn[ch], t_rstd], W=[t_yn[ch]])
                    act(yc[ch][:, :n], yn[ch][:, :n], AF.Silu, R=[t_yn[ch], t_cw], W=[t_yc[ch]],
                        scale=cv[:, 1, ch:ch + 1], bias=cv[:, 2, ch:ch + 1])
                for oc in range(2):
                    for ch in range(2):
                        mm(ppw[oc][:, :n], pw[:, ch, oc * 128:(oc + 1) * 128], yc[ch][:, :n], ch == 0, ch == 1,
                           R=[t_pw, t_yc[ch]], W=[t_ppw[oc]])
                    act(ob[oc][:, :n], ppw[oc][:, :n], AF.Identity, R=[t_ppw[oc], t_cw], W=[t_ob[oc]], scale=1.0, bias=cv[:, 3, oc:oc + 1])
                    P.dma("pool", oT_v[:, 4 + oc, t0:t0 + n], ob[oc][:, :n], R=[t_ob[oc]], W=[Tok()])
            P.barrier()

    def phaseB3(l, with_ctx):
        with ExitStack() as ph:
            tri = sb("gtri", [128, 2, 128], F32, ph); lm = sb("glm", [128, 2, 128], F32, ph)
            bdq = sb("gbdq", [128, 4, 128], BF16, ph); bds = sb("gbds", [128, 256], F32, ph)
            gwt = sb("ggw", [33, 2, 128], BF16, ph); t_k = Tok()
            P.dma("sp", tri[:], tri_d.rearrange("a s t -> s a t"), W=[t_k])
            P.barrier()
            P.dma("sp", lm[:], lmat_d.rearrange("a s t -> s a t"), W=[t_k])
            P.barrier()
            P.dma("pool", bdq[:], bdq_d, W=[t_k])
            P.barrier()
            P.dma("sp", bds[:], bds_d, W=[t_k])
            P.barrier()
            P.dma("pool", gwt[:], gw_d[l].rearrange("a k n -> k a n"), W=[t_k])
            P.barrier()
            Dd = []
            for dr in range(2):
                d = {}
                pf = "g%d" % dr

                def two(name, shape, dt):
                    return [sb(pf + name + str(i), shape, dt, ph) for i in range(2)], [Tok(), Tok()]
                d["zt"], d["t_zt"] = two("zt", [33, 128], BF16)
                d["qq"], d["t_qq"] = two("qq", [128, 128], BF16)
                d["kk"], d["t_kk"] = two("kk", [128, 128], BF16)
                d["vt"], d["t_vt"] = two("vt", [128, 256], BF16)
                d["ec"], d["t_ec"] = two("ec", [128, 128], F32)
                d["qe"], d["t_qe"] = two("qe", [128, 128], BF16)
                d["keT"], d["t_keT"] = two("keT", [128, 128], BF16)
                d["qbd"], d["t_qbd"] = two("qbd", [128, 4, 128], BF16)
                d["ke"], d["t_ke"] = two("ke", [128, 128], BF16)
                d["o1"], d["t_o1"] = two("o1", [128, 256], F32)
                for nm, shape, dt in (("e1", [128, 128], F32), ("sp", [128, 128], F32), ("enc", [128, 128], F32),
                                      ("attm", [128, 4, 128], BF16), ("Sf", [128, 256], F32), ("Sb", [128, 256], BF16),
                                      ("T1", [128, 256], F32)):
                    d[nm] = sb(pf + nm, shape, dt, ph); d["t_" + nm] = Tok()
                d["X"] = pm(pf + "X", [128, 2, 128], F32, ph); d["t_X"] = Tok()
                d["pk"] = pm(pf + "pk", [128, 128], BF16, ph); d["t_pk"] = Tok()
                d["pad"] = pm(pf + "pad", [128, 512], F32, ph); d["t_pad"] = Tok()
                d["pout"] = pm(pf + "pout", [128, 256], F32, ph); d["t_pout"] = Tok()
                d["order"] = [32, 33] + list(range(32)) if dr == 0 else [33, 32] + list(range(31, -1, -1))
                d["t_go"] = Tok()
                for i in range(2):
                    P.op("pool", lambda e: e.memset(d["zt"][i][:], 0.0), W=[d["t_zt"][i]])
                    P.op("pool", lambda e: e.memset(d["zt"][i][32:33, :], 1.0), W=[d["t_zt"][i]])
                P.op("pool", lambda e: e.memset(d["Sf"][:], 0.0), W=[d["t_Sf"]])
                P.op("pool", lambda e: e.memset(d["Sb"][:], 0.0), W=[d["t_Sb"]])
                Dd.append(d)
            P.barrier()
            go_d = [gof_d, gob_d]

            def LD(dr, oi):
                d = Dd[dr]; g = d["order"][oi]; b = oi % 2
                P.dma("sp", d["zt"][b][0:16, :], gz_d[dr, :, g * 128:(g + 1) * 128], W=[d["t_zt"][b]])
                P.dma("sp", d["qq"][b][:], gq_d[:, g * 128:(g + 1) * 128], W=[d["t_qq"][b]])
                P.dma("sp", d["kk"][b][:], gk_d[:, g * 128:(g + 1) * 128], W=[d["t_kk"][b]])
                P.dma("sp", d["vt"][b][:], gv_d[g * 128:(g + 1) * 128, :], W=[d["t_vt"][b]])

            def S1(dr, oi):
                d = Dd[dr]; b = oi % 2
                pa = d["X"][:, 0, :]; pc = d["X"][:, 1, :]
                mm(pa, d["zt"][b][0:33, :], gwt[0:33, dr, :], True, True, R=[d["t_zt"][b], t_k], W=[d["t_X"]])
                yield
                act(d["e1"][:], pa, AF.Exp, R=[d["t_X"]], W=[d["t_e1"]], scale=-1.0)
                yield
                act(d["sp"][:], d["e1"][:], AF.Ln, R=[d["t_e1"]], W=[d["t_sp"]], scale=1.0, bias=1.0)
                yield
                mm(pc, d["sp"][:], lm[:, dr, :], True, True, R=[d["t_sp"], t_k], W=[d["t_X"]])
                yield
                act(d["ec"][b][:], pc, AF.Exp, R=[d["t_X"]], W=[d["t_ec"][b]])
                yield
                act(d["enc"][:], pc, AF.Exp, R=[d["t_X"]], W=[d["t_enc"]], scale=-1.0)
                yield
                tt("dve", d["qe"][b][:], d["qq"][b][:], d["ec"][b][:], ALU.mult, R=[d["t_qq"][b], d["t_ec"][b]], W=[d["t_qe"][b]])
                yield
                tt("dve", d["keT"][b][:], d["kk"][b][:], d["enc"][:], ALU.mult, R=[d["t_kk"][b], d["t_enc"]], W=[d["t_keT"][b]])
                yield
                tt("dve", d["qbd"][b][:], d["qe"][b][:].unsqueeze(1).to_broadcast([128, 4, 128]), bdq[:], ALU.mult,
                   R=[d["t_qe"][b], t_k], W=[d["t_qbd"][b]])
                yield
                tr(d["pk"][:], d["keT"][b][:], identb[:], R=[d["t_keT"][b], t_c], W=[d["t_pk"]])
                yield
                cp("act", d["ke"][b][:], d["pk"][:], R=[d["t_pk"]], W=[d["t_ke"][b]])
                yield

            def S2(dr, oi):
                d = Dd[dr]; b = oi % 2
                g = d["order"][oi]
                need_out = (g < 32) or with_ctx
                patt = d["pad"][:].rearrange("p (h t) -> p h t", h=4)
                pds = d["pad"][:, 0:256]
                if need_out:
                    mm(d["pad"][:], d["keT"][b][:], d["qbd"][b][:].rearrange("p h t -> p (h t)"), True, True,
                       R=[d["t_keT"][b], d["t_qbd"][b]], W=[d["t_pad"]])
                    yield
                    tt("dve", d["attm"][:], patt, tri[:, dr, :].unsqueeze(1).to_broadcast([128, 4, 128]), ALU.mult,
                       R=[d["t_pad"], t_k], W=[d["t_attm"]])
                    yield
                    for h in range(4):
                        mm(d["pout"][:, h * 64:(h + 1) * 64], d["qe"][b][:], d["Sb"][:, h * 64:(h + 1) * 64], True, False,
                           R=[d["t_qe"][b], d["t_Sb"]], W=[d["t_pout"]], inc=False)
                        yield
                        mm(d["pout"][:, h * 64:(h + 1) * 64], d["attm"][:, h, :], d["vt"][b][:, h * 64:(h + 1) * 64], False, True,
                           R=[d["t_attm"], d["t_vt"][b]], W=[d["t_pout"]], inc=(h == 3))
                        yield
                mm(pds, d["ke"][b][:], d["vt"][b][:], True, True, R=[d["t_ke"][b], d["t_vt"][b]], W=[d["t_pad"]])
                yield
                tt("dve", d["T1"][:], pds, bds[:], ALU.mult, R=[d["t_pad"], t_k], W=[d["t_T1"]])
                yield
                tt("dve", d["T1"][:], d["T1"][:], d["Sf"][:], ALU.add, R=[d["t_T1"], d["t_Sf"]], W=[d["t_T1"]])
                yield
                col = 127 if dr == 0 else 0
                ts("dve", d["Sf"][:], d["T1"][:], d["ec"][b][:, col:col + 1], None, ALU.mult, None, R=[d["t_T1"], d["t_ec"][b]], W=[d["t_Sf"]])
                yield
                act(d["Sb"][:], d["T1"][:], AF.Copy, R=[d["t_T1"], d["t_ec"][b]], W=[d["t_Sb"]], scale=d["ec"][b][:, col:col + 1])
                yield
                if need_out:
                    cp("act", d["o1"][b][:], d["pout"][:], R=[d["t_pout"]], W=[d["t_o1"][b]])
                    yield
                    P.dma("pool", go_d[dr][g * 128:(g + 1) * 128, :], d["o1"][b][:], R=[d["t_o1"][b]], W=[d["t_go"]])
                    yield
            def rr(gens):
                gens = list(gens)
                while gens:
                    for gq in list(gens):
                        try:
                            next(gq)
                        except StopIteration:
                            gens.remove(gq)
            for dr in range(2):
                LD(dr, 0)
            rr([S1(0, 0), S1(1, 0)])
            for oi in range(34):
                gl = [S2(0, oi), S2(1, oi)]
                if oi + 1 < 34:
                    for dr in range(2):
                        LD(dr, oi + 1)
                    gl = [S1(0, oi + 1), S2(0, oi), S1(1, oi + 1), S2(1, oi)]
                rr(gl)
            P.barrier()
        with ExitStack() as ph:
            gog = sb("ggog", [128, 256], F32, ph); t_k2 = Tok()
            P.dma("sp", gog[:], goutg_d[l], W=[t_k2])
            of = [sb("hof%d" % i, [128, 256], F32, ph) for i in range(2)]; t_of = [Tok(), Tok()]
            ob_ = [sb("hob%d" % i, [128, 256], F32, ph) for i in range(2)]; t_ob_ = [Tok(), Tok()]
            rt = [sb("hrt%d" % i, [128, 256], BF16, ph) for i in range(2)]; t_rt = [Tok(), Tok()]
            o1 = [sb("ho1%d" % i, [128, 256], F32, ph) for i in range(2)]; t_o1 = [Tok(), Tok()]
            o2 = [sb("ho2%d" % i, [128, 256], F32, ph) for i in range(2)]; t_o2 = [Tok(), Tok()]
            ss4 = [sb("hss%d" % i, [128, 4], F32, ph) for i in range(2)]; t_ss4 = [Tok(), Tok()]
            sr = [sb("hsr%d" % i, [128, 256], F32, ph) for i in range(2)]; t_sr = [Tok(), Tok()]
            oc = [sb("hoc%d" % i, [128, 256], BF16, ph) for i in range(2)]; t_oc = [Tok(), Tok()]
            oTs = [sb("hoTs%d" % i, [128, 2, 128], BF16, ph) for i in range(2)]; t_oTs = [Tok(), Tok()]
            pT = [pm("hpT%d" % i, [128, 2, 128], BF16, ph) for i in range(2)]; t_pT = [Tok(), Tok()]
            tiles = list(range(32)) + ([32, 33] if with_ctx else [])

            def ldo(ti):
                g = tiles[ti]; b = ti % 2
                P.dma("sp", of[b][:], gof_d[g * 128:(g + 1) * 128, :], W=[t_of[b]])
                P.dma("sp", ob_[b][:], gob_d[g * 128:(g + 1) * 128, :], W=[t_ob_[b]])
                P.dma("sp", rt[b][:], gr_d[g * 128:(g + 1) * 128, :], W=[t_rt[b]])
            def out_gen(ti):
                g = tiles[ti]; b = ti % 2
                tt("dve", o1[b][:], of[b][:], ob_[b][:], ALU.add, R=[t_of[b], t_ob_[b]], W=[t_o1[b]])
                yield
                tt("dve", o2[b][:], o1[b][:], o1[b][:], ALU.mult, R=[t_o1[b]], W=[t_o2[b]])
                yield
                P.op("dve", lambda e: e.tensor_reduce(out=ss4[b][:], in_=o2[b][:].rearrange("p (h d) -> p h d", h=4), axis=AX.X, op=ALU.add),
                     R=[t_o2[b]], W=[t_ss4[b]])
                yield
                act(ss4[b][:], ss4[b][:], AF.Ln, R=[t_ss4[b]], W=[t_ss4[b]], scale=1.0 / 64, bias=EPS)
                yield
                act(ss4[b][:], ss4[b][:], AF.Exp, R=[t_ss4[b]], W=[t_ss4[b]], scale=-0.5)
                yield
                act(sr[b][:], rt[b][:], AF.Exp, R=[t_rt[b]], W=[t_sr[b]], scale=-1.0)
                yield
                act(sr[b][:], sr[b][:], AF.Ln, R=[t_sr[b]], W=[t_sr[b]], scale=1.0, bias=1.0)
                yield
                act(sr[b][:], sr[b][:], AF.Exp, R=[t_sr[b]], W=[t_sr[b]], scale=-1.0)
                yield
                tt("dve", o2[b][:].rearrange("p (h d) -> p h d", h=4), o1[b][:].rearrange("p (h d) -> p h d", h=4),
                   ss4[b][:].unsqueeze(2).to_broadcast([128, 4, 64]), ALU.mult, R=[t_o1[b], t_ss4[b]], W=[t_o2[b]])
                yield
                tt("dve", o2[b][:], o2[b][:], gog[:], ALU.mult, R=[t_o2[b], t_k2], W=[t_o2[b]])
                yield
                tt("dve", o2[b][:], o2[b][:], sr[b][:], ALU.mult, R=[t_o2[b], t_sr[b]], W=[t_o2[b]])
                yield
                tt("dve", oc[b][:], o2[b][:], rt[b][:], ALU.mult, R=[t_o2[b], t_rt[b]], W=[t_oc[b]])
                yield
                for j in range(2):
                    tr(pT[b][:, j, :], oc[b][:, j * 128:(j + 1) * 128], identb[:], R=[t_oc[b], t_c], W=[t_pT[b]], inc=(j == 1))
                yield
                cp("act", oTs[b][:], pT[b][:], R=[t_pT[b]], W=[t_oTs[b]])
                yield
                P.dma("pool", oT_v[:, 6:8, g * 128:(g + 1) * 128], oTs[b][:], R=[t_oTs[b]], W=[Tok()])
                yield

            def rr2(gens):
                gens = list(gens)
                while gens:
                    for gq in list(gens):
                        try:
                            next(gq)
                        except StopIteration:
                            gens.remove(gq)
            ldo(0)
            if len(tiles) > 1:
                ldo(1)
            for ti in range(0, len(tiles), 2):
                gl = [out_gen(ti)]
                if ti + 1 < len(tiles):
                    gl.append(out_gen(ti + 1))
                rr2(gl)
                for tj in (ti + 2, ti + 3):
                    if tj < len(tiles):
                        ldo(tj)
            P.barrier()

    def phaseC(l, t_wo, t_w1, t_w2, with_ctx, last):
        W1 = WA[:].rearrange("p (k n) -> p k n", k=8)
        W2 = WB[:].rearrange("p (f n) -> p f n", f=32)
        with ExitStack() as ph:
            tl = norm_tiles(ph, "c")
            xb = [sb("cxb%d" % i, [128, 8, NB], F32, ph) for i in range(2)]; t_xb = [Tok(), Tok()]
            ob = [sb("cob%d" % i, [128, 8, NB], BF16, ph) for i in range(2)]; t_ob = [Tok(), Tok()]
            hT2 = [sb("chT%d" % i, [128, 8, NB], BF16, ph) for i in range(2)]; t_hT2 = [Tok(), Tok()]
            hid = sb("chid", [128, 32, NB], BF16, ph); t_hid = [Tok() for _ in range(32)]
            rl = [sb("crl%d" % i, [128, NB], F32, ph) for i in range(2)]; t_rl = [Tok(), Tok()]
            pacc = [pm("cpa%d" % i, [128, NB], F32, ph) for i in range(2)]; t_pacc = [Tok(), Tok()]
            pup = [pm("cpu%d" % i, [128, NB], F32, ph) for i in range(2)]; t_pup = [Tok(), Tok()]
            if last:
                pTo = pm("cpTo", [128, 8, 128], F32, ph); t_pTo = Tok()
                yo = [sb("cyo%d" % i, [128, 512], F32, ph) for i in range(2)]; t_yo = [Tok(), Tok()]
            bl = blocks(with_ctx)

            def ld(bi):
                t0, n = bl[bi]
                P.dma("sp", xb[bi % 2][:, :, :n], xs_v[:, :, t0:t0 + n], R=[t_xs], W=[t_xb[bi % 2]])
                P.dma("sp", ob[bi % 2][:, :, :n], oT_v[:, :, t0:t0 + n], W=[t_ob[bi % 2]])
            ld(0)
            cc_ = {"ca": 0}

            def front(bj):
                tj, nj = bl[bj]
                bb = bj % 2
                vv = 0 if tj < S else 1
                for nn in range(8):
                    pb = cc_["ca"] % 2; cc_["ca"] += 1
                    for k in range(8):
                        mm(pacc[pb][:, :nj], WO[:, k, nn * 128:(nn + 1) * 128], ob[bb][:, k, :nj], k == 0, k == 7,
                           R=[t_wo[k], t_ob[bb]], W=[t_pacc[pb]])
                    stt("dve", xb[bb][:, nn, :nj], pacc[pb][:, :nj], modT[:, 16 + nn, vv:vv + 1], xb[bb][:, nn, :nj], ALU.mult, ALU.add,
                        R=[t_pacc[pb], t_mod, t_xb[bb]], W=[t_xb[bb]])
                norm_mod(xb[bb], t_xb[bb], nj, hT2[bb], t_hT2[bb], gsc2, 24, vv, tl)
            if len(bl) > 1:
                ld(1)
            front(0)
            cu = 0; cy = 0
            for bi, (t0, n) in enumerate(bl):
                b = bi % 2
                v = 0 if t0 < S else 1
                hT = hT2[b]; t_hT = t_hT2[b]
                for f in range(32):
                    pb = cu % 2; cu += 1
                    for k in range(8):
                        mm(pup[pb][:, :n], W1[:, k, f * 128:(f + 1) * 128], hT[:, k, :n], k == 0, k == 7,
                           R=[t_w1[k], t_hT], W=[t_pup[pb]])
                    act(rl[pb][:, :n], pup[pb][:, :n], AF.Relu, R=[t_pup[pb]], W=[t_rl[pb]])
                    tt("pool" if f % 2 else "dve", hid[:, f, :n], rl[pb][:, :n], rl[pb][:, :n], ALU.mult, R=[t_rl[pb]], W=[t_hid[f]])
                if bi + 1 < len(bl):
                    front(bi + 1)
                for nn in range(8):
                    pb = cc_["ca"] % 2; cc_["ca"] += 1
                    for f in range(32):
                        mm(pacc[pb][:, :n], W2[:, f, nn * 128:(nn + 1) * 128], hid[:, f, :n], f == 0, f == 31,
                           R=[t_w2[f // 4], t_hid[f]], W=[t_pacc[pb]])
                    stt("dve", xb[b][:, nn, :n], pacc[pb][:, :n], modT[:, 40 + nn, v:v + 1], xb[b][:, nn, :n], ALU.mult, ALU.add,
                        R=[t_pacc[pb], t_mod, t_xb[b]], W=[t_xb[b]])
                if not last:
                    P.dma("pool", xs_v[:, :, t0:t0 + n], xb[b][:, :, :n], R=[t_xb[b]], W=[t_xs])
                else:
                    for ti in range(n // 128):
                        for k in range(8):
                            tr(pTo[:, k, :], xb[b][:, k, ti * 128:(ti + 1) * 128], identf[:], R=[t_xb[b], t_c], W=[t_pTo], inc=(k == 7))
                        g0 = t0 + ti * 128
                        cp("act", yo[0][:], pTo[:, 0:4, :].rearrange("p k t -> p (k t)"), R=[t_pTo], W=[t_yo[0]])
                        P.dma("pool", y_out[g0:g0 + 128, 0:512], yo[0][:], R=[t_yo[0]], W=[Tok()])
                        cp("dve", yo[1][:], pTo[:, 4:8, :].rearrange("p k t -> p (k t)"), R=[t_pTo], W=[t_yo[1]])
                        P.dma("pool", y_out[g0:g0 + 128, 512:1024], yo[1][:], R=[t_yo[1]], W=[Tok()])
                if bi + 2 < len(bl):
                    ld(bi + 2)
            P.barrier()

    stages = []
    p0gen = phase0()
    next(p0gen)
    stages.append("p0")
    for l in range(DEPTH):
        last = (l == DEPTH - 1)
        with_ctx_out = not last
        if stop_after is not None and stages and stages[-1] == stop_after:
            break
        if l == 0:
            with ExitStack() as phs:
                npump = 0
                for _ in setup_gen(0, phs, 768):
                    if npump < 32:
                        next(p0gen, None)
                        npump += 1
                P.barrier()
            for _ in p0gen:
                pass
        modT = modTs[l]; gsc1 = gsc1s[l]; gsc2 = gsc2s[l]; t_mod = t_mods[l]
        stages.append("S%d" % l)
        if stop_after == stages[-1]:
            break
        t_win = load_w(WA[:, 0:8 * NCOL].rearrange("p (k n) -> p k n", k=8), win_d[l].rearrange("(k p) n -> p k n", p=128), 8, NCOL)
        t_wo = load_w(WO, wout_d[l].rearrange("(k p) n -> p k n", p=128), 8, 1024)
        t_w2 = []
        WBv = WB[:].rearrange("p (g m) -> p g m", g=8)
        for g in range(8):
            t = Tok()
            P.dma("pool", WBv[:, g, :].rearrange("p (f n) -> p f n", f=4), w2_d[l][g * 512:(g + 1) * 512, :].rearrange("(f p) n -> p f n", p=128), W=[t])
            t_w2.append(t)
        stages.append("W%d" % l)
        if stop_after == stages[-1]:
            break
        phaseA(l, t_win, True); stages.append("A%d" % l)
        if stop_after == stages[-1]:
            break
        t_w1 = load_w(WA[:].rearrange("p (k n) -> p k n", k=8), w1_d[l].rearrange("(k p) n -> p k n", p=128), 8, 4096)
        phaseB1(l, with_ctx_out, 1 if (l == 0 and DEPTH > 1) else None); stages.append("B1%d" % l)
        if stop_after == stages[-1]:
            break
        phaseB2(l, with_ctx_out); stages.append("B2%d" % l)
        if stop_after == stages[-1]:
            break
        phaseB3(l, with_ctx_out); stages.append("B3%d" % l)
        if stop_after == stages[-1]:
            break
        phaseC(l, t_wo, t_w1, t_w2, with_ctx_out, last); stages.append("C%d" % l)
        if stop_after == stages[-1]:
            break
    P.barrier()
    print("instructions emitted:", P.n_inst, {k: v for k, v in P.cnt.items() if not k.startswith("d_")})
    return nc


_NC_CACHE = {}


def kernel(**inputs):
    sh = _prep_shared(inputs)
    x = np.asarray(inputs["x"], dtype=np.float32)
    c = np.asarray(inputs["c"], dtype=np.float32)
    ctx = np.asarray(inputs["ctx"], dtype=np.float32)
    c_ctx = np.asarray(inputs["c_ctx"], dtype=np.float32)
    B = x.shape[0]
    in_maps = []
    for b in range(B):
        m = dict(sh)
        m["x"] = np.ascontiguousarray(x[b])
        m["ctx"] = np.ascontiguousarray(ctx[b])
        cc = np.stack([c[b].reshape(8, 128).T, c_ctx.reshape(8, 128).T], axis=2)
        m["cc"] = np.ascontiguousarray(cc.astype(np.float32))
        in_maps.append(m)
    if "nc" not in _NC_CACHE:
        _NC_CACHE["nc"] = build()
    res = run_bass_kernel_spmd(_NC_CACHE["nc"], in_maps, core_ids=list(range(B)))
    return np.stack([np.asarray(r["y"], dtype=np.float32) for r in res.results], axis=0)
```

```python
import os
from contextlib import ExitStack
import numpy as np
import concourse.bass as bass
import concourse.mybir as mybir
from concourse.bass_utils import run_bass_kernel_spmd

F32 = mybir.dt.float32
BF16 = mybir.dt.bfloat16
AF = mybir.ActivationFunctionType
ALU = mybir.AluOpType
AX = mybir.AxisListType

D = 1024
S = 4096
LC = 256
NTOK = S + LC
DEPTH = 2
NCOL = 3104
NB = 256
EPS = 1e-6
HCW = 4416
NEG = -30000.0


class Tok:
    __slots__ = ("w", "r")

    def __init__(self):
        self.w = None
        self.r = {}


class Prog:
    NDMA = 12

    def __init__(self, nc):
        self.nc = nc
        self.engs = {"pe": nc.tensor, "dve": nc.vector, "act": nc.scalar, "pool": nc.gpsimd, "sp": nc.sync}
        self.sems = {}
        self.cnt = {}
        self.pending = {}
        for k in self.engs:
            self.sems[k] = nc.alloc_semaphore("s_" + k)
            self.cnt[k] = 0
            self.pending[k] = False
        self.seen = {k: {} for k in self.engs}
        self.dq = {}
        for q in ("sp", "pool"):
            ring = []
            for i in range(self.NDMA):
                key = "d_%s_%d" % (q, i)
                self.sems[key] = nc.alloc_semaphore(key)
                self.cnt[key] = 0
                ring.append(key)
            self.dq[q] = [ring, 0]
        self.n_inst = 0

    def _wait(self, e, deps):
        eng = self.engs[e]
        seen = self.seen[e]
        best = {}
        for (k, v) in deps:
            if k == "pe" and e == "pe":
                continue
            if v > best.get(k, 0):
                best[k] = v
        for k, v in best.items():
            if seen.get(k, 0) >= v:
                continue
            eng.wait_ge(self.sems[k], v)
            seen[k] = v
            self.n_inst += 1

    @staticmethod
    def _deps(R, W):
        deps = []
        for t in R:
            if t.w is not None:
                deps.append(t.w)
        for t in W:
            if t.w is not None:
                deps.append(t.w)
            deps.extend(t.r.items())
        return deps

    def op(self, e, fn, R=(), W=(), inc=True):
        self._wait(e, self._deps(R, W))
        ins = fn(self.engs[e])
        self.n_inst += 1
        v = self.cnt[e] + 1
        if inc:
            ins.then_inc(self.sems[e], 1)
            self.cnt[e] = v
            self.pending[e] = False
        else:
            self.pending[e] = True
        for t in R:
            if t.r.get(e, 0) < v:
                t.r[e] = v
        for t in W:
            t.w = (e, v)
            t.r = {}
        return ins

    def dma(self, q, out, in_, R=(), W=()):
        ring, idx = self.dq[q]
        key = ring[idx % len(ring)]
        self.dq[q][1] = idx + 1
        deps = self._deps(R, W)
        if self.cnt[key] > 0:
            deps.append((key, self.cnt[key]))
        self._wait(q, deps)
        ins = self.engs[q].dma_start(out=out, in_=in_)
        self.n_inst += 1
        v = self.cnt[key] + 16
        ins.then_inc(self.sems[key], 16)
        self.cnt[key] = v
        for t in R:
            if t.r.get(key, 0) < v:
                t.r[key] = v
        for t in W:
            t.w = (key, v)
            t.r = {}
        return ins

    def barrier(self):
        deps = [(k, v) for k, v in self.cnt.items() if v > 0]
        for e in self.engs:
            assert not self.pending[e], e
            self._wait(e, deps)


def _rope_tables():
    pos = np.arange(S)
    inv = (10000.0 ** (-np.arange(8, dtype=np.float32) / 8)).astype(np.float32)
    row = (pos // 64).astype(np.float32)
    col = (pos % 64).astype(np.float32)
    cos = np.ones((32, NTOK), np.float32)
    sin = np.zeros((32, NTOK), np.float32)
    for d in range(32):
        p = row if d < 16 else col
        ang = (p * inv[d % 8]).astype(np.float32)
        sgn = -1.0 if (d % 16) < 8 else 1.0
        cos[d, :S] = np.cos(ang)
        sin[d, :S] = sgn * np.sin(ang)
    cos = np.tile(cos, (4, 1))
    sin = np.tile(sin, (4, 1))
    sc = np.float32(32 ** -0.5)
    return np.stack([cos * sc, sin * sc, cos, sin]).astype(np.float32)


def _bias_index():
    Ev = [0, 0, 0, 54, 54]
    ro = np.zeros((5, 128, 5, 128), np.int64)
    co = np.zeros((5, 128, 5, 128), np.int64)
    mask = np.zeros((5, 128, 5, 128), np.float32)
    ki = np.arange(128)[:, None, None]
    c = np.arange(5)[None, :, None]
    qi = np.arange(128)[None, None, :]
    for vi in range(5):
        E = Ev[vi]
        r = E + 2 * vi
        qr = r + qi // 64
        qc = qi % 64
        kr = E + 2 * c + ki // 64
        kc = ki % 64
        r0 = np.clip(qr - 4, 0, 56)
        cs = np.clip(qc - 8, 0, 48)
        valid = (kr >= r0) & (kr < r0 + 8) & (kc >= cs) & (kc < cs + 16)
        ro[vi] = np.clip(kr - qr + 7, 0, 14)
        co[vi] = np.clip(kc - qc + 15, 0, 30)
        mask[vi] = np.where(valid, 0.0, NEG)
    return ro, co, mask


def _prep_shared(inp):
    f = lambda a: np.ascontiguousarray(np.asarray(a, dtype=np.float32))
    w_in = f(inp["w_in"])
    perm = np.arange(128)
    for h in range(4):
        for d in range(32):
            perm[h * 32 + d] = h * 32 + (d + 8 if (d % 16) < 8 else d - 8)
    qg = w_in[:, :, 2048:2176]
    kg = w_in[:, :, 2176:2304]
    win = np.concatenate([
        w_in[:, :, 0:1536],
        w_in[:, :, 2304:2816],
        w_in[:, :, 1536:2048],
        qg, qg[:, :, perm], kg, kg[:, :, perm],
        w_in[:, :, 2816:2848]], axis=2)
    assert win.shape[2] == NCOL
    sh = {"win": f(win), "w_ada": f(inp["w_ada"]), "w_out": f(inp["w_out"]),
          "w1": f(inp["w_mlp_in"]), "w2": f(inp["w_mlp_out"])}
    sh["badaT"] = f(np.asarray(inp["b_ada"]).reshape(DEPTH, 48, 128).transpose(0, 2, 1))
    sh["n1g"] = f(np.asarray(inp["norm1_g"]).reshape(DEPTH, 8, 128).transpose(0, 2, 1))
    sh["n2g"] = f(np.asarray(inp["norm2_g"]).reshape(DEPTH, 8, 128).transpose(0, 2, 1))
    sh["naqg"] = f(np.tile(np.asarray(inp["na_q_g"])[:, None, None, :], (1, 128, 8, 1)).reshape(DEPTH, 128, 512))
    sh["nakg"] = f(np.tile(np.asarray(inp["na_k_g"])[:, None, None, :], (1, 128, 8, 1)).reshape(DEPTH, 128, 512))
    ro, co, mask = _bias_index()
    rpb = np.asarray(inp["na_rpb"], dtype=np.float32)
    g = rpb[:, :, ro, co]
    sh["rpbT"] = f(g.transpose(0, 2, 3, 1, 4, 5))
    sh["bmask"] = f(mask)
    sh["convw"] = f(np.asarray(inp["conv_w"]).reshape(DEPTH, 31, 2, 128).transpose(0, 3, 2, 1))
    v2 = lambda a: f(np.asarray(a).reshape(DEPTH, 2, 128).transpose(0, 2, 1))
    sh["cvec"] = f(np.stack([v2(inp["conv_b"]), v2(inp["conv_ln_g"]), v2(inp["conv_ln_b"]), v2(inp["conv_pw_b"])], axis=2))
    sh["pww"] = f(inp["conv_pw_w"])
    gw = np.zeros((DEPTH, 2, 33, 128), np.float32)
    gw[:, 0, 0:16] = np.asarray(inp["gla_gw_f"]); gw[:, 0, 32] = np.asarray(inp["gla_gb_f"])
    gw[:, 1, 0:16] = np.asarray(inp["gla_gw_b"]); gw[:, 1, 32] = np.asarray(inp["gla_gb_b"])
    sh["gw"] = gw
    sh["goutg"] = f(np.tile(np.asarray(inp["gla_out_g"])[:, None, None, :], (1, 128, 4, 1)).reshape(DEPTH, 128, 256))
    sh["ident"] = np.eye(128, dtype=np.float32)
    sh["rope"] = _rope_tables()
    s_i = np.arange(128)[:, None]
    t_i = np.arange(128)[None, :]
    tri = np.stack([(s_i <= t_i), (s_i >= t_i)]).astype(np.float32)
    sh["tri"] = tri
    sh["lmat"] = (-tri / 16.0).astype(np.float32)
    hd = np.arange(128) // 32
    sh["bdq"] = f(np.tile((hd[:, None] == np.arange(4)[None, :]).astype(np.float32)[:, :, None], (1, 1, 128)))
    sh["bds"] = f((hd[:, None] == (np.arange(256) // 64)[None, :]).astype(np.float32))
    return sh


def build(dbg=False, stop_after=None):
    nc = bass.Bass("TRN2", target_bir_lowering=False)
    P = Prog(nc)

    def din(name, shape, dt=F32):
        return nc.dram_tensor(name, list(shape), dt, kind="ExternalInput").ap()

    def dscr(name, shape, dt):
        if dbg:
            return nc.dram_tensor(name, list(shape), dt, kind="ExternalOutput").ap()
        return nc.dram_tensor(name, list(shape), dt).ap()

    x_in = din("x", [S, D]); ctx_in = din("ctx", [LC, D]); cc_in = din("cc", [128, 8, 2])
    win_d = din("win", [DEPTH, D, NCOL]); wada_d = din("w_ada", [DEPTH, D, 6 * D])
    wout_d = din("w_out", [DEPTH, D, D]); w1_d = din("w1", [DEPTH, D, 4 * D]); w2_d = din("w2", [DEPTH, 4 * D, D])
    badaT_d = din("badaT", [DEPTH, 128, 48]); n1g_d = din("n1g", [DEPTH, 128, 8]); n2g_d = din("n2g", [DEPTH, 128, 8])
    naqg_d = din("naqg", [DEPTH, 128, 512]); nakg_d = din("nakg", [DEPTH, 128, 512])
    rpbT_d = din("rpbT", [DEPTH, 5, 128, 8, 5, 128]); bmask_d = din("bmask", [5, 128, 5, 128])
    convw_d = din("convw", [DEPTH, 128, 2, 31]); cvec_d = din("cvec", [DEPTH, 128, 4, 2]); pww_d = din("pww", [DEPTH, 256, 256])
    gw_d = din("gw", [DEPTH, 2, 33, 128]); goutg_d = din("goutg", [DEPTH, 128, 256])
    ident_d = din("ident", [128, 128]); rope_d = din("rope", [4, 128, NTOK]); tri_d = din("tri", [2, 128, 128])
    lmat_d = din("lmat", [2, 128, 128]); bdq_d = din("bdq", [128, 4, 128]); bds_d = din("bds", [128, 256])
    y_out = nc.dram_tensor("y", [S, D], F32, kind="ExternalOutput").ap()

    xs_d = dscr("xs", [D, NTOK], F32)
    qT_d = dscr("qT", [512, NTOK], BF16); kT_d = dscr("kT", [512, NTOK], BF16); vA_d = dscr("vA", [NTOK, 512], BF16)
    hcT_d = dscr("hcT", [256, HCW], BF16)
    gq_d = dscr("gq", [128, NTOK], BF16); gk_d = dscr("gk", [128, NTOK], BF16)
    gz_d = dscr("gz", [2, 16, NTOK], BF16)
    gv_d = dscr("gv", [NTOK, 256], BF16); gr_d = dscr("gr", [NTOK, 256], BF16)
    gof_d = dscr("gof", [NTOK, 256], F32); gob_d = dscr("gob", [NTOK, 256], F32)
    oT_d = dscr("oT", [D, NTOK], BF16)
    xs_v = xs_d.rearrange("(k p) t -> p k t", p=128)
    oT_v = oT_d.rearrange("(k p) t -> p k t", p=128)

    uid = [0]

    def sb(name, shape, dt, stack=None):
        uid[0] += 1
        name = "%s_%d" % (name, uid[0])
        if stack is None:
            return nc.alloc_sbuf_tensor(name, list(shape), dt)
        return stack.enter_context(nc.sbuf_tensor(name, list(shape), dt))

    def pm(name, shape, dt, stack):
        uid[0] += 1
        name = "%s_%d" % (name, uid[0])
        return stack.enter_context(nc.psum_tensor(name, list(shape), dt))

    def mm(out, lhsT, rhs, start, stop, R, W, inc=None):
        if inc is None:
            inc = stop
        P.op("pe", lambda e: e.matmul(out, lhsT=lhsT, rhs=rhs, start=start, stop=stop), R=R, W=W, inc=inc)

    def tr(out, in_, idn, R, W, inc=True):
        P.op("pe", lambda e: e.transpose(out, in_, idn), R=R, W=W, inc=inc)

    def act(out, in_, func, R, W, bias=None, scale=None, accum=None):
        kw = {}
        if bias is not None:
            kw["bias"] = bias
        if scale is not None:
            kw["scale"] = scale
        if accum is not None:
            kw["accum_out"] = accum
        P.op("act", lambda e: e.activation(out=out, in_=in_, func=func, **kw), R=R, W=W)

    def tt(eng, out, in0, in1, op, R, W):
        P.op(eng, lambda e: e.tensor_tensor(out=out, in0=in0, in1=in1, op=op), R=R, W=W)

    def ts(eng, out, in0, s1, s2, op0, op1, R, W):
        if s2 is None:
            P.op(eng, lambda e: e.tensor_scalar(out=out, in0=in0, scalar1=s1, scalar2=None, op0=op0), R=R, W=W)
        else:
            P.op(eng, lambda e: e.tensor_scalar(out=out, in0=in0, scalar1=s1, scalar2=s2, op0=op0, op1=op1), R=R, W=W)

    def stt(eng, out, in0, scalar, in1, op0, op1, R, W):
        P.op(eng, lambda e: e.scalar_tensor_tensor(out=out, in0=in0, scalar=scalar, in1=in1, op0=op0, op1=op1), R=R, W=W)

    def cp(eng, out, in_, R, W):
        if eng == "act":
            P.op("act", lambda e: e.copy(out=out, in_=in_), R=R, W=W)
        else:
            P.op(eng, lambda e: e.tensor_copy(out=out, in_=in_), R=R, W=W)

    def recip(out, in_, R, W):
        P.op("dve", lambda e: e.reciprocal(out=out, in_=in_), R=R, W=W)

    WA = sb("WA", [128, 32768], BF16)
    WB = sb("WB", [128, 32768], BF16)
    WO = sb("WO", [128, 8, 1024], BF16)
    identb = sb("identb", [128, 128], BF16); identf = sb("identf", [128, 128], F32)
    onesb = sb("onesb", [128, 128], BF16); onesf = sb("onesf", [128, 128], F32)
    ccs = sb("ccs", [128, 8, 2], F32)
    ccsb = sb("ccsb", [128, 8, 2], BF16)
    modTs = [sb("modT%d" % i, [128, 48, 2], F32) for i in range(DEPTH)]
    gsc1s = [sb("gsc1%d" % i, [128, 8, 2], F32) for i in range(DEPTH)]
    gsc2s = [sb("gsc2%d" % i, [128, 8, 2], F32) for i in range(DEPTH)]
    t_mods = [Tok() for _ in range(DEPTH)]
    modT = modTs[0]; gsc1 = gsc1s[0]; gsc2 = gsc2s[0]
    badaTs = [sb("badaTs%d" % i, [128, 48], F32) for i in range(DEPTH)]
    n1gs = [sb("n1gs%d" % i, [128, 8], F32) for i in range(DEPTH)]
    n2gs = [sb("n2gs%d" % i, [128, 8], F32) for i in range(DEPTH)]
    t_c = Tok(); t_cc = Tok(); t_mod = t_mods[0]; t_vec = Tok()
    P.dma("sp", identf[:], ident_d, W=[t_c])
    P.dma("pool", identb[:], ident_d, W=[t_c])
    P.op("pool", lambda e: e.memset(onesb[:], 1.0), W=[t_c])
    P.op("pool", lambda e: e.memset(onesf[:], 1.0 / 256), W=[t_c])
    P.dma("sp", ccs[:], cc_in, W=[t_cc])
    act(ccs[:], ccs[:], AF.Silu, R=[t_cc], W=[t_cc])
    cp("dve", ccsb[:], ccs[:], R=[t_cc], W=[t_cc])
    for i in range(DEPTH):
        P.dma("sp", badaTs[i][:], badaT_d[i], W=[Tok()])
        P.dma("sp", n1gs[i][:], n1g_d[i], W=[Tok()])
        P.dma("sp", n2gs[i][:], n2g_d[i], W=[Tok()])
    P.barrier()

    def blocks(with_ctx):
        bl = [(i * NB, NB) for i in range(S // NB)]
        if with_ctx:
            bl.append((S, LC))
        return bl

    def phase0():
        with ExitStack() as ph:
            xin = [sb("p0x%d" % i, [128, D], F32, ph) for i in range(2)]; t_xin = [Tok(), Tok()]
            xo = [sb("p0o%d" % i, [128, 8, 128], F32, ph) for i in range(2)]; t_xo = [Tok(), Tok()]
            pT = [pm("p0p%d" % i, [128, 8, 128], F32, ph) for i in range(2)]; t_pT = [Tok(), Tok()]
            yield
            for i in range(34):
                b = i % 2
                src = x_in[i * 128:(i + 1) * 128, :] if i < 32 else ctx_in[(i - 32) * 128:(i - 31) * 128, :]
                P.dma("sp", xin[b][:], src, W=[t_xin[b]])
                for k in range(8):
                    tr(pT[b][:, k, :], xin[b][:, k * 128:(k + 1) * 128], identf[:], R=[t_xin[b], t_c], W=[t_pT[b]], inc=(k == 7))
                cp("act", xo[b][:, 0:4, :], pT[b][:, 0:4, :], R=[t_pT[b]], W=[t_xo[b]])
                cp("dve", xo[b][:, 4:8, :], pT[b][:, 4:8, :], R=[t_pT[b]], W=[t_xo[b]])
                P.dma("pool", xs_v[:, :, i * 128:(i + 1) * 128], xo[b][:], R=[t_xo[b]], W=[t_xs])
                yield
            P.barrier()

    t_xs = Tok()

    def setup_gen(l, ph, CB):
        modT = modTs[l]; gsc1 = gsc1s[l]; gsc2 = gsc2s[l]; t_mod = t_mods[l]
        nblk = (6 * D) // CB; cpb = CB // 128
        wst = [sb("wst%d" % i, [128, 8, CB], BF16, ph) for i in range(2)]; t_wst = [[Tok() for _ in range(8)] for _ in range(2)]
        pmod = pm("pmod", [128, 48, 2], F32, ph); t_pmod = Tok()
        tmp = sb("stmp", [128, 8, 2], F32, ph); t_tmp = Tok()
        wv = wada_d[l].rearrange("(k p) n -> p k n", p=128)

        def ldb(jb):
            for k in range(8):
                P.dma("pool", wst[jb % 2][:, k, :], wv[:, k, jb * CB:(jb + 1) * CB], W=[t_wst[jb % 2][k]])
        ldb(0)
        yield
        for jb in range(nblk):
            b = jb % 2
            if jb + 1 < nblk:
                ldb(jb + 1)
                yield
            for jj in range(cpb):
                j = jb * cpb + jj
                for k in range(8):
                    mm(pmod[:, j, :], wst[b][:, k, jj * 128:(jj + 1) * 128], ccsb[:, k, :], k == 0, k == 7,
                       R=[t_wst[b][k], t_cc], W=[t_pmod])
                yield
        tt("dve", modT[:], pmod[:], badaTs[l][:].unsqueeze(2).to_broadcast([128, 48, 2]), ALU.add, R=[t_pmod, t_vec], W=[t_mod])
        ts("dve", tmp[:], modT[:, 8:16, :], 1.0, None, ALU.add, None, R=[t_mod], W=[t_tmp])
        tt("dve", gsc1[:], tmp[:], n1gs[l][:].unsqueeze(2).to_broadcast([128, 8, 2]), ALU.mult, R=[t_tmp, t_vec], W=[t_mod])
        ts("dve", tmp[:], modT[:, 32:40, :], 1.0, None, ALU.add, None, R=[t_mod], W=[t_tmp])
        tt("dve", gsc2[:], tmp[:], n2gs[l][:].unsqueeze(2).to_broadcast([128, 8, 2]), ALU.mult, R=[t_tmp, t_vec], W=[t_mod])
        yield

    def load_w(dst, src_v, nk, width, q="pool"):
        toks = []
        for k in range(nk):
            t = Tok()
            P.dma(q, dst[:, k, :], src_v[:, k, :], W=[t])
            toks.append(t)
        return toks

    def norm_gen(xb, t_xb, n, hT, t_hT, gsc, shbase, v, tl):
        for k in range(8):
            b = k % 2
            act(tl["sq"][b][:, :n], xb[:, k, :n], AF.Square, R=[t_xb], W=[tl["t_sq"][b]])
            mm(tl["pss"][:, :n], onesb[:], tl["sq"][b][:, :n], k == 0, k == 7, R=[tl["t_sq"][b], t_c], W=[tl["t_pss"]], inc=True)
            yield
        act(tl["rs"][:, :n], tl["pss"][:, :n], AF.Ln, R=[tl["t_pss"]], W=[tl["t_rs"]], scale=1.0 / D, bias=EPS)
        act(tl["rs"][:, :n], tl["rs"][:, :n], AF.Exp, R=[tl["t_rs"]], W=[tl["t_rs"]], scale=-0.5)
        yield
        for k in range(8):
            b = k % 2
            tt("dve", tl["tm"][b][:, :n], xb[:, k, :n], tl["rs"][:, :n], ALU.mult, R=[t_xb, tl["t_rs"]], W=[tl["t_tm"][b]])
            act(hT[:, k, :n], tl["tm"][b][:, :n], AF.Identity, R=[tl["t_tm"][b], t_mod], W=[t_hT],
                scale=gsc[:, k, v:v + 1], bias=modT[:, shbase + k, v:v + 1])
            yield

    def norm_mod(*a):
        for _ in norm_gen(*a):
            pass

    def norm_tiles(ph, pfx):
        tl = {}
        tl["sq"] = [sb(pfx + "sq%d" % i, [128, NB], BF16, ph) for i in range(2)]; tl["t_sq"] = [Tok(), Tok()]
        tl["tm"] = [sb(pfx + "tm%d" % i, [128, NB], F32, ph) for i in range(2)]; tl["t_tm"] = [Tok(), Tok()]
        tl["rs"] = sb(pfx + "rs", [128, NB], F32, ph); tl["t_rs"] = Tok()
        tl["pss"] = pm(pfx + "pss", [128, NB], F32, ph); tl["t_pss"] = Tok()
        return tl

    def phaseA(l, t_win, with_ctx):
        WIN = WA[:, 0:8 * NCOL].rearrange("p (k n) -> p k n", k=8)
        with ExitStack() as ph:
            tl = norm_tiles(ph, "a")
            xb = [sb("axb%d" % i, [128, 8, NB], F32, ph) for i in range(2)]; t_xb = [Tok(), Tok()]
            hT = [sb("ahT%d" % i, [128, 8, NB], BF16, ph) for i in range(2)]; t_hT = [Tok(), Tok()]
            qg_s = sb("aqg", [128, 512], F32, ph); kg_s = sb("akg", [128, 512], F32, ph); t_g = Tok()
            rope = sb("arope", [128, 4, NB], F32, ph); t_rope = Tok()
            zero = sb("azero", [128, 32], BF16, ph); t_zero = Tok()
            ptm = [pm("aptm%d" % i, [128, 512], F32, ph) for i in range(2)]; t_ptm = [Tok(), Tok()]
            pfm = [pm("apfm%d" % i, [128, NB], F32, ph) for i in range(2)]; t_pfm = [Tok() for _ in range(2)]
            ptr = [pm("aptr%d" % i, [128, 4, 128], BF16, ph) for i in range(2)]; t_ptr = [Tok(), Tok()]
            st = [sb("ast%d" % i, [128, 512], F32, ph) for i in range(2)]; t_st = [Tok(), Tok()]
            sq2_ = [sb("asq2%d" % i, [128, 512], F32, ph) for i in range(2)]; t_sq2_ = [Tok(), Tok()]
            ssh_ = [sb("assh%d" % i, [128, 8], F32, ph) for i in range(2)]; t_ssh_ = [Tok(), Tok()]
            qn = [sb("aqn%d" % i, [128, 512], BF16, ph) for i in range(2)]; t_qn = [Tok(), Tok()]
            qTs = [sb("aqTs%d" % i, [128, 4, 128], BF16, ph) for i in range(2)]; t_qTs = [Tok(), Tok()]
            vb = [sb("avb%d" % i, [128, 512], BF16, ph) for i in range(2)]; t_vb = [Tok(), Tok()]
            sg_ = [sb("asg%d" % i, [128, NB], F32, ph) for i in range(2)]; t_sg_ = [Tok(), Tok()]
            fo = [sb("afo%d" % i, [128, NB], BF16, ph) for i in range(2)]; t_fo = [Tok(), Tok()]
            r1_ = [sb("ar1%d" % i, [128, NB], F32, ph) for i in range(2)]; t_r1_ = [Tok(), Tok()]
            r2_ = [sb("ar2%d" % i, [128, NB], F32, ph) for i in range(2)]; t_r2_ = [Tok(), Tok()]
            P.dma("sp", qg_s[:], naqg_d[l], W=[t_g])
            P.dma("sp", kg_s[:], nakg_d[l], W=[t_g])
            P.op("pool", lambda e: e.memset(zero[:], 0.0), W=[t_zero])
            P.barrier()
            t_scr = Tok()
            for c0, wd in ((0, 15), (15 + S, 15), (15 + S + 15, 15), (4412 - 15, HCW - 4412 + 15)):
                for ch in range(2):
                    P.dma("pool", hcT_d[ch * 128:(ch + 1) * 128, c0:c0 + wd], zero[:, 0:wd], R=[t_zero], W=[Tok()])
            bl = blocks(with_ctx)
            KA = int(os.environ.get("KA", "255"))
            if os.environ.get("KBL"):
                bl = bl[:int(os.environ["KBL"])]
            def ldx(bj):
                tj, nj = bl[bj]
                P.dma("sp", xb[bj % 2][:, :, :nj], xs_v[:, :, tj:tj + nj], R=[t_xs], W=[t_xb[bj % 2]])

            def emit_norm(bj):
                tj, nj = bl[bj]
                return norm_gen(xb[bj % 2], t_xb[bj % 2], nj, hT[bj % 2], t_hT[bj % 2], gsc1, 0, 0 if tj < S else 1, tl)
            ldx(0)
            if len(bl) > 1:
                ldx(1)
            for _ in emit_norm(0):
                pass
            ng = [None]
            cnt = {"fm": 0, "fo": 0}

            def tile_gen(bi, ti):
                t0, n = bl[bi]; b = bi % 2; g0 = t0 + ti * 128; s_ = ti
                sq2 = sq2_[s_]; t_sq2 = t_sq2_[s_]; ssh = ssh_[s_]; t_ssh = t_ssh_[s_]
                for grp in range(4):
                    for k in range(8):
                        mm(ptm[s_][:], hT[b][:, k, ti * 128:(ti + 1) * 128], WIN[:, k, grp * 512:(grp + 1) * 512], k == 0, k == 7,
                           R=[t_hT[b], t_win[k]], W=[t_ptm[s_]])
                    yield
                    if grp < 2:
                        act(sq2[:], ptm[s_][:], AF.Square, R=[t_ptm[s_]], W=[t_sq2])
                        yield
                        P.op("dve", lambda e: e.tensor_reduce(out=ssh[:], in_=sq2[:].rearrange("p (h d) -> p h d", h=8), axis=AX.X, op=ALU.add),
                             R=[t_sq2], W=[t_ssh])
                        yield
                        if grp == 0:
                            act(ssh[:], ssh[:], AF.Ln, R=[t_ssh], W=[t_ssh], scale=1.0, bias=64 * EPS)
                        else:
                            act(ssh[:], ssh[:], AF.Ln, R=[t_ssh], W=[t_ssh], scale=1.0 / 64, bias=EPS)
                        yield
                        act(ssh[:], ssh[:], AF.Exp, R=[t_ssh], W=[t_ssh], scale=-0.5)
                        yield
                        tt("dve", st[s_][:].rearrange("p (h d) -> p h d", h=8), ptm[s_][:].rearrange("p (h d) -> p h d", h=8),
                           ssh[:].unsqueeze(2).to_broadcast([128, 8, 64]), ALU.mult, R=[t_ptm[s_], t_ssh], W=[t_st[s_]])
                        yield
                        tt("dve", qn[s_][:], st[s_][:], (qg_s if grp == 0 else kg_s)[:], ALU.mult, R=[t_st[s_], t_g], W=[t_qn[s_]])
                        yield
                        for j in range(4):
                            tr(ptr[s_][:, j, :], qn[s_][:, j * 128:(j + 1) * 128], identb[:], R=[t_qn[s_], t_c], W=[t_ptr[s_]], inc=(j == 3))
                        yield
                        cp("act", qTs[s_][:], ptr[s_][:], R=[t_ptr[s_]], W=[t_qTs[s_]])
                        yield
                        dst = (qT_d if grp == 0 else kT_d).rearrange("(j p) t -> p j t", p=128)[:, :, g0:g0 + 128]
                        P.dma("pool", dst, qTs[s_][:], R=[t_qTs[s_]], W=[Tok()])
                        yield
                    else:
                        cp("act" if grp == 2 else "dve", vb[s_][:], ptm[s_][:], R=[t_ptm[s_]], W=[t_vb[s_]])
                        yield
                        if grp == 2:
                            P.dma("pool", vA_d[g0:g0 + 128, :], vb[s_][:], R=[t_vb[s_]], W=[Tok()])
                        else:
                            P.dma("pool", gv_d[g0:g0 + 128, :], vb[s_][:, 0:256], R=[t_vb[s_]], W=[Tok()])
                            P.dma("pool", gr_d[g0:g0 + 128, :], vb[s_][:, 256:512], R=[t_vb[s_]], W=[Tok()])
                        yield

            def fm_gen(bi):
                t0, n = bl[bi]; b = bi % 2

                def fm(col0, width):
                    pb = cnt["fm"] % 2; cnt["fm"] += 1
                    for k in range(8):
                        mm(pfm[pb][0:width, :n], WIN[:, k, col0:col0 + width], hT[b][:, k, :n], k == 0, k == 7,
                           R=[t_hT[b], t_win[k]], W=[t_pfm[pb]])
                    return pb
                hc0 = (15 + t0) if t0 < S else (15 + S + 15 + 15 + (t0 - S))
                for ch in range(2):
                    pa = fm(2048 + ch * 128, 128)
                    yield
                    pg = fm(2304 + ch * 128, 128)
                    yield
                    sg = sg_[ch]; t_sg = t_sg_[ch]
                    act(sg[:, :n], pfm[pg][:, :n], AF.Exp, R=[t_pfm[pg]], W=[t_sg], scale=-1.0)
                    yield
                    act(sg[:, :n], sg[:, :n], AF.Ln, R=[t_sg], W=[t_sg], scale=1.0, bias=1.0)
                    yield
                    act(sg[:, :n], sg[:, :n], AF.Exp, R=[t_sg], W=[t_sg], scale=-1.0)
                    yield
                    fi = cnt["fo"] % 2; cnt["fo"] += 1
                    tt("dve", fo[fi][:, :n], pfm[pa][:, :n], sg[:, :n], ALU.mult, R=[t_pfm[pa], t_sg], W=[t_fo[fi]])
                    yield
                    P.dma("pool", hcT_d[ch * 128:(ch + 1) * 128, hc0:hc0 + n], fo[fi][:, :n], R=[t_fo[fi]], W=[Tok()])
                    yield
                for qi, (dst, cbase) in enumerate(((gq_d, 2560), (gk_d, 2816))):
                    p1 = fm(cbase, 128)
                    yield
                    p2 = fm(cbase + 128, 128)
                    yield
                    r1 = r1_[qi]; t_r1 = t_r1_[qi]; r2 = r2_[qi]; t_r2 = t_r2_[qi]
                    tt("dve", r1[:, :n], pfm[p1][:, :n], rope[:, 2 * qi, :n], ALU.mult, R=[t_pfm[p1], t_rope], W=[t_r1])
                    yield
                    tt("dve", r2[:, :n], pfm[p2][:, :n], rope[:, 2 * qi + 1, :n], ALU.mult, R=[t_pfm[p2], t_rope], W=[t_r2])
                    yield
                    fi = cnt["fo"] % 2; cnt["fo"] += 1
                    tt("dve", fo[fi][:, :n], r1[:, :n], r2[:, :n], ALU.add, R=[t_r1, t_r2], W=[t_fo[fi]])
                    yield
                    P.dma("pool", dst[:, t0:t0 + n], fo[fi][:, :n], R=[t_fo[fi]], W=[Tok()])
                    yield
                for zi in range(2):
                    pz = fm(3072 + zi * 16, 16)
                    yield
                    fi = cnt["fo"] % 2; cnt["fo"] += 1
                    cp("act", fo[fi][0:16, :n], pfm[pz][0:16, :n], R=[t_pfm[pz]], W=[t_fo[fi]])
                    yield
                    P.dma("pool", gz_d[zi, :, t0:t0 + n], fo[fi][0:16, :n], R=[t_fo[fi]], W=[Tok()])
                    yield

            def rr(gens):
                gens = list(gens)
                while gens:
                    for gq in list(gens):
                        try:
                            next(gq)
                        except StopIteration:
                            gens.remove(gq)
            for bi, (t0, n) in enumerate(bl):
                P.dma("sp", rope[:, :, :n], rope_d[:, :, t0:t0 + n].rearrange("a p t -> p a t"), W=[t_rope])
                gl = [tile_gen(bi, ti) for ti in range(n // 128)]
                gl.append(fm_gen(bi))
                if bi + 1 < len(bl):
                    gl.append(emit_norm(bi + 1))
                rr(gl)
                if bi + 2 < len(bl):
                    ldx(bi + 2)
            P.barrier()

    def phaseB1(l, with_ctx, defer_setup=None):
        qT_v = qT_d.rearrange("(j p) t -> p j t", p=128)
        kT_v = kT_d.rearrange("(j p) t -> p j t", p=128)
        with ExitStack() as ph:
            kw = [sb("bk%d" % i, [128, 4, 128], BF16, ph) for i in range(6)]; t_kw = [Tok() for _ in range(6)]
            vw = [sb("bv%d" % i, [128, 8, 80], BF16, ph) for i in range(6)]; t_vw = [Tok() for _ in range(6)]
            kc = sb("bkc", [128, 4, 256], BF16, ph); t_kc = Tok()
            vc = [sb("bvc%d" % i, [128, 8, 80], BF16, ph) for i in range(2)]; t_vc = [Tok(), Tok()]
            qt = [sb("bq%d" % i, [128, 4, 128], BF16, ph) for i in range(2)]; t_qt = [Tok(), Tok()]
            bias = sb("bbias", [128, 8, 5, 128], BF16, ph); t_bias = Tok()
            bst = [sb("bbst%d" % i, [128, 5, 128], F32, ph) for i in range(2)]; t_bst = [Tok(), Tok()]
            bmk = sb("bbmk", [128, 5, 128], F32, ph); t_bmk = Tok()
            PT = [sb("bPT%d" % i, [128, 8, 128], BF16, ph) for i in range(2)]; t_PT = [Tok(), Tok()]
            rden = sb("brden", [128, 2, 4], F32, ph); t_rden = Tok()
            onb = sb("bonb", [128, 512], BF16, ph); t_onb = Tok()
            oTs = [sb("boTs%d" % i, [128, 4, 128], BF16, ph) for i in range(2)]; t_oTs = [Tok(), Tok()]
            ps = [pm("bps%d" % i, [128, 8, 128], F32, ph) for i in range(2)]; t_ps = [Tok(), Tok()]
            po = pm("bpo", [128, 2, 512], F32, ph); t_po = Tok()
            pT = pm("bpT", [128, 4, 128], BF16, ph); t_pT = Tok()
            for i in range(6):
                P.op("pool", lambda e: e.memset(vw[i][:], 1.0), W=[t_vw[i]])
            for i in range(2):
                P.op("pool", lambda e: e.memset(vc[i][:], 1.0), W=[t_vc[i]])
            P.barrier()
            P.dma("sp", kc[:], kT_v[:, :, S:S + LC], W=[t_kc])
            for i in range(2):
                P.dma("sp", vc[i][:, :, 0:64], vA_d[S + i * 128:S + (i + 1) * 128, :].rearrange("t (h d) -> t h d", h=8), W=[t_vc[i]])
            loaded = {}
            cur_var = [-1]
            tiles = list(range(32)) + ([32, 33] if with_ctx else [])
            hcount = 0
            pend = []
            dgen = setup_gen(defer_setup, ph, 384) if defer_setup is not None else None
            for qi_, i in enumerate(tiles):
                qb = qi_ % 2
                P.dma("sp", qt[qb][:], qT_v[:, :, i * 128:(i + 1) * 128], W=[t_qt[qb]])
                chunks = []
                if i < 32:
                    E = min(max(2 * i - 4, 0), 54)
                    var = (2 * i - E) // 2
                    for c in range(5):
                        kt = E // 2 + c
                        slot = kt % 6
                        if loaded.get(slot) != kt:
                            P.dma("sp", kw[slot][:], kT_v[:, :, kt * 128:(kt + 1) * 128], W=[t_kw[slot]])
                            P.dma("sp", vw[slot][:, :, 0:64], vA_d[kt * 128:(kt + 1) * 128, :].rearrange("t (h d) -> t h d", h=8), W=[t_vw[slot]])
                            loaded[slot] = kt
                        chunks.append((kw[slot], None, vw[slot], [t_kw[slot]], [t_vw[slot]], c))
                    if var != cur_var[0]:
                        cur_var[0] = var
                        P.dma("sp", bmk[:], bmask_d[var], W=[t_bmk])
                        for h in range(8):
                            sbi = h % 2
                            P.dma("sp", bst[sbi][:], rpbT_d[l, var, :, h, :, :], W=[t_bst[sbi]])
                            tt("dve", bst[sbi][:], bst[sbi][:], bmk[:], ALU.add, R=[t_bst[sbi], t_bmk], W=[t_bst[sbi]])
                            act(bias[:, h, :, :], bst[sbi][:], AF.Exp, R=[t_bst[sbi]], W=[t_bias])
                for c in range(2):
                    chunks.append((kc, c, vc[c], [t_kc], [t_vc[c]], None))
                ncn = len(chunks)
                for h in range(8):
                    j = h // 2; hp = (h % 2) * 64
                    pb = hcount % 2; hcount += 1
                    for ci, (ktile, csub, vtile, tk, tv, loc) in enumerate(chunks):
                        kap = ktile[hp:hp + 64, j, :] if csub is None else ktile[hp:hp + 64, j, csub * 128:(csub + 1) * 128]
                        last = (ci == ncn - 1)
                        mm(ps[pb][:, ci, :], kap, qt[qb][hp:hp + 64, j, :], True, True, R=tk + [t_qt[qb]], W=[t_ps[pb]], inc=last)
                    n1 = min(ncn, 4)
                    act(PT[pb][:, 0:n1, :], ps[pb][:, 0:n1, :], AF.Exp, R=[t_ps[pb]], W=[t_PT[pb]])
                    if ncn > 4:
                        act(PT[pb][:, 4:ncn, :], ps[pb][:, 4:ncn, :], AF.Exp, R=[t_ps[pb]], W=[t_PT[pb]])
                    if i < 32:
                        tt("dve", PT[pb][:, 0:5, :], PT[pb][:, 0:5, :], bias[:, h, :, :], ALU.mult, R=[t_PT[pb], t_bias], W=[t_PT[pb]])
                    def pv(h=h, pb=pb, chunks=chunks, ncn=ncn, qb=qb, i=i):
                        for ci, (ktile, csub, vtile, tk, tv, loc) in enumerate(chunks):
                            mm(po[:, h // 4, (h % 4) * 66:(h % 4) * 66 + 66], PT[pb][:, ci, :], vtile[:, h, 0:66], ci == 0, ci == ncn - 1,
                               R=[t_PT[pb]] + tv, W=[t_po])
                        if h < 7:
                            return
                        po4 = po[:, :, 0:264].rearrange("p b (h e) -> p b h e", e=66)
                        recip(rden[:], po4[:, :, :, 64], R=[t_po], W=[t_rden])
                        tt("dve", onb[:].rearrange("p (b h e) -> p b h e", b=2, h=4), po4[:, :, :, 0:64],
                           rden[:].unsqueeze(3).to_broadcast([128, 2, 4, 64]), ALU.mult, R=[t_po, t_rden], W=[t_onb])
                        for j in range(4):
                            tr(pT[:, j, :], onb[:, j * 128:(j + 1) * 128], identb[:], R=[t_onb, t_c], W=[t_pT], inc=(j == 3))
                        cp("dve", oTs[qb][:], pT[:], R=[t_pT], W=[t_oTs[qb]])
                        P.dma("pool", oT_v[:, 0:4, i * 128:(i + 1) * 128], oTs[qb][:], R=[t_oTs[qb]], W=[Tok()])
                    if pend:
                        pend.pop(0)()
                    pend.append(pv)
                    if dgen is not None and qi_ >= 6:
                        next(dgen, None)
            while pend:
                pend.pop(0)()
            if dgen is not None:
                for _ in dgen:
                    pass
            P.barrier()

    def phaseB2(l, with_ctx):
        with ExitStack() as ph:
            cw = sb("ccw", [128, 2, 31], F32, ph); cv = sb("ccv", [128, 4, 2], F32, ph); t_cw = Tok()
            pw = sb("cpw", [128, 2, 256], BF16, ph); t_pw = Tok()
            hw = [[[sb("chw%d_%d_%d" % (i, ch, o), [128, NB + 32], BF16, ph) for o in range(2)] for ch in range(2)] for i in range(2)]
            t_hw = [[[Tok(), Tok()] for ch in range(2)] for i in range(2)]
            dg = sb("cdg", [128, 2, 31, 128], BF16, ph); t_dg = Tok()
            pcv = [pm("cpcv%d" % ch, [128, NB], F32, ph) for ch in range(2)]; t_pcv = [Tok(), Tok()]
            acc = [sb("cacc%d" % ch, [128, NB], F32, ph) for ch in range(2)]; t_acc = [Tok(), Tok()]
            sqc = [sb("csq%d" % ch, [128, NB], F32, ph) for ch in range(2)]; t_sqc = [Tok(), Tok()]
            mean = sb("cmean", [128, NB], F32, ph); t_mean = Tok()
            m2 = sb("cm2", [128, NB], F32, ph); t_m2 = Tok()
            rstd = sb("crstd", [128, NB], F32, ph); t_rstd = Tok()
            yn = [sb("cyn%d" % ch, [128, NB], F32, ph) for ch in range(2)]; t_yn = [Tok(), Tok()]
            yc = [sb("cyc%d" % ch, [128, NB], BF16, ph) for ch in range(2)]; t_yc = [Tok(), Tok()]
            ob = [sb("cob%d" % ch, [128, NB], BF16, ph) for ch in range(2)]; t_ob = [Tok(), Tok()]
            pmean = pm("cpm", [128, NB], F32, ph); t_pmean = Tok()
            pex2 = pm("cpe", [128, NB], F32, ph); t_pex2 = Tok()
            ppw = [pm("cpp%d" % i, [128, NB], F32, ph) for i in range(2)]; t_ppw = [Tok(), Tok()]
            P.dma("sp", cw[:], convw_d[l], W=[t_cw])
            P.barrier()
            P.dma("sp", cv[:], cvec_d[l], W=[t_cw])
            P.dma("pool", pw[:], pww_d[l].rearrange("(k p) n -> p k n", p=128), W=[t_pw])
            P.barrier()
            for ch in range(2):
                for j in range(31):
                    ts("dve" if j % 2 else "pool", dg[:, ch, j, :], identf[:], cw[:, ch, j:j + 1], None, ALU.mult, None, R=[t_c, t_cw], W=[t_dg])
            P.barrier()
            bl = blocks(with_ctx)

            def ld(bi):
                t0, n = bl[bi]
                c0 = t0 if t0 < S else (15 + S + 15 + (t0 - S))
                for ch in range(2):
                    for o in range(2):
                        P.dma("sp", hw[bi % 2][ch][o][:, :n + 30], hcT_d[ch * 128:(ch + 1) * 128, c0 + o:c0 + o + n + 30], W=[t_hw[bi % 2][ch][o]])
            ld(0)
            for bi, (t0, n) in enumerate(bl):
                b = bi % 2
                if bi + 1 < len(bl):
                    ld(bi + 1)
                for ch in range(2):
                    for j in range(31):
                        o = j % 2
                        mm(pcv[ch][:, :n], dg[:, ch, j, :], hw[b][ch][o][:, j - o:j - o + n], j == 0, j == 30,
                           R=[t_dg, t_hw[b][ch][o]], W=[t_pcv[ch]])
                    act(acc[ch][:, :n], pcv[ch][:, :n], AF.Identity, R=[t_pcv[ch], t_cw], W=[t_acc[ch]], scale=1.0, bias=cv[:, 0, ch:ch + 1])
                for ch in range(2):
                    act(sqc[ch][:, :n], acc[ch][:, :n], AF.Square, R=[t_acc[ch]], W=[t_sqc[ch]])
                for ch in range(2):
                    mm(pmean[:, :n], onesf[:], acc[ch][:, :n], ch == 0, ch == 1, R=[t_acc[ch], t_c], W=[t_pmean])
                for ch in range(2):
                    mm(pex2[:, :n], onesf[:], sqc[ch][:, :n], ch == 0, ch == 1, R=[t_sqc[ch], t_c], W=[t_pex2])
                cp("act", mean[:, :n], pmean[:, :n], R=[t_pmean], W=[t_mean])
                act(m2[:, :n], pmean[:, :n], AF.Square, R=[t_pmean], W=[t_m2])
                tt("dve", rstd[:, :n], pex2[:, :n], m2[:, :n], ALU.subtract, R=[t_pex2, t_m2], W=[t_rstd])
                act(rstd[:, :n], rstd[:, :n], AF.Ln, R=[t_rstd], W=[t_rstd], scale=1.0, bias=EPS)
                act(rstd[:, :n], rstd[:, :n], AF.Exp, R=[t_rstd], W=[t_rstd], scale=-0.5)
                for ch in range(2):
                    tt("dve", yn[ch][:, :n], acc[ch][:, :n], mean[:, :n], ALU.subtract, R=[t_acc[ch], t_mean], W=[t_yn[ch]])
                    tt("dve", yn[ch][:, :n], yn[ch][:, :n], rstd[:, :n], ALU.mult, R=[t_yn[ch], t_rstd], W=[t_yn[ch]])
                    act(yc[ch][:, :n], yn[ch][:, :n], AF.Silu, R=[t_yn[ch], t_cw], W=[t_yc[ch]],
                        scale=cv[:, 1, ch:ch + 1], bias=cv[:, 2, ch:ch + 1])
                for oc in range(2):
                    for ch in range(2):
                        mm(ppw[oc][:, :n], pw[:, ch, oc * 128:(oc + 1) * 128], yc[ch][:, :n], ch == 0, ch == 1,
                           R=[t_pw, t_yc[ch]], W=[t_ppw[oc]])
                    act(ob[oc][:, :n], ppw[oc][:, :n], AF.Identity, R=[t_ppw[oc], t_cw], W=[t_ob[oc]], scale=1.0, bias=cv[:, 3, oc:oc + 1])
                    P.dma("pool", oT_v[:, 4 + oc, t0:t0 + n], ob[oc][:, :n], R=[t_ob[oc]], W=[Tok()])
            P.barrier()

    def phaseB3(l, with_ctx):
        with ExitStack() as ph:
            tri = sb("gtri", [128, 2, 128], F32, ph); lm = sb("glm", [128, 2, 128], F32, ph)
            bdq = sb("gbdq", [128, 4, 128], BF16, ph); bds = sb("gbds", [128, 256], F32, ph)
            gwt = sb("ggw", [33, 2, 128], BF16, ph); t_k = Tok()
            P.dma("sp", tri[:], tri_d.rearrange("a s t -> s a t"), W=[t_k])
            P.barrier()
            P.dma("sp", lm[:], lmat_d.rearrange("a s t -> s a t"), W=[t_k])
            P.barrier()
            P.dma("pool", bdq[:], bdq_d, W=[t_k])
            P.barrier()
            P.dma("sp", bds[:], bds_d, W=[t_k])
            P.barrier()
            P.dma("pool", gwt[:], gw_d[l].rearrange("a k n -> k a n"), W=[t_k])
            P.barrier()
            Dd = []
            for dr in range(2):
                d = {}
                pf = "g%d" % dr

                def two(name, shape, dt):
                    return [sb(pf + name + str(i), shape, dt, ph) for i in range(2)], [Tok(), Tok()]
                d["zt"], d["t_zt"] = two("zt", [33, 128], BF16)
                d["qq"], d["t_qq"] = two("qq", [128, 128], BF16)
                d["kk"], d["t_kk"] = two("kk", [128, 128], BF16)
                d["vt"], d["t_vt"] = two("vt", [128, 256], BF16)
                d["ec"], d["t_ec"] = two("ec", [128, 128], F32)
                d["qe"], d["t_qe"] = two("qe", [128, 128], BF16)
                d["keT"], d["t_keT"] = two("keT", [128, 128], BF16)
                d["qbd"], d["t_qbd"] = two("qbd", [128, 4, 128], BF16)
                d["ke"], d["t_ke"] = two("ke", [128, 128], BF16)
                d["o1"], d["t_o1"] = two("o1", [128, 256], F32)
                for nm, shape, dt in (("e1", [128, 128], F32), ("sp", [128, 128], F32), ("enc", [128, 128], F32),
                                      ("attm", [128, 4, 128], BF16), ("Sf", [128, 256], F32), ("Sb", [128, 256], BF16),
                                      ("T1", [128, 256], F32)):
                    d[nm] = sb(pf + nm, shape, dt, ph); d["t_" + nm] = Tok()
                d["X"] = pm(pf + "X", [128, 2, 128], F32, ph); d["t_X"] = Tok()
                d["pk"] = pm(pf + "pk", [128, 128], BF16, ph); d["t_pk"] = Tok()
                d["pad"] = pm(pf + "pad", [128, 512], F32, ph); d["t_pad"] = Tok()
                d["pout"] = pm(pf + "pout", [128, 256], F32, ph); d["t_pout"] = Tok()
                d["order"] = [32, 33] + list(range(32)) if dr == 0 else [33, 32] + list(range(31, -1, -1))
                d["t_go"] = Tok()
                for i in range(2):
                    P.op("pool", lambda e: e.memset(d["zt"][i][:], 0.0), W=[d["t_zt"][i]])
                    P.op("pool", lambda e: e.memset(d["zt"][i][32:33, :], 1.0), W=[d["t_zt"][i]])
                P.op("pool", lambda e: e.memset(d["Sf"][:], 0.0), W=[d["t_Sf"]])
                P.op("pool", lambda e: e.memset(d["Sb"][:], 0.0), W=[d["t_Sb"]])
                Dd.append(d)
            P.barrier()
            go_d = [gof_d, gob_d]

            def LD(dr, oi):
                d = Dd[dr]; g = d["order"][oi]; b = oi % 2
                P.dma("sp", d["zt"][b][0:16, :], gz_d[dr, :, g * 128:(g + 1) * 128], W=[d["t_zt"][b]])
                P.dma("sp", d["qq"][b][:], gq_d[:, g * 128:(g + 1) * 128], W=[d["t_qq"][b]])
                P.dma("sp", d["kk"][b][:], gk_d[:, g * 128:(g + 1) * 128], W=[d["t_kk"][b]])
                P.dma("sp", d["vt"][b][:], gv_d[g * 128:(g + 1) * 128, :], W=[d["t_vt"][b]])

            def S1(dr, oi):
                d = Dd[dr]; b = oi % 2
                pa = d["X"][:, 0, :]; pc = d["X"][:, 1, :]
                mm(pa, d["zt"][b][0:33, :], gwt[0:33, dr, :], True, True, R=[d["t_zt"][b], t_k], W=[d["t_X"]])
                yield
                act(d["e1"][:], pa, AF.Exp, R=[d["t_X"]], W=[d["t_e1"]], scale=-1.0)
                yield
                act(d["sp"][:], d["e1"][:], AF.Ln, R=[d["t_e1"]], W=[d["t_sp"]], scale=1.0, bias=1.0)
                yield
                mm(pc, d["sp"][:], lm[:, dr, :], True, True, R=[d["t_sp"], t_k], W=[d["t_X"]])
                yield
                act(d["ec"][b][:], pc, AF.Exp, R=[d["t_X"]], W=[d["t_ec"][b]])
                yield
                act(d["enc"][:], pc, AF.Exp, R=[d["t_X"]], W=[d["t_enc"]], scale=-1.0)
                yield
                tt("dve", d["qe"][b][:], d["qq"][b][:], d["ec"][b][:], ALU.mult, R=[d["t_qq"][b], d["t_ec"][b]], W=[d["t_qe"][b]])
                yield
                tt("dve", d["keT"][b][:], d["kk"][b][:], d["enc"][:], ALU.mult, R=[d["t_kk"][b], d["t_enc"]], W=[d["t_keT"][b]])
                yield
                tt("dve", d["qbd"][b][:], d["qe"][b][:].unsqueeze(1).to_broadcast([128, 4, 128]), bdq[:], ALU.mult,
                   R=[d["t_qe"][b], t_k], W=[d["t_qbd"][b]])
                yield
                tr(d["pk"][:], d["keT"][b][:], identb[:], R=[d["t_keT"][b], t_c], W=[d["t_pk"]])
                yield
                cp("act", d["ke"][b][:], d["pk"][:], R=[d["t_pk"]], W=[d["t_ke"][b]])
                yield

            def S2(dr, oi):
                d = Dd[dr]; b = oi % 2
                g = d["order"][oi]
                need_out = (g < 32) or with_ctx
                patt = d["pad"][:].rearrange("p (h t) -> p h t", h=4)
                pds = d["pad"][:, 0:256]
                if need_out:
                    mm(d["pad"][:], d["keT"][b][:], d["qbd"][b][:].rearrange("p h t -> p (h t)"), True, True,
                       R=[d["t_keT"][b], d["t_qbd"][b]], W=[d["t_pad"]])
                    yield
                    tt("dve", d["attm"][:], patt, tri[:, dr, :].unsqueeze(1).to_broadcast([128, 4, 128]), ALU.mult,
                       R=[d["t_pad"], t_k], W=[d["t_attm"]])
                    yield
                    for h in range(4):
                        mm(d["pout"][:, h * 64:(h + 1) * 64], d["qe"][b][:], d["Sb"][:, h * 64:(h + 1) * 64], True, False,
                           R=[d["t_qe"][b], d["t_Sb"]], W=[d["t_pout"]], inc=False)
                        yield
                        mm(d["pout"][:, h * 64:(h + 1) * 64], d["attm"][:, h, :], d["vt"][b][:, h * 64:(h + 1) * 64], False, True,
                           R=[d["t_attm"], d["t_vt"][b]], W=[d["t_pout"]], inc=(h == 3))
                        yield
                mm(pds, d["ke"][b][:], d["vt"][b][:], True, True, R=[d["t_ke"][b], d["t_vt"][b]], W=[d["t_pad"]])
                yield
                tt("dve", d["T1"][:], pds, bds[:], ALU.mult, R=[d["t_pad"], t_k], W=[d["t_T1"]])
                yield
                tt("dve", d["T1"][:], d["T1"][:], d["Sf"][:], ALU.add, R=[d["t_T1"], d["t_Sf"]], W=[d["t_T1"]])
                yield
                col = 127 if dr == 0 else 0
                ts("dve", d["Sf"][:], d["T1"][:], d["ec"][b][:, col:col + 1], None, ALU.mult, None, R=[d["t_T1"], d["t_ec"][b]], W=[d["t_Sf"]])
                yield
                ts("dve", d["Sb"][:], d["T1"][:], d["ec"][b][:, col:col + 1], None, ALU.mult, None, R=[d["t_T1"], d["t_ec"][b]], W=[d["t_Sb"]])
                yield
                if need_out:
                    cp("act", d["o1"][b][:], d["pout"][:], R=[d["t_pout"]], W=[d["t_o1"][b]])
                    yield
                    P.dma("pool", go_d[dr][g * 128:(g + 1) * 128, :], d["o1"][b][:], R=[d["t_o1"][b]], W=[d["t_go"]])
                    yield
            def rr(gens):
                gens = list(gens)
                while gens:
                    for gq in list(gens):
                        try:
                            next(gq)
                        except StopIteration:
                            gens.remove(gq)
            for dr in range(2):
                LD(dr, 0)
            rr([S1(0, 0), S1(1, 0)])
            for oi in range(34):
                gl = [S2(0, oi), S2(1, oi)]
                if oi + 1 < 34:
                    for dr in range(2):
                        LD(dr, oi + 1)
                    gl = [S1(0, oi + 1), S2(0, oi), S1(1, oi + 1), S2(1, oi)]
                rr(gl)
            P.barrier()
        with ExitStack() as ph:
            gog = sb("ggog", [128, 256], F32, ph); t_k2 = Tok()
            P.dma("sp", gog[:], goutg_d[l], W=[t_k2])
            of = [sb("hof%d" % i, [128, 256], F32, ph) for i in range(2)]; t_of = [Tok(), Tok()]
            ob_ = [sb("hob%d" % i, [128, 256], F32, ph) for i in range(2)]; t_ob_ = [Tok(), Tok()]
            rt = [sb("hrt%d" % i, [128, 256], BF16, ph) for i in range(2)]; t_rt = [Tok(), Tok()]
            o1 = [sb("ho1%d" % i, [128, 256], F32, ph) for i in range(2)]; t_o1 = [Tok(), Tok()]
            o2 = [sb("ho2%d" % i, [128, 256], F32, ph) for i in range(2)]; t_o2 = [Tok(), Tok()]
            ss4 = [sb("hss%d" % i, [128, 4], F32, ph) for i in range(2)]; t_ss4 = [Tok(), Tok()]
            sr = [sb("hsr%d" % i, [128, 256], F32, ph) for i in range(2)]; t_sr = [Tok(), Tok()]
            oc = [sb("hoc%d" % i, [128, 256], BF16, ph) for i in range(2)]; t_oc = [Tok(), Tok()]
            oTs = [sb("hoTs%d" % i, [128, 2, 128], BF16, ph) for i in range(2)]; t_oTs = [Tok(), Tok()]
            pT = [pm("hpT%d" % i, [128, 2, 128], BF16, ph) for i in range(2)]; t_pT = [Tok(), Tok()]
            tiles = list(range(32)) + ([32, 33] if with_ctx else [])

            def ldo(ti):
                g = tiles[ti]; b = ti % 2
                P.dma("sp", of[b][:], gof_d[g * 128:(g + 1) * 128, :], W=[t_of[b]])
                P.dma("sp", ob_[b][:], gob_d[g * 128:(g + 1) * 128, :], W=[t_ob_[b]])
                P.dma("sp", rt[b][:], gr_d[g * 128:(g + 1) * 128, :], W=[t_rt[b]])
            def out_gen(ti):
                g = tiles[ti]; b = ti % 2
                tt("dve", o1[b][:], of[b][:], ob_[b][:], ALU.add, R=[t_of[b], t_ob_[b]], W=[t_o1[b]])
                yield
                tt("dve", o2[b][:], o1[b][:], o1[b][:], ALU.mult, R=[t_o1[b]], W=[t_o2[b]])
                yield
                P.op("dve", lambda e: e.tensor_reduce(out=ss4[b][:], in_=o2[b][:].rearrange("p (h d) -> p h d", h=4), axis=AX.X, op=ALU.add),
                     R=[t_o2[b]], W=[t_ss4[b]])
                yield
                act(ss4[b][:], ss4[b][:], AF.Ln, R=[t_ss4[b]], W=[t_ss4[b]], scale=1.0 / 64, bias=EPS)
                yield
                act(ss4[b][:], ss4[b][:], AF.Exp, R=[t_ss4[b]], W=[t_ss4[b]], scale=-0.5)
                yield
                act(sr[b][:], rt[b][:], AF.Exp, R=[t_rt[b]], W=[t_sr[b]], scale=-1.0)
                yield
                act(sr[b][:], sr[b][:], AF.Ln, R=[t_sr[b]], W=[t_sr[b]], scale=1.0, bias=1.0)
                yield
                act(sr[b][:], sr[b][:], AF.Exp, R=[t_sr[b]], W=[t_sr[b]], scale=-1.0)
                yield
                tt("dve", o2[b][:].rearrange("p (h d) -> p h d", h=4), o1[b][:].rearrange("p (h d) -> p h d", h=4),
                   ss4[b][:].unsqueeze(2).to_broadcast([128, 4, 64]), ALU.mult, R=[t_o1[b], t_ss4[b]], W=[t_o2[b]])
                yield
                tt("dve", o2[b][:], o2[b][:], gog[:], ALU.mult, R=[t_o2[b], t_k2], W=[t_o2[b]])
                yield
                tt("dve", o2[b][:], o2[b][:], sr[b][:], ALU.mult, R=[t_o2[b], t_sr[b]], W=[t_o2[b]])
                yield
                tt("dve", oc[b][:], o2[b][:], rt[b][:], ALU.mult, R=[t_o2[b], t_rt[b]], W=[t_oc[b]])
                yield
                for j in range(2):
                    tr(pT[b][:, j, :], oc[b][:, j * 128:(j + 1) * 128], identb[:], R=[t_oc[b], t_c], W=[t_pT[b]], inc=(j == 1))
                yield
                cp("act", oTs[b][:], pT[b][:], R=[t_pT[b]], W=[t_oTs[b]])
                yield
                P.dma("pool", oT_v[:, 6:8, g * 128:(g + 1) * 128], oTs[b][:], R=[t_oTs[b]], W=[Tok()])
                yield

            def rr2(gens):
                gens = list(gens)
                while gens:
                    for gq in list(gens):
                        try:
                            next(gq)
                        except StopIteration:
                            gens.remove(gq)
            ldo(0)
            if len(tiles) > 1:
                ldo(1)
            for ti in range(0, len(tiles), 2):
                gl = [out_gen(ti)]
                if ti + 1 < len(tiles):
                    gl.append(out_gen(ti + 1))
                rr2(gl)
                for tj in (ti + 2, ti + 3):
                    if tj < len(tiles):
                        ldo(tj)
            P.barrier()

    def phaseC(l, t_wo, t_w1, t_w2, with_ctx, last):
        W1 = WA[:].rearrange("p (k n) -> p k n", k=8)
        W2 = WB[:].rearrange("p (f n) -> p f n", f=32)
        with ExitStack() as ph:
            tl = norm_tiles(ph, "c")
            xb = [sb("cxb%d" % i, [128, 8, NB], F32, ph) for i in range(2)]; t_xb = [Tok(), Tok()]
            ob = [sb("cob%d" % i, [128, 8, NB], BF16, ph) for i in range(2)]; t_ob = [Tok(), Tok()]
            hT2 = [sb("chT%d" % i, [128, 8, NB], BF16, ph) for i in range(2)]; t_hT2 = [Tok(), Tok()]
            hid = sb("chid", [128, 32, NB], BF16, ph); t_hid = [Tok() for _ in range(32)]
            rl = [sb("crl%d" % i, [128, NB], F32, ph) for i in range(2)]; t_rl = [Tok(), Tok()]
            pacc = [pm("cpa%d" % i, [128, NB], F32, ph) for i in range(2)]; t_pacc = [Tok(), Tok()]
            pup = [pm("cpu%d" % i, [128, NB], F32, ph) for i in range(2)]; t_pup = [Tok(), Tok()]
            if last:
                pTo = pm("cpTo", [128, 8, 128], F32, ph); t_pTo = Tok()
                yo = [sb("cyo%d" % i, [128, 512], F32, ph) for i in range(2)]; t_yo = [Tok(), Tok()]
            bl = blocks(with_ctx)

            def ld(bi):
                t0, n = bl[bi]
                P.dma("sp", xb[bi % 2][:, :, :n], xs_v[:, :, t0:t0 + n], R=[t_xs], W=[t_xb[bi % 2]])
                P.dma("sp", ob[bi % 2][:, :, :n], oT_v[:, :, t0:t0 + n], W=[t_ob[bi % 2]])
            ld(0)
            cc_ = {"ca": 0}

            def front(bj):
                tj, nj = bl[bj]
                bb = bj % 2
                vv = 0 if tj < S else 1
                for nn in range(8):
                    pb = cc_["ca"] % 2; cc_["ca"] += 1
                    for k in range(8):
                        mm(pacc[pb][:, :nj], WO[:, k, nn * 128:(nn + 1) * 128], ob[bb][:, k, :nj], k == 0, k == 7,
                           R=[t_wo[k], t_ob[bb]], W=[t_pacc[pb]])
                    stt("dve", xb[bb][:, nn, :nj], pacc[pb][:, :nj], modT[:, 16 + nn, vv:vv + 1], xb[bb][:, nn, :nj], ALU.mult, ALU.add,
                        R=[t_pacc[pb], t_mod, t_xb[bb]], W=[t_xb[bb]])
                norm_mod(xb[bb], t_xb[bb], nj, hT2[bb], t_hT2[bb], gsc2, 24, vv, tl)
            if len(bl) > 1:
                ld(1)
            front(0)
            cu = 0; cy = 0
            for bi, (t0, n) in enumerate(bl):
                b = bi % 2
                v = 0 if t0 < S else 1
                hT = hT2[b]; t_hT = t_hT2[b]
                for f in range(32):
                    pb = cu % 2; cu += 1
                    for k in range(8):
                        mm(pup[pb][:, :n], W1[:, k, f * 128:(f + 1) * 128], hT[:, k, :n], k == 0, k == 7,
                           R=[t_w1[k], t_hT], W=[t_pup[pb]])
                    act(rl[pb][:, :n], pup[pb][:, :n], AF.Relu, R=[t_pup[pb]], W=[t_rl[pb]])
                    tt("dve", hid[:, f, :n], rl[pb][:, :n], rl[pb][:, :n], ALU.mult, R=[t_rl[pb]], W=[t_hid[f]])
                if bi + 1 < len(bl):
                    front(bi + 1)
                for nn in range(8):
                    pb = cc_["ca"] % 2; cc_["ca"] += 1
                    for f in range(32):
                        mm(pacc[pb][:, :n], W2[:, f, nn * 128:(nn + 1) * 128], hid[:, f, :n], f == 0, f == 31,
                           R=[t_w2[f // 4], t_hid[f]], W=[t_pacc[pb]])
                    stt("dve", xb[b][:, nn, :n], pacc[pb][:, :n], modT[:, 40 + nn, v:v + 1], xb[b][:, nn, :n], ALU.mult, ALU.add,
                        R=[t_pacc[pb], t_mod, t_xb[b]], W=[t_xb[b]])
                if not last:
                    P.dma("pool", xs_v[:, :, t0:t0 + n], xb[b][:, :, :n], R=[t_xb[b]], W=[t_xs])
                else:
                    for ti in range(n // 128):
                        for k in range(8):
                            tr(pTo[:, k, :], xb[b][:, k, ti * 128:(ti + 1) * 128], identf[:], R=[t_xb[b], t_c], W=[t_pTo], inc=(k == 7))
                        g0 = t0 + ti * 128
                        cp("act", yo[0][:], pTo[:, 0:4, :].rearrange("p k t -> p (k t)"), R=[t_pTo], W=[t_yo[0]])
                        P.dma("pool", y_out[g0:g0 + 128, 0:512], yo[0][:], R=[t_yo[0]], W=[Tok()])
                        cp("dve", yo[1][:], pTo[:, 4:8, :].rearrange("p k t -> p (k t)"), R=[t_pTo], W=[t_yo[1]])
                        P.dma("pool", y_out[g0:g0 + 128, 512:1024], yo[1][:], R=[t_yo[1]], W=[Tok()])
                if bi + 2 < len(bl):
                    ld(bi + 2)
            P.barrier()

    stages = []
    p0gen = phase0()
    next(p0gen)
    stages.append("p0")
    for l in range(DEPTH):
        last = (l == DEPTH - 1)
        with_ctx_out = not last
        if stop_after is not None and stages and stages[-1] == stop_after:
            break
        if l == 0:
            with ExitStack() as phs:
                npump = 0
                for _ in setup_gen(0, phs, 768):
                    if npump < 32:
                        next(p0gen, None)
                        npump += 1
                P.barrier()
            for _ in p0gen:
                pass
        modT = modTs[l]; gsc1 = gsc1s[l]; gsc2 = gsc2s[l]; t_mod = t_mods[l]
        stages.append("S%d" % l)
        if stop_after == stages[-1]:
            break
        t_win = load_w(WA[:, 0:8 * NCOL].rearrange("p (k n) -> p k n", k=8), win_d[l].rearrange("(k p) n -> p k n", p=128), 8, NCOL)
        t_wo = load_w(WO, wout_d[l].rearrange("(k p) n -> p k n", p=128), 8, 1024)
        t_w2 = []
        WBv = WB[:].rearrange("p (g m) -> p g m", g=8)
        for g in range(8):
            t = Tok()
            P.dma("pool", WBv[:, g, :].rearrange("p (f n) -> p f n", f=4), w2_d[l][g * 512:(g + 1) * 512, :].rearrange("(f p) n -> p f n", p=128), W=[t])
            t_w2.append(t)
        stages.append("W%d" % l)
        if stop_after == stages[-1]:
            break
        phaseA(l, t_win, True); stages.append("A%d" % l)
        if stop_after == stages[-1]:
            break
        t_w1 = load_w(WA[:].rearrange("p (k n) -> p k n", k=8), w1_d[l].rearrange("(k p) n -> p k n", p=128), 8, 4096)
        phaseB1(l, with_ctx_out, 1 if (l == 0 and DEPTH > 1) else None); stages.append("B1%d" % l)
        if stop_after == stages[-1]:
            break
        phaseB2(l, with_ctx_out); stages.append("B2%d" % l)
        if stop_after == stages[-1]:
            break
        phaseB3(l, with_ctx_out); stages.append("B3%d" % l)
        if stop_after == stages[-1]:
            break
        phaseC(l, t_wo, t_w1, t_w2, with_ctx_out, last); stages.append("C%d" % l)
        if stop_after == stages[-1]:
            break
    P.barrier()
    print("instructions emitted:", P.n_inst, {k: v for k, v in P.cnt.items() if not k.startswith("d_")})
    return nc


_NC_CACHE = {}


def kernel(**inputs):
    sh = _prep_shared(inputs)
    x = np.asarray(inputs["x"], dtype=np.float32)
    c = np.asarray(inputs["c"], dtype=np.float32)
    ctx = np.asarray(inputs["ctx"], dtype=np.float32)
    c_ctx = np.asarray(inputs["c_ctx"], dtype=np.float32)
    B = x.shape[0]
    in_maps = []
    for b in range(B):
        m = dict(sh)
        m["x"] = np.ascontiguousarray(x[b])
        m["ctx"] = np.ascontiguousarray(ctx[b])
        cc = np.stack([c[b].reshape(8, 128).T, c_ctx.reshape(8, 128).T], axis=2)
        m["cc"] = np.ascontiguousarray(cc.astype(np.float32))
        in_maps.append(m)
    if "nc" not in _NC_CACHE:
        _NC_CACHE["nc"] = build()
    res = run_bass_kernel_spmd(_NC_CACHE["nc"], in_maps, core_ids=list(range(B)))
    return np.stack([np.asarray(r["y"], dtype=np.float32) for r in res.results], axis=0)
```

```python
import os
from contextlib import ExitStack
import numpy as np
import concourse.bass as bass
import concourse.mybir as mybir
from concourse.bass_utils import run_bass_kernel_spmd

F32 = mybir.dt.float32
BF16 = mybir.dt.bfloat16
AF = mybir.ActivationFunctionType
ALU = mybir.AluOpType
AX = mybir.AxisListType

D = 1024
S = 4096
LC = 256
NTOK = S + LC
DEPTH = 2
NCOL = 3104
NB = 256
EPS = 1e-6
HCW = 4416
NEG = -30000.0


class Tok:
    __slots__ = ("w", "r")

    def __init__(self):
        self.w = None
        self.r = {}


class Prog:
    NDMA = 12

    def __init__(self, nc):
        self.nc = nc
        self.engs = {"pe": nc.tensor, "dve": nc.vector, "act": nc.scalar, "pool": nc.gpsimd, "sp": nc.sync}
        self.sems = {}
        self.cnt = {}
        self.pending = {}
        for k in self.engs:
            self.sems[k] = nc.alloc_semaphore("s_" + k)
            self.cnt[k] = 0
            self.pending[k] = False
        self.seen = {k: {} for k in self.engs}
        self.dq = {}
        for q in ("sp", "pool"):
            ring = []
            for i in range(self.NDMA):
                key = "d_%s_%d" % (q, i)
                self.sems[key] = nc.alloc_semaphore(key)
                self.cnt[key] = 0
                ring.append(key)
            self.dq[q] = [ring, 0]
        self.n_inst = 0

    def _wait(self, e, deps):
        eng = self.engs[e]
        seen = self.seen[e]
        best = {}
        for (k, v) in deps:
            if k == "pe" and e == "pe":
                continue
            if v > best.get(k, 0):
                best[k] = v
        for k, v in best.items():
            if seen.get(k, 0) >= v:
                continue
            eng.wait_ge(self.sems[k], v)
            seen[k] = v
            self.n_inst += 1

    @staticmethod
    def _deps(R, W):
        deps = []
        for t in R:
            if t.w is not None:
                deps.append(t.w)
        for t in W:
            if t.w is not None:
                deps.append(t.w)
            deps.extend(t.r.items())
        return deps

    def op(self, e, fn, R=(), W=(), inc=True):
        self._wait(e, self._deps(R, W))
        ins = fn(self.engs[e])
        self.n_inst += 1
        v = self.cnt[e] + 1
        if inc:
            ins.then_inc(self.sems[e], 1)
            self.cnt[e] = v
            self.pending[e] = False
        else:
            self.pending[e] = True
        for t in R:
            if t.r.get(e, 0) < v:
                t.r[e] = v
        for t in W:
            t.w = (e, v)
            t.r = {}
        return ins

    def dma(self, q, out, in_, R=(), W=()):
        ring, idx = self.dq[q]
        key = ring[idx % len(ring)]
        self.dq[q][1] = idx + 1
        deps = self._deps(R, W)
        if self.cnt[key] > 0:
            deps.append((key, self.cnt[key]))
        self._wait(q, deps)
        ins = self.engs[q].dma_start(out=out, in_=in_)
        self.n_inst += 1
        v = self.cnt[key] + 16
        ins.then_inc(self.sems[key], 16)
        self.cnt[key] = v
        for t in R:
            if t.r.get(key, 0) < v:
                t.r[key] = v
        for t in W:
            t.w = (key, v)
            t.r = {}
        return ins

    def barrier(self):
        deps = [(k, v) for k, v in self.cnt.items() if v > 0]
        for e in self.engs:
            assert not self.pending[e], e
            self._wait(e, deps)


def _rope_tables():
    pos = np.arange(S)
    inv = (10000.0 ** (-np.arange(8, dtype=np.float32) / 8)).astype(np.float32)
    row = (pos // 64).astype(np.float32)
    col = (pos % 64).astype(np.float32)
    cos = np.ones((32, NTOK), np.float32)
    sin = np.zeros((32, NTOK), np.float32)
    for d in range(32):
        p = row if d < 16 else col
        ang = (p * inv[d % 8]).astype(np.float32)
        sgn = -1.0 if (d % 16) < 8 else 1.0
        cos[d, :S] = np.cos(ang)
        sin[d, :S] = sgn * np.sin(ang)
    cos = np.tile(cos, (4, 1))
    sin = np.tile(sin, (4, 1))
    sc = np.float32(32 ** -0.5)
    return np.stack([cos * sc, sin * sc, cos, sin]).astype(np.float32)


def _bias_index():
    Ev = [0, 0, 0, 54, 54]
    ro = np.zeros((5, 128, 5, 128), np.int64)
    co = np.zeros((5, 128, 5, 128), np.int64)
    mask = np.zeros((5, 128, 5, 128), np.float32)
    ki = np.arange(128)[:, None, None]
    c = np.arange(5)[None, :, None]
    qi = np.arange(128)[None, None, :]
    for vi in range(5):
        E = Ev[vi]
        r = E + 2 * vi
        qr = r + qi // 64
        qc = qi % 64
        kr = E + 2 * c + ki // 64
        kc = ki % 64
        r0 = np.clip(qr - 4, 0, 56)
        cs = np.clip(qc - 8, 0, 48)
        valid = (kr >= r0) & (kr < r0 + 8) & (kc >= cs) & (kc < cs + 16)
        ro[vi] = np.clip(kr - qr + 7, 0, 14)
        co[vi] = np.clip(kc - qc + 15, 0, 30)
        mask[vi] = np.where(valid, 0.0, NEG)
    return ro, co, mask


def _prep_shared(inp):
    f = lambda a: np.ascontiguousarray(np.asarray(a, dtype=np.float32))
    w_in = f(inp["w_in"])
    perm = np.arange(128)
    for h in range(4):
        for d in range(32):
            perm[h * 32 + d] = h * 32 + (d + 8 if (d % 16) < 8 else d - 8)
    qg = w_in[:, :, 2048:2176]
    kg = w_in[:, :, 2176:2304]
    win = np.concatenate([
        w_in[:, :, 0:1536],
        w_in[:, :, 2304:2816],
        w_in[:, :, 1536:2048],
        qg, qg[:, :, perm], kg, kg[:, :, perm],
        w_in[:, :, 2816:2848]], axis=2)
    assert win.shape[2] == NCOL
    sh = {"win": f(win), "w_ada": f(inp["w_ada"]), "w_out": f(inp["w_out"]),
          "w1": f(inp["w_mlp_in"]), "w2": f(inp["w_mlp_out"])}
    sh["badaT"] = f(np.asarray(inp["b_ada"]).reshape(DEPTH, 48, 128).transpose(0, 2, 1))
    sh["n1g"] = f(np.asarray(inp["norm1_g"]).reshape(DEPTH, 8, 128).transpose(0, 2, 1))
    sh["n2g"] = f(np.asarray(inp["norm2_g"]).reshape(DEPTH, 8, 128).transpose(0, 2, 1))
    sh["naqg"] = f(np.tile(np.asarray(inp["na_q_g"])[:, None, None, :], (1, 128, 8, 1)).reshape(DEPTH, 128, 512))
    sh["nakg"] = f(np.tile(np.asarray(inp["na_k_g"])[:, None, None, :], (1, 128, 8, 1)).reshape(DEPTH, 128, 512))
    ro, co, mask = _bias_index()
    rpb = np.asarray(inp["na_rpb"], dtype=np.float32)
    g = rpb[:, :, ro, co]
    sh["rpbT"] = f(g.transpose(0, 2, 3, 1, 4, 5))
    sh["bmask"] = f(mask)
    sh["convw"] = f(np.asarray(inp["conv_w"]).reshape(DEPTH, 31, 2, 128).transpose(0, 3, 2, 1))
    v2 = lambda a: f(np.asarray(a).reshape(DEPTH, 2, 128).transpose(0, 2, 1))
    sh["cvec"] = f(np.stack([v2(inp["conv_b"]), v2(inp["conv_ln_g"]), v2(inp["conv_ln_b"]), v2(inp["conv_pw_b"])], axis=2))
    sh["pww"] = f(inp["conv_pw_w"])
    gw = np.zeros((DEPTH, 2, 33, 128), np.float32)
    gw[:, 0, 0:16] = np.asarray(inp["gla_gw_f"]); gw[:, 0, 32] = np.asarray(inp["gla_gb_f"])
    gw[:, 1, 0:16] = np.asarray(inp["gla_gw_b"]); gw[:, 1, 32] = np.asarray(inp["gla_gb_b"])
    sh["gw"] = gw
    sh["goutg"] = f(np.tile(np.asarray(inp["gla_out_g"])[:, None, None, :], (1, 128, 4, 1)).reshape(DEPTH, 128, 256))
    sh["ident"] = np.eye(128, dtype=np.float32)
    sh["rope"] = _rope_tables()
    s_i = np.arange(128)[:, None]
    t_i = np.arange(128)[None, :]
    tri = np.stack([(s_i <= t_i), (s_i >= t_i)]).astype(np.float32)
    sh["tri"] = tri
    sh["lmat"] = (-tri / 16.0).astype(np.float32)
    hd = np.arange(128) // 32
    sh["bdq"] = f(np.tile((hd[:, None] == np.arange(4)[None, :]).astype(np.float32)[:, :, None], (1, 1, 128)))
    sh["bds"] = f((hd[:, None] == (np.arange(256) // 64)[None, :]).astype(np.float32))
    return sh


def build(dbg=False, stop_after=None):
    nc = bass.Bass("TRN2", target_bir_lowering=False)
    P = Prog(nc)

    def din(name, shape, dt=F32):
        return nc.dram_tensor(name, list(shape), dt, kind="ExternalInput").ap()

    def dscr(name, shape, dt):
        if dbg:
            return nc.dram_tensor(name, list(shape), dt, kind="ExternalOutput").ap()
        return nc.dram_tensor(name, list(shape), dt).ap()

    x_in = din("x", [S, D]); ctx_in = din("ctx", [LC, D]); cc_in = din("cc", [128, 8, 2])
    win_d = din("win", [DEPTH, D, NCOL]); wada_d = din("w_ada", [DEPTH, D, 6 * D])
    wout_d = din("w_out", [DEPTH, D, D]); w1_d = din("w1", [DEPTH, D, 4 * D]); w2_d = din("w2", [DEPTH, 4 * D, D])
    badaT_d = din("badaT", [DEPTH, 128, 48]); n1g_d = din("n1g", [DEPTH, 128, 8]); n2g_d = din("n2g", [DEPTH, 128, 8])
    naqg_d = din("naqg", [DEPTH, 128, 512]); nakg_d = din("nakg", [DEPTH, 128, 512])
    rpbT_d = din("rpbT", [DEPTH, 5, 128, 8, 5, 128]); bmask_d = din("bmask", [5, 128, 5, 128])
    convw_d = din("convw", [DEPTH, 128, 2, 31]); cvec_d = din("cvec", [DEPTH, 128, 4, 2]); pww_d = din("pww", [DEPTH, 256, 256])
    gw_d = din("gw", [DEPTH, 2, 33, 128]); goutg_d = din("goutg", [DEPTH, 128, 256])
    ident_d = din("ident", [128, 128]); rope_d = din("rope", [4, 128, NTOK]); tri_d = din("tri", [2, 128, 128])
    lmat_d = din("lmat", [2, 128, 128]); bdq_d = din("bdq", [128, 4, 128]); bds_d = din("bds", [128, 256])
    y_out = nc.dram_tensor("y", [S, D], F32, kind="ExternalOutput").ap()

    xs_d = dscr("xs", [D, NTOK], F32)
    qT_d = dscr("qT", [512, NTOK], BF16); kT_d = dscr("kT", [512, NTOK], BF16); vA_d = dscr("vA", [NTOK, 512], BF16)
    hcT_d = dscr("hcT", [256, HCW], BF16)
    gq_d = dscr("gq", [128, NTOK], BF16); gk_d = dscr("gk", [128, NTOK], BF16)
    gz_d = dscr("gz", [2, 16, NTOK], BF16)
    gv_d = dscr("gv", [NTOK, 256], BF16); gr_d = dscr("gr", [NTOK, 256], BF16)
    gof_d = dscr("gof", [NTOK, 256], F32); gob_d = dscr("gob", [NTOK, 256], F32)
    oT_d = dscr("oT", [D, NTOK], BF16)
    xs_v = xs_d.rearrange("(k p) t -> p k t", p=128)
    oT_v = oT_d.rearrange("(k p) t -> p k t", p=128)

    uid = [0]

    def sb(name, shape, dt, stack=None):
        uid[0] += 1
        name = "%s_%d" % (name, uid[0])
        if stack is None:
            return nc.alloc_sbuf_tensor(name, list(shape), dt)
        return stack.enter_context(nc.sbuf_tensor(name, list(shape), dt))

    def pm(name, shape, dt, stack):
        uid[0] += 1
        name = "%s_%d" % (name, uid[0])
        return stack.enter_context(nc.psum_tensor(name, list(shape), dt))

    def mm(out, lhsT, rhs, start, stop, R, W, inc=None):
        if inc is None:
            inc = stop
        P.op("pe", lambda e: e.matmul(out, lhsT=lhsT, rhs=rhs, start=start, stop=stop), R=R, W=W, inc=inc)

    def tr(out, in_, idn, R, W, inc=True):
        P.op("pe", lambda e: e.transpose(out, in_, idn), R=R, W=W, inc=inc)

    def act(out, in_, func, R, W, bias=None, scale=None, accum=None):
        kw = {}
        if bias is not None:
            kw["bias"] = bias
        if scale is not None:
            kw["scale"] = scale
        if accum is not None:
            kw["accum_out"] = accum
        P.op("act", lambda e: e.activation(out=out, in_=in_, func=func, **kw), R=R, W=W)

    def tt(eng, out, in0, in1, op, R, W):
        P.op(eng, lambda e: e.tensor_tensor(out=out, in0=in0, in1=in1, op=op), R=R, W=W)

    def ts(eng, out, in0, s1, s2, op0, op1, R, W):
        if s2 is None:
            P.op(eng, lambda e: e.tensor_scalar(out=out, in0=in0, scalar1=s1, scalar2=None, op0=op0), R=R, W=W)
        else:
            P.op(eng, lambda e: e.tensor_scalar(out=out, in0=in0, scalar1=s1, scalar2=s2, op0=op0, op1=op1), R=R, W=W)

    def stt(eng, out, in0, scalar, in1, op0, op1, R, W):
        P.op(eng, lambda e: e.scalar_tensor_tensor(out=out, in0=in0, scalar=scalar, in1=in1, op0=op0, op1=op1), R=R, W=W)

    def cp(eng, out, in_, R, W):
        if eng == "act":
            P.op("act", lambda e: e.copy(out=out, in_=in_), R=R, W=W)
        else:
            P.op(eng, lambda e: e.tensor_copy(out=out, in_=in_), R=R, W=W)

    def recip(out, in_, R, W):
        P.op("dve", lambda e: e.reciprocal(out=out, in_=in_), R=R, W=W)

    WA = sb("WA", [128, 32768], BF16)
    WB = sb("WB", [128, 32768], BF16)
    WO = sb("WO", [128, 8, 1024], BF16)
    identb = sb("identb", [128, 128], BF16); identf = sb("identf", [128, 128], F32)
    onesb = sb("onesb", [128, 128], BF16); onesf = sb("onesf", [128, 128], F32)
    ccs = sb("ccs", [128, 8, 2], F32)
    ccsb = sb("ccsb", [128, 8, 2], BF16)
    modTs = [sb("modT%d" % i, [128, 48, 2], F32) for i in range(DEPTH)]
    gsc1s = [sb("gsc1%d" % i, [128, 8, 2], F32) for i in range(DEPTH)]
    gsc2s = [sb("gsc2%d" % i, [128, 8, 2], F32) for i in range(DEPTH)]
    t_mods = [Tok() for _ in range(DEPTH)]
    modT = modTs[0]; gsc1 = gsc1s[0]; gsc2 = gsc2s[0]
    badaTs = [sb("badaTs%d" % i, [128, 48], F32) for i in range(DEPTH)]
    n1gs = [sb("n1gs%d" % i, [128, 8], F32) for i in range(DEPTH)]
    n2gs = [sb("n2gs%d" % i, [128, 8], F32) for i in range(DEPTH)]
    t_c = Tok(); t_cc = Tok(); t_mod = t_mods[0]; t_vec = Tok()
    P.dma("sp", identf[:], ident_d, W=[t_c])
    P.dma("pool", identb[:], ident_d, W=[t_c])
    P.op("pool", lambda e: e.memset(onesb[:], 1.0), W=[t_c])
    P.op("pool", lambda e: e.memset(onesf[:], 1.0 / 256), W=[t_c])
    P.dma("sp", ccs[:], cc_in, W=[t_cc])
    act(ccs[:], ccs[:], AF.Silu, R=[t_cc], W=[t_cc])
    cp("dve", ccsb[:], ccs[:], R=[t_cc], W=[t_cc])
    for i in range(DEPTH):
        P.dma("sp", badaTs[i][:], badaT_d[i], W=[Tok()])
        P.dma("sp", n1gs[i][:], n1g_d[i], W=[Tok()])
        P.dma("sp", n2gs[i][:], n2g_d[i], W=[Tok()])
    P.barrier()

    def blocks(with_ctx):
        bl = [(i * NB, NB) for i in range(S // NB)]
        if with_ctx:
            bl.append((S, LC))
        return bl

    def phase0():
        with ExitStack() as ph:
            xin = [sb("p0x%d" % i, [128, D], F32, ph) for i in range(2)]; t_xin = [Tok(), Tok()]
            xo = [sb("p0o%d" % i, [128, 8, 128], F32, ph) for i in range(2)]; t_xo = [Tok(), Tok()]
            pT = [pm("p0p%d" % i, [128, 8, 128], F32, ph) for i in range(2)]; t_pT = [Tok(), Tok()]
            yield
            for i in range(34):
                b = i % 2
                src = x_in[i * 128:(i + 1) * 128, :] if i < 32 else ctx_in[(i - 32) * 128:(i - 31) * 128, :]
                P.dma("sp", xin[b][:], src, W=[t_xin[b]])
                for k in range(8):
                    tr(pT[b][:, k, :], xin[b][:, k * 128:(k + 1) * 128], identf[:], R=[t_xin[b], t_c], W=[t_pT[b]], inc=(k == 7))
                cp("act", xo[b][:, 0:4, :], pT[b][:, 0:4, :], R=[t_pT[b]], W=[t_xo[b]])
                cp("dve", xo[b][:, 4:8, :], pT[b][:, 4:8, :], R=[t_pT[b]], W=[t_xo[b]])
                P.dma("pool", xs_v[:, :, i * 128:(i + 1) * 128], xo[b][:], R=[t_xo[b]], W=[t_xs])
                yield
            P.barrier()

    t_xs = Tok()

    def setup_gen(l, ph, CB):
        modT = modTs[l]; gsc1 = gsc1s[l]; gsc2 = gsc2s[l]; t_mod = t_mods[l]
        nblk = (6 * D) // CB; cpb = CB // 128
        wst = [sb("wst%d" % i, [128, 8, CB], BF16, ph) for i in range(2)]; t_wst = [[Tok() for _ in range(8)] for _ in range(2)]
        pmod = pm("pmod", [128, 48, 2], F32, ph); t_pmod = Tok()
        tmp = sb("stmp", [128, 8, 2], F32, ph); t_tmp = Tok()
        wv = wada_d[l].rearrange("(k p) n -> p k n", p=128)

        def ldb(jb):
            for k in range(8):
                P.dma("pool", wst[jb % 2][:, k, :], wv[:, k, jb * CB:(jb + 1) * CB], W=[t_wst[jb % 2][k]])
        ldb(0)
        yield
        for jb in range(nblk):
            b = jb % 2
            if jb + 1 < nblk:
                ldb(jb + 1)
                yield
            for jj in range(cpb):
                j = jb * cpb + jj
                for k in range(8):
                    mm(pmod[:, j, :], wst[b][:, k, jj * 128:(jj + 1) * 128], ccsb[:, k, :], k == 0, k == 7,
                       R=[t_wst[b][k], t_cc], W=[t_pmod])
                yield
        tt("dve", modT[:], pmod[:], badaTs[l][:].unsqueeze(2).to_broadcast([128, 48, 2]), ALU.add, R=[t_pmod, t_vec], W=[t_mod])
        ts("dve", tmp[:], modT[:, 8:16, :], 1.0, None, ALU.add, None, R=[t_mod], W=[t_tmp])
        tt("dve", gsc1[:], tmp[:], n1gs[l][:].unsqueeze(2).to_broadcast([128, 8, 2]), ALU.mult, R=[t_tmp, t_vec], W=[t_mod])
        ts("dve", tmp[:], modT[:, 32:40, :], 1.0, None, ALU.add, None, R=[t_mod], W=[t_tmp])
        tt("dve", gsc2[:], tmp[:], n2gs[l][:].unsqueeze(2).to_broadcast([128, 8, 2]), ALU.mult, R=[t_tmp, t_vec], W=[t_mod])
        yield

    def load_w(dst, src_v, nk, width, q="pool"):
        toks = []
        for k in range(nk):
            t = Tok()
            P.dma(q, dst[:, k, :], src_v[:, k, :], W=[t])
            toks.append(t)
        return toks

    def norm_gen(xb, t_xb, n, hT, t_hT, gsc, shbase, v, tl):
        for k in range(8):
            b = k % 2
            act(tl["sq"][b][:, :n], xb[:, k, :n], AF.Square, R=[t_xb], W=[tl["t_sq"][b]])
            mm(tl["pss"][:, :n], onesb[:], tl["sq"][b][:, :n], k == 0, k == 7, R=[tl["t_sq"][b], t_c], W=[tl["t_pss"]], inc=True)
            yield
        act(tl["rs"][:, :n], tl["pss"][:, :n], AF.Ln, R=[tl["t_pss"]], W=[tl["t_rs"]], scale=1.0 / D, bias=EPS)
        act(tl["rs"][:, :n], tl["rs"][:, :n], AF.Exp, R=[tl["t_rs"]], W=[tl["t_rs"]], scale=-0.5)
        yield
        for k in range(8):
            b = k % 2
            tt("dve", tl["tm"][b][:, :n], xb[:, k, :n], tl["rs"][:, :n], ALU.mult, R=[t_xb, tl["t_rs"]], W=[tl["t_tm"][b]])
            act(hT[:, k, :n], tl["tm"][b][:, :n], AF.Identity, R=[tl["t_tm"][b], t_mod], W=[t_hT],
                scale=gsc[:, k, v:v + 1], bias=modT[:, shbase + k, v:v + 1])
            yield

    def norm_mod(*a):
        for _ in norm_gen(*a):
            pass

    def norm_tiles(ph, pfx):
        tl = {}
        tl["sq"] = [sb(pfx + "sq%d" % i, [128, NB], BF16, ph) for i in range(2)]; tl["t_sq"] = [Tok(), Tok()]
        tl["tm"] = [sb(pfx + "tm%d" % i, [128, NB], F32, ph) for i in range(2)]; tl["t_tm"] = [Tok(), Tok()]
        tl["rs"] = sb(pfx + "rs", [128, NB], F32, ph); tl["t_rs"] = Tok()
        tl["pss"] = pm(pfx + "pss", [128, NB], F32, ph); tl["t_pss"] = Tok()
        return tl

    def phaseA(l, t_win, with_ctx):
        WIN = WA[:, 0:8 * NCOL].rearrange("p (k n) -> p k n", k=8)
        with ExitStack() as ph:
            tl = norm_tiles(ph, "a")
            xb = [sb("axb%d" % i, [128, 8, NB], F32, ph) for i in range(2)]; t_xb = [Tok(), Tok()]
            hT = [sb("ahT%d" % i, [128, 8, NB], BF16, ph) for i in range(2)]; t_hT = [Tok(), Tok()]
            qg_s = sb("aqg", [128, 512], F32, ph); kg_s = sb("akg", [128, 512], F32, ph); t_g = Tok()
            rope = sb("arope", [128, 4, NB], F32, ph); t_rope = Tok()
            zero = sb("azero", [128, 32], BF16, ph); t_zero = Tok()
            ptm = [pm("aptm%d" % i, [128, 512], F32, ph) for i in range(2)]; t_ptm = [Tok(), Tok()]
            pfm = [pm("apfm%d" % i, [128, NB], F32, ph) for i in range(2)]; t_pfm = [Tok() for _ in range(2)]
            ptr = [pm("aptr%d" % i, [128, 4, 128], BF16, ph) for i in range(2)]; t_ptr = [Tok(), Tok()]
            st = [sb("ast%d" % i, [128, 512], F32, ph) for i in range(2)]; t_st = [Tok(), Tok()]
            sq2_ = [sb("asq2%d" % i, [128, 512], F32, ph) for i in range(2)]; t_sq2_ = [Tok(), Tok()]
            ssh_ = [sb("assh%d" % i, [128, 8], F32, ph) for i in range(2)]; t_ssh_ = [Tok(), Tok()]
            qn = [sb("aqn%d" % i, [128, 512], BF16, ph) for i in range(2)]; t_qn = [Tok(), Tok()]
            qTs = [sb("aqTs%d" % i, [128, 4, 128], BF16, ph) for i in range(2)]; t_qTs = [Tok(), Tok()]
            vb = [sb("avb%d" % i, [128, 512], BF16, ph) for i in range(2)]; t_vb = [Tok(), Tok()]
            sg_ = [sb("asg%d" % i, [128, NB], F32, ph) for i in range(2)]; t_sg_ = [Tok(), Tok()]
            fo = [sb("afo%d" % i, [128, NB], BF16, ph) for i in range(2)]; t_fo = [Tok(), Tok()]
            r1_ = [sb("ar1%d" % i, [128, NB], F32, ph) for i in range(2)]; t_r1_ = [Tok(), Tok()]
            r2_ = [sb("ar2%d" % i, [128, NB], F32, ph) for i in range(2)]; t_r2_ = [Tok(), Tok()]
            P.dma("sp", qg_s[:], naqg_d[l], W=[t_g])
            P.dma("sp", kg_s[:], nakg_d[l], W=[t_g])
            P.op("pool", lambda e: e.memset(zero[:], 0.0), W=[t_zero])
            P.barrier()
            t_scr = Tok()
            for c0, wd in ((0, 15), (15 + S, 15), (15 + S + 15, 15), (4412 - 15, HCW - 4412 + 15)):
                for ch in range(2):
                    P.dma("pool", hcT_d[ch * 128:(ch + 1) * 128, c0:c0 + wd], zero[:, 0:wd], R=[t_zero], W=[Tok()])
            bl = blocks(with_ctx)
            KA = int(os.environ.get("KA", "255"))
            if os.environ.get("KBL"):
                bl = bl[:int(os.environ["KBL"])]
            def ldx(bj):
                tj, nj = bl[bj]
                P.dma("sp", xb[bj % 2][:, :, :nj], xs_v[:, :, tj:tj + nj], R=[t_xs], W=[t_xb[bj % 2]])

            def emit_norm(bj):
                tj, nj = bl[bj]
                return norm_gen(xb[bj % 2], t_xb[bj % 2], nj, hT[bj % 2], t_hT[bj % 2], gsc1, 0, 0 if tj < S else 1, tl)
            ldx(0)
            if len(bl) > 1:
                ldx(1)
            for _ in emit_norm(0):
                pass
            ng = [None]
            cnt = {"fm": 0, "fo": 0}

            def tile_gen(bi, ti):
                t0, n = bl[bi]; b = bi % 2; g0 = t0 + ti * 128; s_ = ti
                sq2 = sq2_[s_]; t_sq2 = t_sq2_[s_]; ssh = ssh_[s_]; t_ssh = t_ssh_[s_]
                for grp in range(4):
                    for k in range(8):
                        mm(ptm[s_][:], hT[b][:, k, ti * 128:(ti + 1) * 128], WIN[:, k, grp * 512:(grp + 1) * 512], k == 0, k == 7,
                           R=[t_hT[b], t_win[k]], W=[t_ptm[s_]])
                    yield
                    if grp < 2:
                        act(sq2[:], ptm[s_][:], AF.Square, R=[t_ptm[s_]], W=[t_sq2])
                        yield
                        P.op("dve", lambda e: e.tensor_reduce(out=ssh[:], in_=sq2[:].rearrange("p (h d) -> p h d", h=8), axis=AX.X, op=ALU.add),
                             R=[t_sq2], W=[t_ssh])
                        yield
                        if grp == 0:
                            act(ssh[:], ssh[:], AF.Ln, R=[t_ssh], W=[t_ssh], scale=1.0, bias=64 * EPS)
                        else:
                            act(ssh[:], ssh[:], AF.Ln, R=[t_ssh], W=[t_ssh], scale=1.0 / 64, bias=EPS)
                        yield
                        act(ssh[:], ssh[:], AF.Exp, R=[t_ssh], W=[t_ssh], scale=-0.5)
                        yield
                        tt("dve", st[s_][:].rearrange("p (h d) -> p h d", h=8), ptm[s_][:].rearrange("p (h d) -> p h d", h=8),
                           ssh[:].unsqueeze(2).to_broadcast([128, 8, 64]), ALU.mult, R=[t_ptm[s_], t_ssh], W=[t_st[s_]])
                        yield
                        tt("dve", qn[s_][:], st[s_][:], (qg_s if grp == 0 else kg_s)[:], ALU.mult, R=[t_st[s_], t_g], W=[t_qn[s_]])
                        yield
                        for j in range(4):
                            tr(ptr[s_][:, j, :], qn[s_][:, j * 128:(j + 1) * 128], identb[:], R=[t_qn[s_], t_c], W=[t_ptr[s_]], inc=(j == 3))
                        yield
                        cp("act", qTs[s_][:], ptr[s_][:], R=[t_ptr[s_]], W=[t_qTs[s_]])
                        yield
                        dst = (qT_d if grp == 0 else kT_d).rearrange("(j p) t -> p j t", p=128)[:, :, g0:g0 + 128]
                        P.dma("sp", dst, qTs[s_][:], R=[t_qTs[s_]], W=[Tok()])
                        yield
                    else:
                        cp("act" if grp == 2 else "dve", vb[s_][:], ptm[s_][:], R=[t_ptm[s_]], W=[t_vb[s_]])
                        yield
                        if grp == 2:
                            P.dma("pool", vA_d[g0:g0 + 128, :], vb[s_][:], R=[t_vb[s_]], W=[Tok()])
                        else:
                            P.dma("pool", gv_d[g0:g0 + 128, :], vb[s_][:, 0:256], R=[t_vb[s_]], W=[Tok()])
                            P.dma("pool", gr_d[g0:g0 + 128, :], vb[s_][:, 256:512], R=[t_vb[s_]], W=[Tok()])
                        yield

            def fm_gen(bi):
                t0, n = bl[bi]; b = bi % 2

                def fm(col0, width):
                    pb = cnt["fm"] % 2; cnt["fm"] += 1
                    for k in range(8):
                        mm(pfm[pb][0:width, :n], WIN[:, k, col0:col0 + width], hT[b][:, k, :n], k == 0, k == 7,
                           R=[t_hT[b], t_win[k]], W=[t_pfm[pb]])
                    return pb
                hc0 = (15 + t0) if t0 < S else (15 + S + 15 + 15 + (t0 - S))
                for ch in range(2):
                    pa = fm(2048 + ch * 128, 128)
                    yield
                    pg = fm(2304 + ch * 128, 128)
                    yield
                    sg = sg_[ch]; t_sg = t_sg_[ch]
                    act(sg[:, :n], pfm[pg][:, :n], AF.Exp, R=[t_pfm[pg]], W=[t_sg], scale=-1.0)
                    yield
                    act(sg[:, :n], sg[:, :n], AF.Ln, R=[t_sg], W=[t_sg], scale=1.0, bias=1.0)
                    yield
                    act(sg[:, :n], sg[:, :n], AF.Exp, R=[t_sg], W=[t_sg], scale=-1.0)
                    yield
                    fi = cnt["fo"] % 2; cnt["fo"] += 1
                    tt("dve", fo[fi][:, :n], pfm[pa][:, :n], sg[:, :n], ALU.mult, R=[t_pfm[pa], t_sg], W=[t_fo[fi]])
                    yield
                    P.dma("pool", hcT_d[ch * 128:(ch + 1) * 128, hc0:hc0 + n], fo[fi][:, :n], R=[t_fo[fi]], W=[Tok()])
                    yield
                for qi, (dst, cbase) in enumerate(((gq_d, 2560), (gk_d, 2816))):
                    p1 = fm(cbase, 128)
                    yield
                    p2 = fm(cbase + 128, 128)
                    yield
                    r1 = r1_[qi]; t_r1 = t_r1_[qi]; r2 = r2_[qi]; t_r2 = t_r2_[qi]
                    tt("dve", r1[:, :n], pfm[p1][:, :n], rope[:, 2 * qi, :n], ALU.mult, R=[t_pfm[p1], t_rope], W=[t_r1])
                    yield
                    tt("dve", r2[:, :n], pfm[p2][:, :n], rope[:, 2 * qi + 1, :n], ALU.mult, R=[t_pfm[p2], t_rope], W=[t_r2])
                    yield
                    fi = cnt["fo"] % 2; cnt["fo"] += 1
                    tt("dve", fo[fi][:, :n], r1[:, :n], r2[:, :n], ALU.add, R=[t_r1, t_r2], W=[t_fo[fi]])
                    yield
                    P.dma("pool", dst[:, t0:t0 + n], fo[fi][:, :n], R=[t_fo[fi]], W=[Tok()])
                    yield
                for zi in range(2):
                    pz = fm(3072 + zi * 16, 16)
                    yield
                    fi = cnt["fo"] % 2; cnt["fo"] += 1
                    cp("act", fo[fi][0:16, :n], pfm[pz][0:16, :n], R=[t_pfm[pz]], W=[t_fo[fi]])
                    yield
                    P.dma("pool", gz_d[zi, :, t0:t0 + n], fo[fi][0:16, :n], R=[t_fo[fi]], W=[Tok()])
                    yield

            def rr(gens):
                gens = list(gens)
                while gens:
                    for gq in list(gens):
                        try:
                            next(gq)
                        except StopIteration:
                            gens.remove(gq)
            for bi, (t0, n) in enumerate(bl):
                P.dma("sp", rope[:, :, :n], rope_d[:, :, t0:t0 + n].rearrange("a p t -> p a t"), W=[t_rope])
                gl = [tile_gen(bi, ti) for ti in range(n // 128)]
                gl.append(fm_gen(bi))
                if bi + 1 < len(bl):
                    gl.append(emit_norm(bi + 1))
                rr(gl)
                if bi + 2 < len(bl):
                    ldx(bi + 2)
            P.barrier()

    def phaseB1(l, with_ctx, defer_setup=None):
        qT_v = qT_d.rearrange("(j p) t -> p j t", p=128)
        kT_v = kT_d.rearrange("(j p) t -> p j t", p=128)
        with ExitStack() as ph:
            kw = [sb("bk%d" % i, [128, 4, 128], BF16, ph) for i in range(6)]; t_kw = [Tok() for _ in range(6)]
            vw = [sb("bv%d" % i, [128, 8, 80], BF16, ph) for i in range(6)]; t_vw = [Tok() for _ in range(6)]
            kc = sb("bkc", [128, 4, 256], BF16, ph); t_kc = Tok()
            vc = [sb("bvc%d" % i, [128, 8, 80], BF16, ph) for i in range(2)]; t_vc = [Tok(), Tok()]
            qt = [sb("bq%d" % i, [128, 4, 128], BF16, ph) for i in range(2)]; t_qt = [Tok(), Tok()]
            bias = sb("bbias", [128, 8, 5, 128], BF16, ph); t_bias = Tok()
            bst = [sb("bbst%d" % i, [128, 5, 128], F32, ph) for i in range(2)]; t_bst = [Tok(), Tok()]
            bmk = sb("bbmk", [128, 5, 128], F32, ph); t_bmk = Tok()
            PT = [sb("bPT%d" % i, [128, 8, 128], BF16, ph) for i in range(2)]; t_PT = [Tok(), Tok()]
            rden = sb("brden", [128, 2, 4], F32, ph); t_rden = Tok()
            onb = sb("bonb", [128, 512], BF16, ph); t_onb = Tok()
            oTs = [sb("boTs%d" % i, [128, 4, 128], BF16, ph) for i in range(2)]; t_oTs = [Tok(), Tok()]
            ps = [pm("bps%d" % i, [128, 8, 128], F32, ph) for i in range(2)]; t_ps = [Tok(), Tok()]
            po = pm("bpo", [128, 2, 512], F32, ph); t_po = Tok()
            pT = pm("bpT", [128, 4, 128], BF16, ph); t_pT = Tok()
            for i in range(6):
                P.op("pool", lambda e: e.memset(vw[i][:], 1.0), W=[t_vw[i]])
            for i in range(2):
                P.op("pool", lambda e: e.memset(vc[i][:], 1.0), W=[t_vc[i]])
            P.barrier()
            P.dma("sp", kc[:], kT_v[:, :, S:S + LC], W=[t_kc])
            for i in range(2):
                P.dma("sp", vc[i][:, :, 0:64], vA_d[S + i * 128:S + (i + 1) * 128, :].rearrange("t (h d) -> t h d", h=8), W=[t_vc[i]])
            loaded = {}
            cur_var = [-1]
            tiles = list(range(32)) + ([32, 33] if with_ctx else [])
            hcount = 0
            pend = []
            dgen = setup_gen(defer_setup, ph, 384) if defer_setup is not None else None
            for qi_, i in enumerate(tiles):
                qb = qi_ % 2
                P.dma("sp", qt[qb][:], qT_v[:, :, i * 128:(i + 1) * 128], W=[t_qt[qb]])
                chunks = []
                if i < 32:
                    E = min(max(2 * i - 4, 0), 54)
                    var = (2 * i - E) // 2
                    for c in range(5):
                        kt = E // 2 + c
                        slot = kt % 6
                        if loaded.get(slot) != kt:
                            P.dma("sp", kw[slot][:], kT_v[:, :, kt * 128:(kt + 1) * 128], W=[t_kw[slot]])
                            P.dma("sp", vw[slot][:, :, 0:64], vA_d[kt * 128:(kt + 1) * 128, :].rearrange("t (h d) -> t h d", h=8), W=[t_vw[slot]])
                            loaded[slot] = kt
                        chunks.append((kw[slot], None, vw[slot], [t_kw[slot]], [t_vw[slot]], c))
                    if var != cur_var[0]:
                        cur_var[0] = var
                        P.dma("sp", bmk[:], bmask_d[var], W=[t_bmk])
                        for h in range(8):
                            sbi = h % 2
                            P.dma("sp", bst[sbi][:], rpbT_d[l, var, :, h, :, :], W=[t_bst[sbi]])
                            tt("dve", bst[sbi][:], bst[sbi][:], bmk[:], ALU.add, R=[t_bst[sbi], t_bmk], W=[t_bst[sbi]])
                            act(bias[:, h, :, :], bst[sbi][:], AF.Exp, R=[t_bst[sbi]], W=[t_bias])
                for c in range(2):
                    chunks.append((kc, c, vc[c], [t_kc], [t_vc[c]], None))
                ncn = len(chunks)
                for h in range(8):
                    j = h // 2; hp = (h % 2) * 64
                    pb = hcount % 2; hcount += 1
                    for ci, (ktile, csub, vtile, tk, tv, loc) in enumerate(chunks):
                        kap = ktile[hp:hp + 64, j, :] if csub is None else ktile[hp:hp + 64, j, csub * 128:(csub + 1) * 128]
                        last = (ci == ncn - 1)
                        mm(ps[pb][:, ci, :], kap, qt[qb][hp:hp + 64, j, :], True, True, R=tk + [t_qt[qb]], W=[t_ps[pb]], inc=last)
                    n1 = min(ncn, 4)
                    act(PT[pb][:, 0:n1, :], ps[pb][:, 0:n1, :], AF.Exp, R=[t_ps[pb]], W=[t_PT[pb]])
                    if ncn > 4:
                        act(PT[pb][:, 4:ncn, :], ps[pb][:, 4:ncn, :], AF.Exp, R=[t_ps[pb]], W=[t_PT[pb]])
                    if i < 32:
                        tt("dve", PT[pb][:, 0:5, :], PT[pb][:, 0:5, :], bias[:, h, :, :], ALU.mult, R=[t_PT[pb], t_bias], W=[t_PT[pb]])
                    def pv(h=h, pb=pb, chunks=chunks, ncn=ncn, qb=qb, i=i):
                        for ci, (ktile, csub, vtile, tk, tv, loc) in enumerate(chunks):
                            mm(po[:, h // 4, (h % 4) * 66:(h % 4) * 66 + 66], PT[pb][:, ci, :], vtile[:, h, 0:66], ci == 0, ci == ncn - 1,
                               R=[t_PT[pb]] + tv, W=[t_po])
                        if h < 7:
                            return
                        po4 = po[:, :, 0:264].rearrange("p b (h e) -> p b h e", e=66)
                        recip(rden[:], po4[:, :, :, 64], R=[t_po], W=[t_rden])
                        tt("dve", onb[:].rearrange("p (b h e) -> p b h e", b=2, h=4), po4[:, :, :, 0:64],
                           rden[:].unsqueeze(3).to_broadcast([128, 2, 4, 64]), ALU.mult, R=[t_po, t_rden], W=[t_onb])
                        for j in range(4):
                            tr(pT[:, j, :], onb[:, j * 128:(j + 1) * 128], identb[:], R=[t_onb, t_c], W=[t_pT], inc=(j == 3))
                        cp("dve", oTs[qb][:], pT[:], R=[t_pT], W=[t_oTs[qb]])
                        P.dma("pool", oT_v[:, 0:4, i * 128:(i + 1) * 128], oTs[qb][:], R=[t_oTs[qb]], W=[Tok()])
                    if pend:
                        pend.pop(0)()
                    pend.append(pv)
                    if dgen is not None and qi_ >= 6:
                        next(dgen, None)
            while pend:
                pend.pop(0)()
            if dgen is not None:
                for _ in dgen:
                    pass
            P.barrier()

    def phaseB2(l, with_ctx):
        with ExitStack() as ph:
            cw = sb("ccw", [128, 2, 31], F32, ph); cv = sb("ccv", [128, 4, 2], F32, ph); t_cw = Tok()
            pw = sb("cpw", [128, 2, 256], BF16, ph); t_pw = Tok()
            hw = [[[sb("chw%d_%d_%d" % (i, ch, o), [128, NB + 32], BF16, ph) for o in range(2)] for ch in range(2)] for i in range(2)]
            t_hw = [[[Tok(), Tok()] for ch in range(2)] for i in range(2)]
            dg = sb("cdg", [128, 2, 31, 128], BF16, ph); t_dg = Tok()
            pcv = [pm("cpcv%d" % ch, [128, NB], F32, ph) for ch in range(2)]; t_pcv = [Tok(), Tok()]
            acc = [sb("cacc%d" % ch, [128, NB], F32, ph) for ch in range(2)]; t_acc = [Tok(), Tok()]
            sqc = [sb("csq%d" % ch, [128, NB], F32, ph) for ch in range(2)]; t_sqc = [Tok(), Tok()]
            mean = sb("cmean", [128, NB], F32, ph); t_mean = Tok()
            m2 = sb("cm2", [128, NB], F32, ph); t_m2 = Tok()
            rstd = sb("crstd", [128, NB], F32, ph); t_rstd = Tok()
            yn = [sb("cyn%d" % ch, [128, NB], F32, ph) for ch in range(2)]; t_yn = [Tok(), Tok()]
            yc = [sb("cyc%d" % ch, [128, NB], BF16, ph) for ch in range(2)]; t_yc = [Tok(), Tok()]
            ob = [sb("cob%d" % ch, [128, NB], BF16, ph) for ch in range(2)]; t_ob = [Tok(), Tok()]
            pmean = pm("cpm", [128, NB], F32, ph); t_pmean = Tok()
            pex2 = pm("cpe", [128, NB], F32, ph); t_pex2 = Tok()
            ppw = [pm("cpp%d" % i, [128, NB], F32, ph) for i in range(2)]; t_ppw = [Tok(), Tok()]
            P.dma("sp", cw[:], convw_d[l], W=[t_cw])
            P.barrier()
            P.dma("sp", cv[:], cvec_d[l], W=[t_cw])
            P.dma("pool", pw[:], pww_d[l].rearrange("(k p) n -> p k n", p=128), W=[t_pw])
            P.barrier()
            for ch in range(2):
                for j in range(31):
                    ts("dve" if j % 2 else "pool", dg[:, ch, j, :], identf[:], cw[:, ch, j:j + 1], None, ALU.mult, None, R=[t_c, t_cw], W=[t_dg])
            P.barrier()
            bl = blocks(with_ctx)

            def ld(bi):
                t0, n = bl[bi]
                c0 = t0 if t0 < S else (15 + S + 15 + (t0 - S))
                for ch in range(2):
                    for o in range(2):
                        P.dma("sp", hw[bi % 2][ch][o][:, :n + 30], hcT_d[ch * 128:(ch + 1) * 128, c0 + o:c0 + o + n + 30], W=[t_hw[bi % 2][ch][o]])
            ld(0)
            for bi, (t0, n) in enumerate(bl):
                b = bi % 2
                if bi + 1 < len(bl):
                    ld(bi + 1)
                for ch in range(2):
                    for j in range(31):
                        o = j % 2
                        mm(pcv[ch][:, :n], dg[:, ch, j, :], hw[b][ch][o][:, j - o:j - o + n], j == 0, j == 30,
                           R=[t_dg, t_hw[b][ch][o]], W=[t_pcv[ch]])
                    act(acc[ch][:, :n], pcv[ch][:, :n], AF.Identity, R=[t_pcv[ch], t_cw], W=[t_acc[ch]], scale=1.0, bias=cv[:, 0, ch:ch + 1])
                for ch in range(2):
                    act(sqc[ch][:, :n], acc[ch][:, :n], AF.Square, R=[t_acc[ch]], W=[t_sqc[ch]])
                for ch in range(2):
                    mm(pmean[:, :n], onesf[:], acc[ch][:, :n], ch == 0, ch == 1, R=[t_acc[ch], t_c], W=[t_pmean])
                for ch in range(2):
                    mm(pex2[:, :n], onesf[:], sqc[ch][:, :n], ch == 0, ch == 1, R=[t_sqc[ch], t_c], W=[t_pex2])
                cp("act", mean[:, :n], pmean[:, :n], R=[t_pmean], W=[t_mean])
                act(m2[:, :n], pmean[:, :n], AF.Square, R=[t_pmean], W=[t_m2])
                tt("dve", rstd[:, :n], pex2[:, :n], m2[:, :n], ALU.subtract, R=[t_pex2, t_m2], W=[t_rstd])
                act(rstd[:, :n], rstd[:, :n], AF.Ln, R=[t_rstd], W=[t_rstd], scale=1.0, bias=EPS)
                act(rstd[:, :n], rstd[:, :n], AF.Exp, R=[t_rstd], W=[t_rstd], scale=-0.5)
                for ch in range(2):
                    tt("dve", yn[ch][:, :n], acc[ch][:, :n], mean[:, :n], ALU.subtract, R=[t_acc[ch], t_mean], W=[t_yn[ch]])
                    tt("dve", yn[ch][:, :n], yn[ch][:, :n], rstd[:, :n], ALU.mult, R=[t_yn[ch], t_rstd], W=[t_yn[ch]])
                    act(yc[ch][:, :n], yn[ch][:, :n], AF.Silu, R=[t_yn[ch], t_cw], W=[t_yc[ch]],
                        scale=cv[:, 1, ch:ch + 1], bias=cv[:, 2, ch:ch + 1])
                for oc in range(2):
                    for ch in range(2):
                        mm(ppw[oc][:, :n], pw[:, ch, oc * 128:(oc + 1) * 128], yc[ch][:, :n], ch == 0, ch == 1,
                           R=[t_pw, t_yc[ch]], W=[t_ppw[oc]])
                    act(ob[oc][:, :n], ppw[oc][:, :n], AF.Identity, R=[t_ppw[oc], t_cw], W=[t_ob[oc]], scale=1.0, bias=cv[:, 3, oc:oc + 1])
                    P.dma("pool", oT_v[:, 4 + oc, t0:t0 + n], ob[oc][:, :n], R=[t_ob[oc]], W=[Tok()])
            P.barrier()

    def phaseB3(l, with_ctx):
        with ExitStack() as ph:
            tri = sb("gtri", [128, 2, 128], F32, ph); lm = sb("glm", [128, 2, 128], F32, ph)
            bdq = sb("gbdq", [128, 4, 128], BF16, ph); bds = sb("gbds", [128, 256], F32, ph)
            gwt = sb("ggw", [33, 2, 128], BF16, ph); t_k = Tok()
            P.dma("sp", tri[:], tri_d.rearrange("a s t -> s a t"), W=[t_k])
            P.barrier()
            P.dma("sp", lm[:], lmat_d.rearrange("a s t -> s a t"), W=[t_k])
            P.barrier()
            P.dma("pool", bdq[:], bdq_d, W=[t_k])
            P.barrier()
            P.dma("sp", bds[:], bds_d, W=[t_k])
            P.barrier()
            P.dma("pool", gwt[:], gw_d[l].rearrange("a k n -> k a n"), W=[t_k])
            P.barrier()
            Dd = []
            for dr in range(2):
                d = {}
                pf = "g%d" % dr

                def two(name, shape, dt):
                    return [sb(pf + name + str(i), shape, dt, ph) for i in range(2)], [Tok(), Tok()]
                d["zt"], d["t_zt"] = two("zt", [33, 128], BF16)
                d["qq"], d["t_qq"] = two("qq", [128, 128], BF16)
                d["kk"], d["t_kk"] = two("kk", [128, 128], BF16)
                d["vt"], d["t_vt"] = two("vt", [128, 256], BF16)
                d["ec"], d["t_ec"] = two("ec", [128, 128], F32)
                d["qe"], d["t_qe"] = two("qe", [128, 128], BF16)
                d["keT"], d["t_keT"] = two("keT", [128, 128], BF16)
                d["qbd"], d["t_qbd"] = two("qbd", [128, 4, 128], BF16)
                d["ke"], d["t_ke"] = two("ke", [128, 128], BF16)
                d["o1"], d["t_o1"] = two("o1", [128, 256], F32)
                for nm, shape, dt in (("e1", [128, 128], F32), ("sp", [128, 128], F32), ("enc", [128, 128], F32),
                                      ("attm", [128, 4, 128], BF16), ("Sf", [128, 256], F32), ("Sb", [128, 256], BF16),
                                      ("T1", [128, 256], F32)):
                    d[nm] = sb(pf + nm, shape, dt, ph); d["t_" + nm] = Tok()
                d["X"] = pm(pf + "X", [128, 2, 128], F32, ph); d["t_X"] = Tok()
                d["pk"] = pm(pf + "pk", [128, 128], BF16, ph); d["t_pk"] = Tok()
                d["pad"] = pm(pf + "pad", [128, 512], F32, ph); d["t_pad"] = Tok()
                d["pout"] = pm(pf + "pout", [128, 256], F32, ph); d["t_pout"] = Tok()
                d["order"] = [32, 33] + list(range(32)) if dr == 0 else [33, 32] + list(range(31, -1, -1))
                d["t_go"] = Tok()
                for i in range(2):
                    P.op("pool", lambda e: e.memset(d["zt"][i][:], 0.0), W=[d["t_zt"][i]])
                    P.op("pool", lambda e: e.memset(d["zt"][i][32:33, :], 1.0), W=[d["t_zt"][i]])
                P.op("pool", lambda e: e.memset(d["Sf"][:], 0.0), W=[d["t_Sf"]])
                P.op("pool", lambda e: e.memset(d["Sb"][:], 0.0), W=[d["t_Sb"]])
                Dd.append(d)
            P.barrier()
            go_d = [gof_d, gob_d]

            def LD(dr, oi):
                d = Dd[dr]; g = d["order"][oi]; b = oi % 2
                P.dma("sp", d["zt"][b][0:16, :], gz_d[dr, :, g * 128:(g + 1) * 128], W=[d["t_zt"][b]])
                P.dma("sp", d["qq"][b][:], gq_d[:, g * 128:(g + 1) * 128], W=[d["t_qq"][b]])
                P.dma("sp", d["kk"][b][:], gk_d[:, g * 128:(g + 1) * 128], W=[d["t_kk"][b]])
                P.dma("sp", d["vt"][b][:], gv_d[g * 128:(g + 1) * 128, :], W=[d["t_vt"][b]])

            def S1(dr, oi):
                d = Dd[dr]; b = oi % 2
                pa = d["X"][:, 0, :]; pc = d["X"][:, 1, :]
                mm(pa, d["zt"][b][0:33, :], gwt[0:33, dr, :], True, True, R=[d["t_zt"][b], t_k], W=[d["t_X"]])
                yield
                act(d["e1"][:], pa, AF.Exp, R=[d["t_X"]], W=[d["t_e1"]], scale=-1.0)
                yield
                act(d["sp"][:], d["e1"][:], AF.Ln, R=[d["t_e1"]], W=[d["t_sp"]], scale=1.0, bias=1.0)
                yield
                mm(pc, d["sp"][:], lm[:, dr, :], True, True, R=[d["t_sp"], t_k], W=[d["t_X"]])
                yield
                act(d["ec"][b][:], pc, AF.Exp, R=[d["t_X"]], W=[d["t_ec"][b]])
                yield
                act(d["enc"][:], pc, AF.Exp, R=[d["t_X"]], W=[d["t_enc"]], scale=-1.0)
                yield
                tt("dve", d["qe"][b][:], d["qq"][b][:], d["ec"][b][:], ALU.mult, R=[d["t_qq"][b], d["t_ec"][b]], W=[d["t_qe"][b]])
                yield
                tt("dve", d["keT"][b][:], d["kk"][b][:], d["enc"][:], ALU.mult, R=[d["t_kk"][b], d["t_enc"]], W=[d["t_keT"][b]])
                yield
                tt("dve", d["qbd"][b][:], d["qe"][b][:].unsqueeze(1).to_broadcast([128, 4, 128]), bdq[:], ALU.mult,
                   R=[d["t_qe"][b], t_k], W=[d["t_qbd"][b]])
                yield
                tr(d["pk"][:], d["keT"][b][:], identb[:], R=[d["t_keT"][b], t_c], W=[d["t_pk"]])
                yield
                cp("act", d["ke"][b][:], d["pk"][:], R=[d["t_pk"]], W=[d["t_ke"][b]])
                yield

            def S2(dr, oi):
                d = Dd[dr]; b = oi % 2
                g = d["order"][oi]
                need_out = (g < 32) or with_ctx
                patt = d["pad"][:].rearrange("p (h t) -> p h t", h=4)
                pds = d["pad"][:, 0:256]
                if need_out:
                    mm(d["pad"][:], d["keT"][b][:], d["qbd"][b][:].rearrange("p h t -> p (h t)"), True, True,
                       R=[d["t_keT"][b], d["t_qbd"][b]], W=[d["t_pad"]])
                    yield
                    tt("dve", d["attm"][:], patt, tri[:, dr, :].unsqueeze(1).to_broadcast([128, 4, 128]), ALU.mult,
                       R=[d["t_pad"], t_k], W=[d["t_attm"]])
                    yield
                    for h in range(4):
                        mm(d["pout"][:, h * 64:(h + 1) * 64], d["qe"][b][:], d["Sb"][:, h * 64:(h + 1) * 64], True, False,
                           R=[d["t_qe"][b], d["t_Sb"]], W=[d["t_pout"]], inc=False)
                        yield
                        mm(d["pout"][:, h * 64:(h + 1) * 64], d["attm"][:, h, :], d["vt"][b][:, h * 64:(h + 1) * 64], False, True,
                           R=[d["t_attm"], d["t_vt"][b]], W=[d["t_pout"]], inc=(h == 3))
                        yield
                mm(pds, d["ke"][b][:], d["vt"][b][:], True, True, R=[d["t_ke"][b], d["t_vt"][b]], W=[d["t_pad"]])
                yield
                tt("dve", d["T1"][:], pds, bds[:], ALU.mult, R=[d["t_pad"], t_k], W=[d["t_T1"]])
                yield
                tt("dve", d["T1"][:], d["T1"][:], d["Sf"][:], ALU.add, R=[d["t_T1"], d["t_Sf"]], W=[d["t_T1"]])
                yield
                col = 127 if dr == 0 else 0
                ts("dve", d["Sf"][:], d["T1"][:], d["ec"][b][:, col:col + 1], None, ALU.mult, None, R=[d["t_T1"], d["t_ec"][b]], W=[d["t_Sf"]])
                yield
                ts("dve", d["Sb"][:], d["T1"][:], d["ec"][b][:, col:col + 1], None, ALU.mult, None, R=[d["t_T1"], d["t_ec"][b]], W=[d["t_Sb"]])
                yield
                if need_out:
                    cp("act", d["o1"][b][:], d["pout"][:], R=[d["t_pout"]], W=[d["t_o1"][b]])
                    yield
                    P.dma("pool", go_d[dr][g * 128:(g + 1) * 128, :], d["o1"][b][:], R=[d["t_o1"][b]], W=[d["t_go"]])
                    yield
            def rr(gens):
                gens = list(gens)
                while gens:
                    for gq in list(gens):
                        try:
                            next(gq)
                        except StopIteration:
                            gens.remove(gq)
            for dr in range(2):
                LD(dr, 0)
            rr([S1(0, 0), S1(1, 0)])
            for oi in range(34):
                gl = [S2(0, oi), S2(1, oi)]
                if oi + 1 < 34:
                    for dr in range(2):
                        LD(dr, oi + 1)
                    gl = [S1(0, oi + 1), S2(0, oi), S1(1, oi + 1), S2(1, oi)]
                rr(gl)
            P.barrier()
        with ExitStack() as ph:
            gog = sb("ggog", [128, 256], F32, ph); t_k2 = Tok()
            P.dma("sp", gog[:], goutg_d[l], W=[t_k2])
            of = [sb("hof%d" % i, [128, 256], F32, ph) for i in range(2)]; t_of = [Tok(), Tok()]
            ob_ = [sb("hob%d" % i, [128, 256], F32, ph) for i in range(2)]; t_ob_ = [Tok(), Tok()]
            rt = [sb("hrt%d" % i, [128, 256], BF16, ph) for i in range(2)]; t_rt = [Tok(), Tok()]
            o1 = [sb("ho1%d" % i, [128, 256], F32, ph) for i in range(2)]; t_o1 = [Tok(), Tok()]
            o2 = [sb("ho2%d" % i, [128, 256], F32, ph) for i in range(2)]; t_o2 = [Tok(), Tok()]
            ss4 = [sb("hss%d" % i, [128, 4], F32, ph) for i in range(2)]; t_ss4 = [Tok(), Tok()]
            sr = [sb("hsr%d" % i, [128, 256], F32, ph) for i in range(2)]; t_sr = [Tok(), Tok()]
            oc = [sb("hoc%d" % i, [128, 256], BF16, ph) for i in range(2)]; t_oc = [Tok(), Tok()]
            oTs = [sb("hoTs%d" % i, [128, 2, 128], BF16, ph) for i in range(2)]; t_oTs = [Tok(), Tok()]
            pT = [pm("hpT%d" % i, [128, 2, 128], BF16, ph) for i in range(2)]; t_pT = [Tok(), Tok()]
            tiles = list(range(32)) + ([32, 33] if with_ctx else [])

            def ldo(ti):
                g = tiles[ti]; b = ti % 2
                P.dma("sp", of[b][:], gof_d[g * 128:(g + 1) * 128, :], W=[t_of[b]])
                P.dma("sp", ob_[b][:], gob_d[g * 128:(g + 1) * 128, :], W=[t_ob_[b]])
                P.dma("sp", rt[b][:], gr_d[g * 128:(g + 1) * 128, :], W=[t_rt[b]])
            def out_gen(ti):
                g = tiles[ti]; b = ti % 2
                tt("dve", o1[b][:], of[b][:], ob_[b][:], ALU.add, R=[t_of[b], t_ob_[b]], W=[t_o1[b]])
                yield
                tt("dve", o2[b][:], o1[b][:], o1[b][:], ALU.mult, R=[t_o1[b]], W=[t_o2[b]])
                yield
                P.op("dve", lambda e: e.tensor_reduce(out=ss4[b][:], in_=o2[b][:].rearrange("p (h d) -> p h d", h=4), axis=AX.X, op=ALU.add),
                     R=[t_o2[b]], W=[t_ss4[b]])
                yield
                act(ss4[b][:], ss4[b][:], AF.Ln, R=[t_ss4[b]], W=[t_ss4[b]], scale=1.0 / 64, bias=EPS)
                yield
                act(ss4[b][:], ss4[b][:], AF.Exp, R=[t_ss4[b]], W=[t_ss4[b]], scale=-0.5)
                yield
                act(sr[b][:], rt[b][:], AF.Exp, R=[t_rt[b]], W=[t_sr[b]], scale=-1.0)
                yield
                act(sr[b][:], sr[b][:], AF.Ln, R=[t_sr[b]], W=[t_sr[b]], scale=1.0, bias=1.0)
                yield
                act(sr[b][:], sr[b][:], AF.Exp, R=[t_sr[b]], W=[t_sr[b]], scale=-1.0)
                yield
                tt("dve", o2[b][:].rearrange("p (h d) -> p h d", h=4), o1[b][:].rearrange("p (h d) -> p h d", h=4),
                   ss4[b][:].unsqueeze(2).to_broadcast([128, 4, 64]), ALU.mult, R=[t_o1[b], t_ss4[b]], W=[t_o2[b]])
                yield
                tt("dve", o2[b][:], o2[b][:], gog[:], ALU.mult, R=[t_o2[b], t_k2], W=[t_o2[b]])
                yield
                tt("dve", o2[b][:], o2[b][:], sr[b][:], ALU.mult, R=[t_o2[b], t_sr[b]], W=[t_o2[b]])
                yield
                tt("dve", oc[b][:], o2[b][:], rt[b][:], ALU.mult, R=[t_o2[b], t_rt[b]], W=[t_oc[b]])
                yield
                for j in range(2):
                    tr(pT[b][:, j, :], oc[b][:, j * 128:(j + 1) * 128], identb[:], R=[t_oc[b], t_c], W=[t_pT[b]], inc=(j == 1))
                yield
                cp("act", oTs[b][:], pT[b][:], R=[t_pT[b]], W=[t_oTs[b]])
                yield
                P.dma("pool", oT_v[:, 6:8, g * 128:(g + 1) * 128], oTs[b][:], R=[t_oTs[b]], W=[Tok()])
                yield

            def rr2(gens):
                gens = list(gens)
                while gens:
                    for gq in list(gens):
                        try:
                            next(gq)
                        except StopIteration:
                            gens.remove(gq)
            ldo(0)
            if len(tiles) > 1:
                ldo(1)
            for ti in range(0, len(tiles), 2):
                gl = [out_gen(ti)]
                if ti + 1 < len(tiles):
                    gl.append(out_gen(ti + 1))
                rr2(gl)
                for tj in (ti + 2, ti + 3):
                    if tj < len(tiles):
                        ldo(tj)
            P.barrier()

    def phaseC(l, t_wo, t_w1, t_w2, with_ctx, last):
        W1 = WA[:].rearrange("p (k n) -> p k n", k=8)
        W2 = WB[:].rearrange("p (f n) -> p f n", f=32)
        with ExitStack() as ph:
            tl = norm_tiles(ph, "c")
            xb = [sb("cxb%d" % i, [128, 8, NB], F32, ph) for i in range(2)]; t_xb = [Tok(), Tok()]
            ob = [sb("cob%d" % i, [128, 8, NB], BF16, ph) for i in range(2)]; t_ob = [Tok(), Tok()]
            hT2 = [sb("chT%d" % i, [128, 8, NB], BF16, ph) for i in range(2)]; t_hT2 = [Tok(), Tok()]
            hid = sb("chid", [128, 32, NB], BF16, ph); t_hid = [Tok() for _ in range(32)]
            rl = [sb("crl%d" % i, [128, NB], F32, ph) for i in range(2)]; t_rl = [Tok(), Tok()]
            pacc = [pm("cpa%d" % i, [128, NB], F32, ph) for i in range(2)]; t_pacc = [Tok(), Tok()]
            pup = [pm("cpu%d" % i, [128, NB], F32, ph) for i in range(2)]; t_pup = [Tok(), Tok()]
            if last:
                pTo = pm("cpTo", [128, 8, 128], F32, ph); t_pTo = Tok()
                yo = [sb("cyo%d" % i, [128, 512], F32, ph) for i in range(2)]; t_yo = [Tok(), Tok()]
            bl = blocks(with_ctx)

            def ld(bi):
                t0, n = bl[bi]
                P.dma("sp", xb[bi % 2][:, :, :n], xs_v[:, :, t0:t0 + n], R=[t_xs], W=[t_xb[bi % 2]])
                P.dma("sp", ob[bi % 2][:, :, :n], oT_v[:, :, t0:t0 + n], W=[t_ob[bi % 2]])
            ld(0)
            cc_ = {"ca": 0}

            def front(bj):
                tj, nj = bl[bj]
                bb = bj % 2
                vv = 0 if tj < S else 1
                for nn in range(8):
                    pb = cc_["ca"] % 2; cc_["ca"] += 1
                    for k in range(8):
                        mm(pacc[pb][:, :nj], WO[:, k, nn * 128:(nn + 1) * 128], ob[bb][:, k, :nj], k == 0, k == 7,
                           R=[t_wo[k], t_ob[bb]], W=[t_pacc[pb]])
                    stt("dve", xb[bb][:, nn, :nj], pacc[pb][:, :nj], modT[:, 16 + nn, vv:vv + 1], xb[bb][:, nn, :nj], ALU.mult, ALU.add,
                        R=[t_pacc[pb], t_mod, t_xb[bb]], W=[t_xb[bb]])
                norm_mod(xb[bb], t_xb[bb], nj, hT2[bb], t_hT2[bb], gsc2, 24, vv, tl)
            if len(bl) > 1:
                ld(1)
            front(0)
            cu = 0; cy = 0
            for bi, (t0, n) in enumerate(bl):
                b = bi % 2
                v = 0 if t0 < S else 1
                hT = hT2[b]; t_hT = t_hT2[b]
                for f in range(32):
                    pb = cu % 2; cu += 1
                    for k in range(8):
                        mm(pup[pb][:, :n], W1[:, k, f * 128:(f + 1) * 128], hT[:, k, :n], k == 0, k == 7,
                           R=[t_w1[k], t_hT], W=[t_pup[pb]])
                    act(rl[pb][:, :n], pup[pb][:, :n], AF.Relu, R=[t_pup[pb]], W=[t_rl[pb]])
                    tt("pool" if f % 2 else "dve", hid[:, f, :n], rl[pb][:, :n], rl[pb][:, :n], ALU.mult, R=[t_rl[pb]], W=[t_hid[f]])
                if bi + 1 < len(bl):
                    front(bi + 1)
                for nn in range(8):
                    pb = cc_["ca"] % 2; cc_["ca"] += 1
                    for f in range(32):
                        mm(pacc[pb][:, :n], W2[:, f, nn * 128:(nn + 1) * 128], hid[:, f, :n], f == 0, f == 31,
                           R=[t_w2[f // 4], t_hid[f]], W=[t_pacc[pb]])
                    stt("dve", xb[b][:, nn, :n], pacc[pb][:, :n], modT[:, 40 + nn, v:v + 1], xb[b][:, nn, :n], ALU.mult, ALU.add,
                        R=[t_pacc[pb], t_mod, t_xb[b]], W=[t_xb[b]])
                if not last:
                    P.dma("pool", xs_v[:, :, t0:t0 + n], xb[b][:, :, :n], R=[t_xb[b]], W=[t_xs])
                else:
                    for ti in range(n // 128):
                        for k in range(8):
                            tr(pTo[:, k, :], xb[b][:, k, ti * 128:(ti + 1) * 128], identf[:], R=[t_xb[b], t_c], W=[t_pTo], inc=(k == 7))
                        g0 = t0 + ti * 128
                        cp("act", yo[0][:], pTo[:, 0:4, :].rearrange("p k t -> p (k t)"), R=[t_pTo], W=[t_yo[0]])
                        P.dma("pool", y_out[g0:g0 + 128, 0:512], yo[0][:], R=[t_yo[0]], W=[Tok()])
                        cp("dve", yo[1][:], pTo[:, 4:8, :].rearrange("p k t -> p (k t)"), R=[t_pTo], W=[t_yo[1]])
                        P.dma("pool", y_out[g0:g0 + 128, 512:1024], yo[1][:], R=[t_yo[1]], W=[Tok()])
                if bi + 2 < len(bl):
                    ld(bi + 2)
            P.barrier()

    stages = []
    p0gen = phase0()
    next(p0gen)
    stages.append("p0")
    for l in range(DEPTH):
        last = (l == DEPTH - 1)
        with_ctx_out = not last
        if stop_after is not None and stages and stages[-1] == stop_after:
            break
        if l == 0:
            with ExitStack() as phs:
                npump = 0
                for _ in setup_gen(0, phs, 768):
                    if npump < 32:
                        next(p0gen, None)
                        npump += 1
                P.barrier()
            for _ in p0gen:
                pass
        modT = modTs[l]; gsc1 = gsc1s[l]; gsc2 = gsc2s[l]; t_mod = t_mods[l]
        stages.append("S%d" % l)
        if stop_after == stages[-1]:
            break
        t_win = load_w(WA[:, 0:8 * NCOL].rearrange("p (k n) -> p k n", k=8), win_d[l].rearrange("(k p) n -> p k n", p=128), 8, NCOL)
        t_wo = load_w(WO, wout_d[l].rearrange("(k p) n -> p k n", p=128), 8, 1024)
        t_w2 = []
        WBv = WB[:].rearrange("p (g m) -> p g m", g=8)
        for g in range(8):
            t = Tok()
            P.dma("pool", WBv[:, g, :].rearrange("p (f n) -> p f n", f=4), w2_d[l][g * 512:(g + 1) * 512, :].rearrange("(f p) n -> p f n", p=128), W=[t])
            t_w2.append(t)
        stages.append("W%d" % l)
        if stop_after == stages[-1]:
            break
        phaseA(l, t_win, True); stages.append("A%d" % l)
        if stop_after == stages[-1]:
            break
        t_w1 = load_w(WA[:].rearrange("p (k n) -> p k n", k=8), w1_d[l].rearrange("(k p) n -> p k n", p=128), 8, 4096)
        phaseB1(l, with_ctx_out, 1 if (l == 0 and DEPTH > 1) else None); stages.append("B1%d" % l)
        if stop_after == stages[-1]:
            break
        phaseB2(l, with_ctx_out); stages.append("B2%d" % l)
        if stop_after == stages[-1]:
            break
        phaseB3(l, with_ctx_out); stages.append("B3%d" % l)
        if stop_after == stages[-1]:
            break
        phaseC(l, t_wo, t_w1, t_w2, with_ctx_out, last); stages.append("C%d" % l)
        if stop_after == stages[-1]:
            break
    P.barrier()
    print("instructions emitted:", P.n_inst, {k: v for k, v in P.cnt.items() if not k.startswith("d_")})
    return nc


_NC_CACHE = {}


def kernel(**inputs):
    sh = _prep_shared(inputs)
    x = np.asarray(inputs["x"], dtype=np.float32)
    c = np.asarray(inputs["c"], dtype=np.float32)
    ctx = np.asarray(inputs["ctx"], dtype=np.float32)
    c_ctx = np.asarray(inputs["c_ctx"], dtype=np.float32)
    B = x.shape[0]
    in_maps = []
    for b in range(B):
        m = dict(sh)
        m["x"] = np.ascontiguousarray(x[b])
        m["ctx"] = np.ascontiguousarray(ctx[b])
        cc = np.stack([c[b].reshape(8, 128).T, c_ctx.reshape(8, 128).T], axis=2)
        m["cc"] = np.ascontiguousarray(cc.astype(np.float32))
        in_maps.append(m)
    if "nc" not in _NC_CACHE:
        _NC_CACHE["nc"] = build()
    res = run_bass_kernel_spmd(_NC_CACHE["nc"], in_maps, core_ids=list(range(B)))
    return np.stack([np.asarray(r["y"], dtype=np.float32) for r in res.results], axis=0)
```
